# Optimizing a Trainium2 kernel written in Bass

```python
import math
import jax
import jax.numpy as jnp
from jax import lax
import numpy as np

D_MODEL = 1024
BATCH = 4
SEQ = 4096
DEPTH = 4
DEC_BATCH = 128
DEC_SEQ = 1
PAST_LEN = 2048
PAGE_SIZE = 128

N_EVEN = (DEPTH + 1) // 2
N_ODD = DEPTH // 2
H_A = 4
D_HA = 64
DA = 2 * D_HA
W_A = H_A * DA
H_B = 4
DK_B = 128
DV_B = 128
W_B = H_B * DV_B
QKV_B = 2 * H_B * DK_B + H_B * DV_B
GDN_CONV = 4
GDN_CHUNK = 64
D_C = D_MODEL
SC_WIDTH = 3
NUM_BUCKETS = 32
MAX_EXACT = NUM_BUCKETS // 2
MAX_DISTANCE = 128
Q_BLOCK = 128
EPS = 1e-6
NEG = -1e30
EVEN_SIZES = (W_A, W_A, W_A, W_A, QKV_B, W_B, H_B, H_B)
P_EVEN = sum(EVEN_SIZES)
P_ODD = 4 * D_C

kernel_name = 'hybrid_diffattn_gdn_shortconv_step'


def split_cols(p, sizes):
    idx, acc = [], 0
    for s in sizes[:-1]:
        acc += s
        idx.append(acc)
    return jnp.split(p, idx, axis=-1)


def rmsnorm(x, w):
    xf = x.astype(jnp.float32)
    y = xf * lax.rsqrt(jnp.mean(xf * xf, axis=-1, keepdims=True) + EPS)
    return (y * w.astype(jnp.float32)).astype(x.dtype)


def l2norm(x):
    xf = x.astype(jnp.float32)
    return xf * lax.rsqrt(jnp.sum(xf * xf, axis=-1, keepdims=True) + EPS)


def causal_conv(x, buf, w):
    W = w.shape[0]
    T = x.shape[1]
    xp = jnp.concatenate([buf.astype(x.dtype), x], axis=1)
    y = xp[:, 0:T] * w[0]
    for j in range(1, W):
        y = y + xp[:, j:j + T] * w[j]
    return y, xp[:, T:]


def t5_bucket(rel):
    n = jnp.maximum(rel, 0)
    nf = jnp.maximum(n, 1).astype(jnp.float32)
    large = MAX_EXACT + (jnp.log(nf / MAX_EXACT) / math.log(MAX_DISTANCE / MAX_EXACT)
                         * (NUM_BUCKETS - MAX_EXACT)).astype(jnp.int32)
    large = jnp.minimum(large, NUM_BUCKETS - 1)
    return jnp.where(n < MAX_EXACT, n, large)


def diff_attention(q, k, v, q_pos, k_pos, lam, rel_table):
    B, T, H, _ = q.shape
    QB = math.gcd(Q_BLOCK, T)
    nb = T // QB
    qb = jnp.swapaxes(q.reshape(B, nb, QB, H, DA), 0, 1)
    pb = q_pos.reshape(nb, QB)
    k1, k2 = k[..., :D_HA], k[..., D_HA:]
    scale = D_HA ** -0.5

    def one_block(args):
        qi, pi = args
        bias = jnp.transpose(rel_table[t5_bucket(pi[:, None] - k_pos[None, :])].astype(jnp.float32), (2, 0, 1))
        mask = k_pos[None, :] <= pi[:, None]

        def probs(qa, ka):
            s = jnp.einsum('bqhd,bkhd->bhqk', qa, ka).astype(jnp.float32) * scale + bias
            return jax.nn.softmax(jnp.where(mask, s, NEG), axis=-1)

        a = probs(qi[..., :D_HA], k1) - lam * probs(qi[..., D_HA:], k2)
        return jnp.einsum('bhqk,bkhd->bqhd', a.astype(v.dtype), v)

    o = lax.map(one_block, (qb, pb))
    return jnp.swapaxes(o, 0, 1).reshape(B, T, H, DA)


def gated_delta_chunked(q, k, v, g, beta, S0):
    B, T, H, DK = q.shape
    DV = v.shape[-1]
    C = math.gcd(GDN_CHUNK, T)
    N = T // C

    def blk(a):
        return jnp.moveaxis(a.reshape((B, N, C, H) + a.shape[3:]), 3, 2)

    q, k, v, g, beta = blk(q), blk(k), blk(v), blk(g), blk(beta)
    g = jnp.cumsum(g, axis=-1)
    incl = jnp.tril(jnp.ones((C, C), dtype=bool))
    strict = jnp.tril(jnp.ones((C, C), dtype=bool), -1)
    decay = jnp.exp(jnp.where(incl, g[..., :, None] - g[..., None, :], NEG))
    kb = k * beta[..., None]
    M = jnp.where(strict, jnp.einsum('bnhid,bnhjd->bnhij', kb, k) * decay, 0.0)
    eye = jnp.eye(C, dtype=jnp.float32)
    Tm = lax.linalg.triangular_solve(eye + M, jnp.broadcast_to(eye, M.shape), left_side=True, lower=True)
    u = Tm @ (v * beta[..., None])
    w = Tm @ (kb * jnp.exp(g)[..., None])
    attn = jnp.einsum('bnhid,bnhjd->bnhij', q, k) * decay
    qg = q * jnp.exp(g)[..., None]
    kg = k * jnp.exp(g[..., -1:] - g)[..., None]
    gl = jnp.exp(g[..., -1])

    def step(S, inp):
        qg_c, kg_c, u_c, w_c, attn_c, gl_c = inp
        v_new = u_c - w_c @ S
        o = qg_c @ S + attn_c @ v_new
        S = S * gl_c[..., None, None] + jnp.swapaxes(kg_c, -1, -2) @ v_new
        return S, o

    xs = tuple(jnp.moveaxis(a, 1, 0) for a in (qg, kg, u, w, attn, gl))
    S, o = lax.scan(step, S0, xs)
    o = jnp.moveaxis(jnp.moveaxis(o, 0, 1), 2, 3).reshape(B, T, H, DV)
    return o, S


def even_layer(x, past_kv, S0, conv0, li, ei, lambda_init, P):
    B, T, _ = x.shape
    h = rmsnorm(x, P['norm_w'][li])
    p = h @ P['w_in_even'][ei]
    qa, ka, va, za, qkv_b, zb, a_b, b_b = split_cols(p, EVEN_SIZES)
    qa = rmsnorm(qa.reshape(B, T, H_A, 2, D_HA), P['qn_w'][ei]).reshape(B, T, H_A, DA)
    ka = rmsnorm(ka.reshape(B, T, H_A, 2, D_HA), P['kn_w'][ei]).reshape(B, T, H_A, DA)
    va = va.reshape(B, T, H_A, DA)
    lam = (jnp.exp(jnp.sum(P['lam_q1'][ei] * P['lam_k1'][ei]).astype(jnp.float32))
           - jnp.exp(jnp.sum(P['lam_q2'][ei] * P['lam_k2'][ei]).astype(jnp.float32)) + lambda_init)
    if past_kv is None:
        k_all, v_all = ka, va
    else:
        k_all = jnp.concatenate([past_kv[0].astype(ka.dtype), ka], axis=1)
        v_all = jnp.concatenate([past_kv[1].astype(va.dtype), va], axis=1)
    L = k_all.shape[1]
    k_pos = jnp.arange(L, dtype=jnp.int32)
    q_pos = (L - T) + jnp.arange(T, dtype=jnp.int32)
    oa = diff_attention(qa, k_all, v_all, q_pos, k_pos, lam, P['rel_table'])
    oa = rmsnorm(oa, P['subln_w'][ei]) * (1.0 - lambda_init)
    oa = oa.reshape(B, T, W_A) * jax.nn.silu(za)
    cb, conv_new = causal_conv(qkv_b, conv0, P['gdn_conv_w'][ei])
    cb = jax.nn.silu(cb)
    qb, kb, vb = jnp.split(cb, [H_B * DK_B, 2 * H_B * DK_B], axis=-1)
    qb = l2norm(qb.reshape(B, T, H_B, DK_B)) * (DK_B ** -0.5)
    kb = l2norm(kb.reshape(B, T, H_B, DK_B))
    vb = vb.reshape(B, T, H_B, DV_B).astype(jnp.float32)
    g = -jnp.exp(P['gdn_a_log'][ei].astype(jnp.float32)) * jax.nn.softplus(
        a_b.astype(jnp.float32) + P['gdn_dt_bias'][ei].astype(jnp.float32))
    beta = jax.nn.sigmoid(b_b.astype(jnp.float32))
    ob, S_new = gated_delta_chunked(qb, kb, vb, g, beta, S0.astype(jnp.float32))
    ob = rmsnorm(ob, P['gdn_norm_w'][ei]).astype(x.dtype).reshape(B, T, W_B) * jax.nn.silu(zb)
    y = jnp.concatenate([oa, ob], axis=-1) @ P['w_out_even'][ei]
    return x + y, ka, va, S_new.astype(S0.dtype), conv_new


def odd_layer(x, buf0, li, oi, P):
    h = rmsnorm(x, P['norm_w'][li])
    p = h @ P['w_in_odd'][oi]
    bg, cg, hh, z = jnp.split(p, 4, axis=-1)
    cv, buf = causal_conv(cg * hh, buf0, P['sc_conv_w'][oi])
    y = (bg * cv * jax.nn.silu(z)) @ P['w_out_odd'][oi]
    return x + y, buf


def trunk(x, paged, S0s, gconv0s, sconv0s, P):
    ks, vs, Ss, gcs, scs = [], [], [], [], []
    ei, oi = 0, 0
    for li in range(DEPTH):
        if li % 2 == 0:
            if paged is None:
                past = None
            else:
                cache_k, cache_v, page_table = paged
                nb = page_table.shape[0]
                past = (cache_k[ei][page_table].reshape(nb, -1, H_A, DA),
                        cache_v[ei][page_table].reshape(nb, -1, H_A, DA))
            lambda_init = 0.8 - 0.6 * math.exp(-0.3 * li)
            x, k_new, v_new, S_new, gc_new = even_layer(x, past, S0s[ei], gconv0s[ei], li, ei, lambda_init, P)
            ks.append(k_new)
            vs.append(v_new)
            Ss.append(S_new)
            gcs.append(gc_new)
            ei += 1
        else:
            x, sc_new = odd_layer(x, sconv0s[oi], li, oi, P)
            scs.append(sc_new)
            oi += 1
    return x, jnp.stack(ks), jnp.stack(vs), jnp.stack(Ss), jnp.stack(gcs), jnp.stack(scs)


def setup_inputs(seed: int = 0) -> dict:
    key = jax.random.key(seed)
    kk = jax.random.split(key, 32)
    f = jnp.float32
    n_pages = PAST_LEN // PAGE_SIZE
    n_used = DEC_BATCH * n_pages
    n_pool = n_used + n_used // 4

    def nrm(k, shape, scale):
        return jax.random.normal(k, shape, f) * scale

    page_table = jax.random.permutation(kk[0], n_pool)[:n_used].reshape(DEC_BATCH, n_pages).astype(jnp.int32)
    a_log = jnp.log(jax.random.uniform(kk[1], (N_EVEN, H_B), f, 1.0, 16.0))
    dt = jnp.exp(jax.random.uniform(kk[2], (N_EVEN, H_B), f, math.log(1e-3), math.log(1e-1)))
    dt_bias = dt + jnp.log(-jnp.expm1(-dt))
    out_scale = DEPTH ** -0.5
    return {
        'x_prompt': nrm(kk[3], (BATCH, SEQ, D_MODEL), 1.0),
        'x_sample': nrm(kk[4], (DEC_BATCH, DEC_SEQ, D_MODEL), 1.0),
        'cache_k': nrm(kk[5], (N_EVEN, n_pool, PAGE_SIZE, H_A, DA), 1.0),
        'cache_v': nrm(kk[6], (N_EVEN, n_pool, PAGE_SIZE, H_A, DA), 1.0),
        'page_table': page_table,
        'state_gdn': nrm(kk[7], (N_EVEN, DEC_BATCH, H_B, DK_B, DV_B), 0.1),
        'state_gdn_conv': nrm(kk[8], (N_EVEN, DEC_BATCH, GDN_CONV - 1, QKV_B), 1.0),
        'state_shortconv': nrm(kk[9], (N_ODD, DEC_BATCH, SC_WIDTH - 1, D_C), 1.0),
        'norm_w': 1.0 + nrm(kk[10], (DEPTH, D_MODEL), 0.1),
        'rel_table': nrm(kk[11], (NUM_BUCKETS, H_A), 0.5),
        'w_in_even': nrm(kk[12], (N_EVEN, D_MODEL, P_EVEN), D_MODEL ** -0.5),
        'w_out_even': nrm(kk[13], (N_EVEN, W_A + W_B, D_MODEL), (W_A + W_B) ** -0.5 * out_scale),
        'qn_w': 1.0 + nrm(kk[14], (N_EVEN, D_HA), 0.1),
        'kn_w': 1.0 + nrm(kk[15], (N_EVEN, D_HA), 0.1),
        'lam_q1': nrm(kk[16], (N_EVEN, D_HA), 0.1),
        'lam_k1': nrm(kk[17], (N_EVEN, D_HA), 0.1),
        'lam_q2': nrm(kk[18], (N_EVEN, D_HA), 0.1),
        'lam_k2': nrm(kk[19], (N_EVEN, D_HA), 0.1),
        'subln_w': 1.0 + nrm(kk[20], (N_EVEN, DA), 0.1),
        'gdn_conv_w': nrm(kk[21], (N_EVEN, GDN_CONV, QKV_B), GDN_CONV ** -0.5),
        'gdn_a_log': a_log,
        'gdn_dt_bias': dt_bias,
        'gdn_norm_w': 1.0 + nrm(kk[22], (N_EVEN, DV_B), 0.1),
        'w_in_odd': nrm(kk[23], (N_ODD, D_MODEL, P_ODD), D_MODEL ** -0.5),
        'sc_conv_w': nrm(kk[24], (N_ODD, SC_WIDTH, D_C), SC_WIDTH ** -0.5),
        'w_out_odd': nrm(kk[25], (N_ODD, D_C, D_MODEL), D_C ** -0.5 * out_scale),
    }


def reference(x_prompt, x_sample, cache_k, cache_v, page_table, state_gdn, state_gdn_conv, state_shortconv,
              norm_w, rel_table, w_in_even, w_out_even, qn_w, kn_w, lam_q1, lam_k1, lam_q2, lam_k2, subln_w,
              gdn_conv_w, gdn_a_log, gdn_dt_bias, gdn_norm_w, w_in_odd, sc_conv_w, w_out_odd):
    P = {'norm_w': norm_w, 'rel_table': rel_table, 'w_in_even': w_in_even, 'w_out_even': w_out_even,
         'qn_w': qn_w, 'kn_w': kn_w, 'lam_q1': lam_q1, 'lam_k1': lam_k1, 'lam_q2': lam_q2, 'lam_k2': lam_k2,
         'subln_w': subln_w, 'gdn_conv_w': gdn_conv_w, 'gdn_a_log': gdn_a_log, 'gdn_dt_bias': gdn_dt_bias,
         'gdn_norm_w': gdn_norm_w, 'w_in_odd': w_in_odd, 'sc_conv_w': sc_conv_w, 'w_out_odd': w_out_odd}
    bp = x_prompt.shape[0]
    zS = jnp.zeros((N_EVEN, bp, H_B, DK_B, DV_B), state_gdn.dtype)
    zgc = jnp.zeros((N_EVEN, bp, GDN_CONV - 1, QKV_B), x_prompt.dtype)
    zsc = jnp.zeros((N_ODD, bp, SC_WIDTH - 1, D_C), x_prompt.dtype)
    y_prompt, k_p, v_p, S_p, gc_p, sc_p = trunk(x_prompt, None, zS, zgc, zsc, P)
    y_sample, k_s, v_s, S_s, gc_s, sc_s = trunk(x_sample, (cache_k, cache_v, page_table), state_gdn,
                                                 state_gdn_conv, state_shortconv, P)
    return (y_prompt, y_sample, k_p, v_p, S_p, gc_p, sc_p, k_s, v_s, S_s, gc_s, sc_s)
```

```python
import math
import numpy as np
import concourse.bass as bass
import concourse.mybir as mybir
from concourse.bass_utils import run_bass_kernel_spmd

F32, BF16, I32 = mybir.dt.float32, mybir.dt.bfloat16, mybir.dt.int32
AF = mybir.ActivationFunctionType
ALU = mybir.AluOpType
AX = mybir.AxisListType

D = 1024
H_A = 4
DA = 128
H_B = 4
QKV_B = 1536
P_EVEN = 4104
P_ODD = 4096
EPS = 1e-6
NEG = -30000.0
NUM_BUCKETS = 32


def t5_bucket_np(n):
    n = np.maximum(n, 0)
    nf = np.maximum(n, 1).astype(np.float32)
    large = 16 + (np.log(nf / np.float32(16)) / np.float32(math.log(8.0)) * 16).astype(np.int32)
    large = np.minimum(large, 31)
    return np.where(n < 16, n, large)


class Buf:
    __slots__ = ("ap", "w", "r", "name", "psum")

    def __init__(self, ap, name="", psum=False):
        self.ap = ap
        self.w = None
        self.r = {}
        self.name = name
        self.psum = psum

    def __getitem__(self, k):
        return self.ap[k]


class Sched:
    LIMIT = 30000
    NDS = 20

    def __init__(self, nc):
        self.nc = nc
        self.E = {"pe": nc.tensor, "dve": nc.vector, "act": nc.scalar, "pool": nc.gpsimd, "sp": nc.sync}
        self.sem = {}
        self.cnt = {}
        self.nsem = 0
        self.waited = {e: {} for e in self.E}
        self.pending = {e: False for e in self.E}
        for e in self.E:
            self._newsem(e)
        self.dsems = {}
        self.dnext = {}
        for q in ("sp", "pool", "act"):
            self.dsems[q] = [[self._alloc(f"d_{q}_{i}"), 0] for i in range(self.NDS)]
            self.dnext[q] = 0
        self.dbufs = {}
        self.nins = 0

    def _alloc(self, name):
        self.nsem += 1
        return (self.nc.alloc_semaphore(name), name)

    def _newsem(self, e):
        self.sem[e] = self._alloc(f"c_{e}_{self.nsem}")
        self.cnt[e] = 0

    def _wait(self, e, tok):
        sem, val = tok
        if e == "pe" and sem[1] == self.sem["pe"][1]:
            return
        w = self.waited[e]
        if w.get(sem[1], 0) >= val:
            return
        self.E[e].wait_ge(sem[0], val)
        self.nins += 1
        w[sem[1]] = val

    def _deps(self, e, reads, writes):
        for b in reads:
            if b.w is not None:
                self._wait(e, b.w)
            if b.psum:
                for s, tok in b.r.items():
                    self._wait(e, tok)
        for b in writes:
            if b.w is not None:
                self._wait(e, b.w)
            for s, tok in b.r.items():
                self._wait(e, tok)

    def _mark(self, tok, reads, writes):
        for b in reads:
            b.r[tok[0][1]] = tok
        for b in writes:
            b.w = tok
            b.r = {}

    def op(self, e, fn, reads=(), writes=(), sig=True):
        self._deps(e, reads, writes)
        ins = fn(self.E[e])
        self.nins += 1
        if sig:
            if self.cnt[e] >= self.LIMIT:
                self._newsem(e)
            self.cnt[e] += 1
            ins.then_inc(self.sem[e][0], 1)
            tok = (self.sem[e], self.cnt[e])
            self.pending[e] = False
        else:
            assert self.cnt[e] < self.LIMIT - 1
            tok = (self.sem[e], self.cnt[e] + 1)
            self.pending[e] = True
        self._mark(tok, reads, writes)
        return ins

    def dma(self, q, out, in_, reads=(), writes=(), **kw):
        self._deps(q, reads, writes)
        k = self.dnext[q]
        self.dnext[q] = (k + 1) % self.NDS
        ent = self.dsems[q][k]
        if ent[1] > 0:
            self._wait(q, (ent[0], ent[1]))
        if ent[1] + 16 > 60000:
            ent[0] = self._alloc(f"d_{q}_{k}_{self.nsem}")
            ent[1] = 0
        ins = self.E[q].dma_start(out=out, in_=in_, **kw)
        self.nins += 1
        ent[1] += 16
        ins.then_inc(ent[0][0], 16)
        tok = (ent[0], ent[1])
        self._mark(tok, reads, writes)
        return ins

    def idma(self, out, in_, idx_ap, reads=(), writes=()):
        q = "pool"
        self._deps(q, reads, writes)
        k = self.dnext[q]
        self.dnext[q] = (k + 1) % self.NDS
        ent = self.dsems[q][k]
        if ent[1] > 0:
            self._wait(q, (ent[0], ent[1]))
        ins = self.E[q].indirect_dma_start(out=out, out_offset=None, in_=in_,
                                           in_offset=bass.IndirectOffsetOnAxis(ap=idx_ap, axis=0))
        self.nins += 1
        ent[1] += 16
        ins.then_inc(ent[0][0], 16)
        self._mark((ent[0], ent[1]), reads, writes)
        return ins

    def barrier(self):
        toks = []
        for q in self.dsems:
            for ent in self.dsems[q]:
                if ent[1] > 0:
                    toks.append((ent[0], ent[1]))
        for f in self.E:
            assert not self.pending[f]
            if self.cnt[f] > 0:
                toks.append((self.sem[f], self.cnt[f]))
        for e in self.E:
            for tok in toks:
                self._wait(e, tok)

    def db(self, name, key=0):
        k = (name, key)
        if k not in self.dbufs:
            self.dbufs[k] = Buf(None, f"{name}:{key}")
        return self.dbufs[k]

    def finish(self):
        for q in self.dsems:
            for ent in self.dsems[q]:
                if ent[1] > 0:
                    self._wait("sp", (ent[0], ent[1]))
        for e in self.E:
            assert not self.pending[e], e
            if self.cnt[e] > 0:
                self._wait("sp", (self.sem[e], self.cnt[e]))


class Builder:
    def __init__(self, cfg):
        self.cfg = cfg
        self.TP = cfg["TP"]
        self.NS = cfg["NS"]
        self.NPOOL = cfg["NPOOL"]
        self.NPAGES = cfg["NPAGES"]
        self.layers = cfg["layers"]
        self.NE = sum(1 for c in self.layers if c == "e")
        self.NO = sum(1 for c in self.layers if c == "o")
        self.do_prompt = cfg.get("prompt", True)
        self.do_decode = cfg.get("decode", True)
        self.nc = bass.Bass("TRN2", target_bir_lowering=False)
        self.S = Sched(self.nc)
        self.nsb = 0
        self.psn = 0

    def sb(self, shape, dt=F32, name=None):
        self.nsb += 1
        h = self.nc.alloc_sbuf_tensor(name or f"sb{self.nsb}", list(shape), dt)
        return Buf(h.ap(), name or f"sb{self.nsb}")

    def arena_reset(self):
        if not hasattr(self, "arena"):
            self.ARW = 22528
            self.arena = self.nc.alloc_sbuf_tensor("arena", [128, self.ARW], F32).ap()
        self.S.barrier()
        self.aoff = 0

    def ar(self, shape, dt=F32, name=None):
        n = int(np.prod(shape[1:]))
        words = n if dt in (F32, I32) else (n + 1) // 2
        words = (words + 7) // 8 * 8
        assert self.aoff + words <= self.ARW, (self.aoff, words, name)
        ap = self.arena[0:shape[0], self.aoff:self.aoff + words]
        self.aoff += words
        if dt != F32:
            ap = ap.bitcast(dt)
        ap = ap[:, 0:n]
        if len(shape) == 3:
            ap = ap.rearrange("p (a b) -> p a b", a=shape[1])
        elif len(shape) == 4:
            ap = ap.rearrange("p (a b c) -> p a b c", a=shape[1], b=shape[2])
        self.nsb += 1
        return Buf(ap, name or f"ar{self.nsb}")

    def dram(self, name, shape, dt=F32, kind="Internal"):
        return self.nc.dram_tensor(name, list(shape), dt, kind=kind).ap()

    def ps(self):
        b = self.psb[self.psn % 8]
        self.psn += 1
        return b

    def declare(self):
        c = self
        TP, NS, NE, NO = self.TP, self.NS, max(self.NE, 1), self.NO
        i = lambda n, s, dt=F32: c.dram(n, s, dt, "ExternalInput")
        o = lambda n, s, dt=F32: c.dram(n, s, dt, "ExternalOutput")
        c.x_prompt = i("x_prompt", [TP, D])
        c.x_sample = i("x_sample", [NS, D])
        c.cache_k = i("cache_k", [NE, self.NPOOL * 128, 512])
        c.cache_v = i("cache_v", [NE, self.NPOOL * 128, 512])
        c.page_table = i("page_table", [NS, self.NPAGES], I32)
        c.state_gdn = i("state_gdn", [NE, NS, H_B, 128, 128])
        c.state_gdn_conv = i("state_gdn_conv", [NE, NS, 3, QKV_B])
        c.state_shortconv = i("state_shortconv", [max(NO, 1), NS, 2, D])
        c.norm_w = i("norm_w", [len(self.layers), D])
        c.rel_table = i("rel_table", [32, 4])
        c.w_in_even = i("w_in_even", [NE, D, P_EVEN])
        c.w_out_even = i("w_out_even", [NE, D, D])
        for n in ("qn_w", "kn_w", "lam_q1", "lam_k1", "lam_q2", "lam_k2"):
            setattr(c, n, i(n, [NE, 64]))
        c.subln_w = i("subln_w", [NE, 128])
        c.gdn_conv_w = i("gdn_conv_w", [NE, 4, QKV_B])
        c.gdn_a_log = i("gdn_a_log", [NE, 4])
        c.gdn_dt_bias = i("gdn_dt_bias", [NE, 4])
        c.gdn_norm_w = i("gdn_norm_w", [NE, 128])
        c.w_in_odd = i("w_in_odd", [max(NO, 1), D, P_ODD])
        c.sc_conv_w = i("sc_conv_w", [max(NO, 1), 3, D])
        c.w_out_odd = i("w_out_odd", [max(NO, 1), D, D])
        c.cst = i("cst", [128, CST_W])
        c.oh = i("onehot", [33, OH_W])
        c.y_prompt = o("y_prompt", [TP, D])
        c.y_sample = o("y_sample", [NS, D])
        c.k_prompt = o("k_prompt", [NE, TP, 512])
        c.v_prompt = o("v_prompt", [NE, TP, 512])
        c.gdn_prompt = o("gdn_prompt", [NE, H_B, 128, 128])
        c.gdn_conv_prompt = o("gdn_conv_prompt", [NE, 3, QKV_B])
        c.sc_prompt = o("sc_prompt", [max(NO, 1), 2, D])
        c.k_sample = o("k_sample", [NE, NS, 512])
        c.v_sample = o("v_sample", [NE, NS, 512])
        c.gdn_sample = o("gdn_sample", [NE, NS, H_B, 128, 128])
        c.gdn_conv_sample = o("gdn_conv_sample", [NE, NS, 3, QKV_B])
        c.sc_sample = o("sc_sample", [max(NO, 1), NS, 2, D])
        c.xres = c.dram("xres", [TP, D])
        c.qT_s = c.dram("qT_s", [4, 128, TP], BF16)
        c.kT_s = c.dram("kT_s", [4, 128, TP], BF16)
        c.v_s = c.dram("v_s", [TP, 512], BF16)
        c.za_s = c.dram("za_s", [TP, 512], BF16)
        c.zb_s = c.dram("zb_s", [TP, 512], BF16)
        c.gb_s = c.dram("gb_s", [TP, 8])
        c.g_s = c.dram("g_s", [12, 128, TP])
        c.u_s = c.dram("u_s", [TP, D], BF16)
        c.bias_s = c.dram("bias_s", [4, 2 * 128 * 128])
        if c.cfg.get("dbg"):
            c.dbg_u = c.dram("dbg_u", [TP, D], BF16, "ExternalOutput")

    def setup(self):
        c, S = self, self.S
        c.psb = []
        for k in range(8):
            h = self.nc.alloc_psum_tensor(f"psb{k}", [128, 512], F32)
            c.psb.append(Buf(h.ap(), f"psb{k}", psum=True))
        c.cstb = c.sb([128, CST_W], F32, "cstb")
        S.dma("sp", c.cstb.ap, c.cst, writes=[c.cstb])
        c.ident = c.cstb.ap[:, 0:128]
        c.bones = c.cstb.ap[:, 128:256]
        c.ones = c.cstb.ap[:, 256:384]
        c.tri = c.cstb.ap[:, 384:512]
        c.negm_s = c.cstb.ap[:, 512:640]
        c.hones = c.cstb.ap[:, 640:896]
        c.cbf = c.sb([128, 384], BF16, "cbf")
        S.op("dve", lambda e: e.tensor_copy(out=c.cbf.ap, in_=c.cstb.ap[:, 0:384]), reads=[c.cstb], writes=[c.cbf])
        c.zeros_b = c.sb([128, 512], BF16, "zeros_b")
        S.op("pool", lambda e: e.memset(c.zeros_b.ap, 0.0), writes=[c.zeros_b])
        c.ident_b = c.cbf.ap[:, 0:128]
        c.bones_b = c.cbf.ap[:, 128:256]
        c.ones_b = c.cbf.ap[:, 256:384]

    def bcast_rows(self, dst_buf, src_ap, ncols):
        self.S.dma("sp", dst_buf.ap, src_ap.partition_broadcast(128), writes=[dst_buf])

    def load_w(self, dst, w_ap, ncol, stg):
        S = self.S
        CH = 1026
        i = 0
        for kc in range(8):
            for c0 in range(0, ncol, CH):
                cw = min(CH, ncol - c0)
                st = stg[i % 2]
                S.dma("sp", st.ap[:, 0:cw], w_ap[kc * 128:(kc + 1) * 128, c0:c0 + cw], writes=[st])
                eng = ("dve", "pool", "act")[i % 3]
                if eng == "act":
                    S.op("act", lambda e, st=st, kc=kc, c0=c0, cw=cw: e.activation(
                        out=dst.ap[:, kc, c0:c0 + cw], in_=st.ap[:, 0:cw], func=AF.Copy), reads=[st], writes=[dst])
                else:
                    S.op(eng, lambda e, st=st, kc=kc, c0=c0, cw=cw: e.tensor_copy(
                        out=dst.ap[:, kc, c0:c0 + cw], in_=st.ap[:, 0:cw]), reads=[st], writes=[dst])
                i += 1


CST_W = 1024 + 512 + 1
OH_W = 2 * 128 * 128 + 128


def make_consts():
    cst = np.zeros((128, CST_W), np.float32)
    cst[:, 0:128] = np.eye(128)
    cst[0:64, 128:192] = 1
    cst[64:128, 192:256] = 1
    cst[:, 256:384] = 1
    j = np.arange(128)[:, None]
    i = np.arange(128)[None, :]
    cst[:, 384:512] = (j <= i)
    cst[:, 512:640] = np.where(i > j, 0.0, NEG)
    cst[0:64, 640:768] = 1
    cst[64:128, 768:896] = 1
    cst[:, 896:1024] = np.where(j > i, 0.0, NEG)
    for hh in range(4):
        cst[hh, 1024 + hh * 128:1024 + (hh + 1) * 128] = 1
    cst[:, 1536] = np.arange(128)
    oh = np.zeros((33, 2, 128, 128), np.float32)
    k = np.arange(128)[:, None]
    q = np.arange(128)[None, :]
    for t, off in enumerate((0, 128)):
        n = q - k + off
        bk = t5_bucket_np(n)
        for b in range(32):
            oh[b, t] = (bk == b) & (n >= 0)
        oh[32, t] = (n < 0)
    ohd = np.zeros((33, 128), np.float32)
    bk = t5_bucket_np(128 - np.arange(128))
    for b in range(32):
        ohd[b] = (bk == b)
    return cst, np.concatenate([oh.reshape(33, 2 * 128 * 128), ohd], axis=1)


def _prompt_common(B):
    c, S = B, B.S
    c.ss = [c.sb([128, 1], F32, f"ss{i}") for i in range(4)]
    c.normw = c.sb([128, D], F32, "normw")
    c.wstg = [c.sb([128, 1026], F32, f"wstg{i}") for i in range(2)]
    c.win = c.sb([128, 8, 4224], BF16, "win")
    c.wout = c.sb([128, 8, D], BF16, "wout")
    c.mhalf = c.sb([128, 512], F32, "mhalf")
    S.op("pool", lambda e: e.memset(c.mhalf.ap, -0.5), writes=[c.mhalf])
    c.nss = 0


def _rstd(B, dst, src_ap, src_bufs, scale, bias, ncol):
    S = B.S
    S.op("dve", lambda e: e.tensor_scalar(out=dst.ap[:, 0:ncol], in0=src_ap, scalar1=scale, scalar2=bias,
                                          op0=ALU.mult, op1=ALU.add), reads=src_bufs, writes=[dst])
    _rsq(B, dst, dst.ap[:, 0:ncol])


def _rsq(B, buf, ap, extra=()):
    S = B.S
    S.op("act", lambda e: e.activation(out=ap, in_=ap, func=AF.Sqrt), reads=[buf] + list(extra), writes=[buf])
    S.op("dve", lambda e: e.reciprocal(out=ap, in_=ap), reads=[buf], writes=[buf])


def _norm_T(B, xg, li, hT):
    c, S = B, B.S
    for t in range(4):
        ss = c.ss[c.nss % 4]
        c.nss += 1
        S.op("act", lambda e: e.activation(out=c.junk.ap, in_=xg.ap[:, t, :], func=AF.Square, accum_out=ss.ap),
             reads=[xg], writes=[c.junk, ss])
        _rstd(B, ss, ss.ap, [ss], 1.0 / D, EPS, 1)
        hb = c.hb[t % 2]
        S.op("dve", lambda e: e.scalar_tensor_tensor(out=hb.ap, in0=xg.ap[:, t, :], scalar=ss.ap[:, 0:1],
                                                      in1=c.normw.ap, op0=ALU.mult, op1=ALU.mult),
             reads=[xg, ss, c.normw], writes=[hb])
        pst = c.ps()
        pv = pst.ap.bitcast(BF16).rearrange("p (k t) -> p k t", k=8)
        for kc in range(8):
            S.op("pe", lambda e, kc=kc: e.transpose(out=pv[:, kc, :], in_=hb.ap[:, kc * 128:(kc + 1) * 128],
                                                    identity=c.ident_b), reads=[hb, c.cbf], writes=[pst], sig=(kc == 7))
        S.op("act", lambda e: e.activation(out=hT.ap[:, :, t * 128:(t + 1) * 128], in_=pv, func=AF.Copy),
             reads=[pst], writes=[hT])


def _proj_fm(B, hT, col0, pst):
    c, S = B, B.S
    for kc in range(8):
        S.op("pe", lambda e, kc=kc: e.matmul(pst.ap, lhsT=c.win.ap[:, kc, col0:col0 + 128], rhs=hT.ap[:, kc, :],
                                             start=(kc == 0), stop=(kc == 7)),
             reads=[c.win, hT], writes=[pst], sig=(kc == 7))


def _out_proj_store(B, uT_ap, uT_bufs, xg, g, last):
    c, S = B, B.S
    for t in range(4):
        for half in range(2):
            pst = c.ps()
            for cb in range(8):
                S.op("pe", lambda e, cb=cb: e.matmul(pst.ap, lhsT=uT_ap(cb, t), rhs=c.wout.ap[:, cb, half * 512:(half + 1) * 512],
                                                     start=(cb == 0), stop=(cb == 7)),
                     reads=uT_bufs + [c.wout], writes=[pst], sig=(cb == 7))
            S.op("dve", lambda e: e.tensor_tensor(out=xg.ap[:, t, half * 512:(half + 1) * 512], in0=pst.ap,
                                                  in1=xg.ap[:, t, half * 512:(half + 1) * 512], op=ALU.add),
                 reads=[pst, xg], writes=[xg])
    dst = c.y_prompt if last else c.xres
    S.dma("pool", dst[g * 512:(g + 1) * 512, :].rearrange("(t p) d -> p t d", p=128), xg.ap,
          reads=[xg], writes=_xk(B, g))


def _xk(B, g):
    return [B.S.db("xres", 4 * g + i) for i in range(4)]


def _proj_bufs(B, li):
    c, S = B, B.S
    c.arena_reset()
    c.xg = [c.ar([128, 4, D], F32, "xg0")]
    c.hT = [c.ar([128, 8, 512], BF16, f"hT{i}") for i in range(2)]
    c.hb = [c.ar([128, D], BF16, f"hb{i}") for i in range(2)]
    c.junk = c.ar([128, D], BF16, "junk")
    S.dma("sp", c.normw.ap, c.norm_w[li:li + 1, :].partition_broadcast(128), writes=[c.normw])


def _load_xg(B, li, g):
    c, S = B, B.S
    xg = c.xg[g % len(c.xg)]
    src = c.x_prompt if li == 0 else c.xres
    S.dma("sp", xg.ap, src[g * 512:(g + 1) * 512, :].rearrange("(t p) d -> p t d", p=128),
          reads=_xk(B, g), writes=[xg])
    return xg


def prompt_odd(B, li, oi):
    c, S = B, B.S
    G = c.TP // 512
    last = (li == len(c.layers) - 1)
    c.load_w(c.win, c.w_in_odd[oi], P_ODD, c.wstg)
    c.load_w(c.wout, c.w_out_odd[oi], D, c.wstg)
    scw = c.scw
    for j in range(3):
        S.dma("sp", scw.ap[:, :, j], c.sc_conv_w[oi, j].rearrange("(c p) -> p c", p=128), writes=[scw],
              allow_slow_non_contiguous=True)
    _proj_bufs(B, li)
    c.CHw = [c.ar([128, 514], F32, f"CHw{i}") for i in range(2)]
    c.ocar = c.ar([128, 8, 2], F32, "ocar")
    c.otmp = [c.ar([128, 512], F32, f"otmp{i}") for i in range(3)]
    c.uT = [c.ar([128, 8, 512], BF16, f"uT{i}") for i in range(1)]
    S.op("pool", lambda e: e.memset(c.ocar.ap, 0.0), writes=[c.ocar])
    for g in range(G):
        xg = _load_xg(B, li, g)
        hT = c.hT[g % 2]
        _norm_T(B, xg, li, hT)
        uT = c.uT[0]
        for cb in range(8):
            p_b, p_c, p_h, p_z = c.ps(), c.ps(), c.ps(), c.ps()
            _proj_fm(B, hT, 0 * D + cb * 128, p_b)
            _proj_fm(B, hT, 1 * D + cb * 128, p_c)
            _proj_fm(B, hT, 2 * D + cb * 128, p_h)
            _proj_fm(B, hT, 3 * D + cb * 128, p_z)
            CH = c.CHw[cb % 2]
            t0, t1, t2 = c.otmp[0], c.otmp[1], c.otmp[2]
            S.op("pool", lambda e, cb=cb: e.tensor_copy(out=CH.ap[:, 0:2], in_=c.ocar.ap[:, cb, :]), reads=[c.ocar], writes=[CH])
            S.op("act", lambda e: e.activation(out=t0.ap, in_=p_h.ap, func=AF.Copy), reads=[p_h], writes=[t0])
            S.op("dve", lambda e: e.tensor_tensor(out=CH.ap[:, 2:514], in0=p_c.ap, in1=t0.ap, op=ALU.mult),
                 reads=[p_c, t0], writes=[CH])
            S.op("dve", lambda e: e.tensor_scalar(out=t1.ap, in0=CH.ap[:, 0:512], scalar1=scw.ap[:, cb, 0:1], scalar2=None,
                                                  op0=ALU.mult), reads=[CH, scw], writes=[t1])
            for j in (1, 2):
                S.op("dve", lambda e, j=j: e.scalar_tensor_tensor(out=t1.ap, in0=CH.ap[:, j:j + 512], scalar=scw.ap[:, cb, j:j + 1],
                                                                  in1=t1.ap, op0=ALU.mult, op1=ALU.add),
                     reads=[CH, scw, t1], writes=[t1])
            S.op("act", lambda e: e.activation(out=t2.ap, in_=p_z.ap, func=AF.Silu), reads=[p_z], writes=[t2])
            S.op("dve", lambda e: e.tensor_tensor(out=t1.ap, in0=p_b.ap, in1=t1.ap, op=ALU.mult), reads=[p_b, t1], writes=[t1])
            S.op("pool", lambda e, cb=cb: e.tensor_tensor(out=uT.ap[:, cb, :], in0=t1.ap, in1=t2.ap, op=ALU.mult),
                 reads=[t1, t2], writes=[uT])
            if g == G - 1:
                S.dma("pool", c.sc_prompt[oi, :, cb * 128:(cb + 1) * 128].rearrange("j p -> p j"), CH.ap[:, 512:514],
                      reads=[CH], allow_slow_non_contiguous=True)
            else:
                S.op("pool", lambda e, cb=cb: e.tensor_copy(out=c.ocar.ap[:, cb, :], in_=CH.ap[:, 512:514]), reads=[CH], writes=[c.ocar])
        _out_proj_store(B, lambda cb, t: uT.ap[:, cb, t * 128:(t + 1) * 128], [uT], xg, g, last)


def prompt_alloc(B):
    c = B
    _prompt_common(B)
    c.scw = c.sb([128, 8, 3], F32, "scw")


def _proj_tm(B, hT, t, col0, ncol, pst):
    c, S = B, B.S
    for kc in range(8):
        S.op("pe", lambda e, kc=kc: e.matmul(pst.ap[:, 0:ncol], lhsT=hT.ap[:, kc, t * 128:(t + 1) * 128],
                                             rhs=c.win.ap[:, kc, col0:col0 + ncol], start=(kc == 0), stop=(kc == 7)),
             reads=[hT, c.win], writes=[pst], sig=(kc == 7))


def even_alloc(B):
    c, S = B, B.S
    c.qnw8 = c.sb([128, 1], F32, "qnw8")
    c.knw8 = c.sb([128, 1], F32, "knw8")
    c.gcw = c.sb([128, 12, 4], F32, "gcw")
    c.dtb = c.sb([128, 4], F32, "dtb")
    c.negA = c.sb([128, 4], F32, "negA")
    c.crow = c.sb([128, 4], F32, "crow")
    c.ncrow = c.sb([128, 4], F32, "ncrow")
    c.lamv = c.sb([128, 4, 64], F32, "lamv")
    c.lams = c.sb([128, 4], F32, "lams")
    c.nlam = c.sb([128, 1], F32, "nlam")
    c.sw_row = c.sb([128, 128], F32, "sw_row")
    c.gnw_row = c.sb([128, 128], F32, "gnw_row")
    c.EB = c.sb([128, 4, 2, 128], F32, "EB")
    c.relx = c.sb([33, 4], F32, "relx")
    c.st = [c.sb([128, 4], F32, f"st{i}") for i in range(4)]
    c.nst = 0
    c.arena_reset()
    c.oht = [c.ar([33, 512], F32, f"oht{i}") for i in range(2)]
    c.b4 = [c.ar([4, 512], F32, f"b4{i}") for i in range(2)]
    S.op("pool", lambda e: e.memset(c.relx.ap[32:33, :], NEG), writes=[c.relx])
    S.dma("sp", c.relx.ap[0:32, :], c.rel_table, writes=[c.relx])
    S.dma("sp", c.crow.ap, c.rel_table[31:32, :].partition_broadcast(128), writes=[c.crow])
    S.op("dve", lambda e: e.tensor_scalar(out=c.ncrow.ap, in0=c.crow.ap, scalar1=-1.0, scalar2=None, op0=ALU.mult),
         reads=[c.crow], writes=[c.ncrow])
    for ch in range(2 * 128 * 128 // 512):
        oht = c.oht[ch % 2]
        S.dma("sp", oht.ap, c.oh[:, ch * 512:(ch + 1) * 512], writes=[oht])
        pst = c.ps()
        S.op("pe", lambda e: e.matmul(pst.ap[0:4, :], lhsT=c.relx.ap, rhs=oht.ap, start=True, stop=True),
             reads=[c.relx, oht], writes=[pst])
        b4 = c.b4[ch % 2]
        S.op("act", lambda e: e.activation(out=b4.ap, in_=pst.ap[0:4, :], func=AF.Copy), reads=[pst], writes=[b4])
        S.dma("pool", c.bias_s[:, ch * 512:(ch + 1) * 512], b4.ap, reads=[b4], writes=[S.db("bias_s")])
    for h in range(4):
        for t in range(2):
            S.dma("sp", c.EB.ap[:, h, t, :], c.bias_s[h, t * 16384:(t + 1) * 16384].rearrange("(k q) -> k q", q=128),
                  reads=[S.db("bias_s")], writes=[c.EB])
    for h in range(4):
        S.op("act", lambda e, h=h: e.activation(out=c.EB.ap[:, h, :, :], in_=c.EB.ap[:, h, :, :], func=AF.Exp,
                                                bias=c.ncrow.ap[:, h:h + 1]), reads=[c.EB, c.ncrow], writes=[c.EB])


def even_params(B, li, ei):
    c, S = B, B.S
    lambda_init = 0.8 - 0.6 * math.exp(-0.3 * li)
    for dst, src in ((c.qnw8, c.qn_w), (c.knw8, c.kn_w)):
        for hf in range(2):
            S.dma("sp", dst.ap[hf * 64:(hf + 1) * 64, :], src[ei].rearrange("(p o) -> p o", o=1), writes=[dst])
        S.op("dve", lambda e, dst=dst: e.tensor_scalar(out=dst.ap, in0=dst.ap, scalar1=8.0, scalar2=None, op0=ALU.mult),
             reads=[dst], writes=[dst])
    for i in range(4):
        S.dma("sp", c.gcw.ap[:, :, i], c.gdn_conv_w[ei, i].rearrange("(j p) -> p j", p=128), writes=[c.gcw],
              allow_slow_non_contiguous=True)
    S.dma("sp", c.dtb.ap, c.gdn_dt_bias[ei:ei + 1, :].partition_broadcast(128), writes=[c.dtb])
    S.dma("sp", c.negA.ap, c.gdn_a_log[ei:ei + 1, :].partition_broadcast(128), writes=[c.negA])
    S.op("act", lambda e: e.activation(out=c.negA.ap, in_=c.negA.ap, func=AF.Exp), reads=[c.negA], writes=[c.negA])
    S.op("dve", lambda e: e.tensor_scalar(out=c.negA.ap, in0=c.negA.ap, scalar1=-1.0, scalar2=None, op0=ALU.mult),
         reads=[c.negA], writes=[c.negA])
    for i, src in enumerate((c.lam_q1, c.lam_k1, c.lam_q2, c.lam_k2)):
        S.dma("sp", c.lamv.ap[:, i, :], src[ei:ei + 1, :].partition_broadcast(128), writes=[c.lamv])
    for i in range(2):
        S.op("dve", lambda e, i=i: e.tensor_tensor(out=c.lamv.ap[:, 2 * i, :], in0=c.lamv.ap[:, 2 * i, :],
                                                   in1=c.lamv.ap[:, 2 * i + 1, :], op=ALU.mult), reads=[c.lamv], writes=[c.lamv])
        S.op("dve", lambda e, i=i: e.tensor_reduce(out=c.lams.ap[:, i:i + 1], in_=c.lamv.ap[:, 2 * i, :], axis=AX.X, op=ALU.add),
             reads=[c.lamv], writes=[c.lams])
    S.op("act", lambda e: e.activation(out=c.lams.ap[:, 0:2], in_=c.lams.ap[:, 0:2], func=AF.Exp), reads=[c.lams], writes=[c.lams])
    S.op("dve", lambda e: e.tensor_tensor(out=c.lams.ap[:, 2:3], in0=c.lams.ap[:, 1:2], in1=c.lams.ap[:, 0:1], op=ALU.subtract),
         reads=[c.lams], writes=[c.lams])
    S.op("dve", lambda e: e.tensor_scalar(out=c.nlam.ap, in0=c.lams.ap[:, 2:3], scalar1=-lambda_init, scalar2=None, op0=ALU.add),
         reads=[c.lams], writes=[c.nlam])
    S.dma("sp", c.sw_row.ap, c.subln_w[ei:ei + 1, :].partition_broadcast(128), writes=[c.sw_row])
    S.op("dve", lambda e: e.tensor_scalar(out=c.sw_row.ap, in0=c.sw_row.ap, scalar1=1.0 - lambda_init, scalar2=None, op0=ALU.mult),
         reads=[c.sw_row], writes=[c.sw_row])
    S.dma("sp", c.gnw_row.ap, c.gdn_norm_w[ei:ei + 1, :].partition_broadcast(128), writes=[c.gnw_row])


def even_proj(B, li, ei):
    c, S = B, B.S
    G = c.TP // 512
    _proj_bufs(B, li)
    c.sqb = [c.ar([128, 512], BF16, f"sqb{i}") for i in range(2)]
    c.rsb = [c.ar([128, 512], F32, f"rsb{i}") for i in range(2)]
    c.qob = [c.ar([128, 512], BF16, f"qob{i}") for i in range(2)]
    c.kfb = [c.ar([128, 512], F32, f"kfb{i}") for i in range(2)]
    c.kob = [c.ar([128, 512], BF16, f"kob{i}") for i in range(2)]
    c.kout = [c.ar([128, 4, 128], F32, f"kout{i}") for i in range(2)]
    c.vf = [c.ar([128, 512], F32, f"vf{i}") for i in range(2)]
    c.vbf = [c.ar([128, 512], BF16, f"vbf{i}") for i in range(2)]
    c.zab = [c.ar([128, 512], BF16, f"zab{i}") for i in range(2)]
    c.zbb = [c.ar([128, 512], BF16, f"zbb{i}") for i in range(2)]
    c.gbt = [c.ar([128, 4, 8], F32, f"gbt{i}") for i in range(2)]
    c.CBw = [c.ar([128, 515], F32, f"CBw{i}") for i in range(2)]
    c.gcar = c.ar([128, 12, 3], F32, "gcar")
    c.gtmp = [c.ar([128, 512], F32, f"gtmp{i}") for i in range(4)]
    c.gout = [c.ar([128, 512], F32, f"gout{i}") for i in range(2)]
    S.op("pool", lambda e: e.memset(c.gcar.ap, 0.0), writes=[c.gcar])
    nb = 0
    for g in range(G):
        xg = _load_xg(B, li, g)
        hT = c.hT[g % 2]
        _norm_T(B, xg, li, hT)
        sec = c.cfg.get("sec", 127)
        for kind in (("q", "k") if sec & 1 else ()):
            for h in range(4):
                pq = c.ps()
                _proj_fm(B, hT, (0 if kind == "q" else 512) + h * 128, pq)
                sq = c.sqb[nb % 2]
                rs = c.rsb[nb % 2]
                nb += 1
                S.op("act", lambda e: e.activation(out=sq.ap, in_=pq.ap, func=AF.Square), reads=[pq], writes=[sq])
                p2 = c.ps()
                S.op("pe", lambda e: e.matmul(p2.ap, lhsT=c.bones_b, rhs=sq.ap, start=True, stop=True),
                     reads=[c.cbf, sq], writes=[p2])
                _rstd(B, rs, p2.ap, [p2], 1.0, 64 * EPS, 512)
                if kind == "q":
                    qo = c.qob[h % 2]
                    S.op("dve", lambda e: e.scalar_tensor_tensor(out=qo.ap, in0=pq.ap, scalar=c.qnw8.ap[:, 0:1], in1=rs.ap,
                                                                  op0=ALU.mult, op1=ALU.mult), reads=[pq, c.qnw8, rs], writes=[qo])
                    S.dma("pool", c.qT_s[h, :, g * 512:(g + 1) * 512], qo.ap, reads=[qo], writes=[S.db("qT_s", h)])
                else:
                    kf = c.kfb[h % 2]
                    ko = c.kob[h % 2]
                    S.op("dve", lambda e: e.scalar_tensor_tensor(out=kf.ap, in0=pq.ap, scalar=c.knw8.ap[:, 0:1], in1=rs.ap,
                                                                  op0=ALU.mult, op1=ALU.mult), reads=[pq, c.knw8, rs], writes=[kf])
                    S.op("pool", lambda e: e.tensor_copy(out=ko.ap, in_=kf.ap), reads=[kf], writes=[ko])
                    S.dma("pool", c.kT_s[h, :, g * 512:(g + 1) * 512], ko.ap, reads=[ko], writes=[S.db("kT_s", h)])
                    pt = c.ps()
                    for t in range(4):
                        S.op("pe", lambda e, t=t: e.transpose(out=pt.ap[:, t * 128:(t + 1) * 128], in_=kf.ap[:, t * 128:(t + 1) * 128],
                                                              identity=c.ident), reads=[kf, c.cstb], writes=[pt], sig=(t == 3))
                    kout = c.kout[h % 2]
                    S.op("act", lambda e: e.activation(out=kout.ap, in_=pt.ap.rearrange("p (t d) -> p t d", t=4), func=AF.Copy),
                         reads=[pt], writes=[kout])
                    S.dma("pool", c.k_prompt[ei, g * 512:(g + 1) * 512, h * 128:(h + 1) * 128].rearrange("(t p) d -> p t d", p=128),
                          kout.ap, reads=[kout])
        for j in (range(12) if sec & 2 else ()):
            pb = c.ps()
            _proj_fm(B, hT, 2048 + j * 128, pb)
            CB = c.CBw[j % 2]
            S.op("pool", lambda e, j=j: e.tensor_copy(out=CB.ap[:, 0:3], in_=c.gcar.ap[:, j, :]), reads=[c.gcar], writes=[CB])
            S.op("act", lambda e: e.activation(out=CB.ap[:, 3:515], in_=pb.ap, func=AF.Copy), reads=[pb], writes=[CB])
            acc = c.gtmp[(2 * j) % 4]
            sl = c.gtmp[(2 * j + 1) % 4]
            S.op("dve", lambda e: e.tensor_scalar(out=acc.ap, in0=CB.ap[:, 0:512], scalar1=c.gcw.ap[:, j, 0:1], scalar2=None,
                                                  op0=ALU.mult), reads=[CB, c.gcw], writes=[acc])
            for i in (1, 2, 3):
                S.op("dve", lambda e, i=i: e.scalar_tensor_tensor(out=acc.ap, in0=CB.ap[:, i:i + 512], scalar=c.gcw.ap[:, j, i:i + 1],
                                                                  in1=acc.ap, op0=ALU.mult, op1=ALU.add),
                     reads=[CB, c.gcw, acc], writes=[acc])
            S.op("act", lambda e: e.activation(out=sl.ap, in_=acc.ap, func=AF.Silu), reads=[acc], writes=[sl])
            go = c.gout[j % 2]
            if j < 8:
                sq = c.sqb[nb % 2]
                rs = c.rsb[nb % 2]
                nb += 1
                S.op("act", lambda e: e.activation(out=sq.ap, in_=sl.ap, func=AF.Square), reads=[sl], writes=[sq])
                p2 = c.ps()
                S.op("pe", lambda e: e.matmul(p2.ap, lhsT=c.ones_b, rhs=sq.ap, start=True, stop=True),
                     reads=[c.cbf, sq], writes=[p2])
                _rstd(B, rs, p2.ap, [p2], 1.0, EPS, 512)
                scl = (128 ** -0.5) if j < 4 else 1.0
                S.op("dve", lambda e: e.scalar_tensor_tensor(out=go.ap, in0=sl.ap, scalar=scl, in1=rs.ap, op0=ALU.mult, op1=ALU.mult),
                     reads=[sl, rs], writes=[go])
            else:
                S.op("pool", lambda e: e.tensor_copy(out=go.ap, in_=sl.ap), reads=[sl], writes=[go])
            S.dma("pool", c.g_s[j, :, g * 512:(g + 1) * 512], go.ap, reads=[go], writes=[S.db("g_s", j)])
            if g == G - 1:
                S.dma("pool", c.gdn_conv_prompt[ei, :, j * 128:(j + 1) * 128].rearrange("i p -> p i"), CB.ap[:, 512:515],
                      reads=[CB], allow_slow_non_contiguous=True)
            else:
                S.op("pool", lambda e, j=j: e.tensor_copy(out=c.gcar.ap[:, j, :], in_=CB.ap[:, 512:515]), reads=[CB], writes=[c.gcar])
        gbt = c.gbt[g % 2]
        if not (sec & 4):
            continue
        for t in range(4):
            vf, vbf, zab, zbb = c.vf[t % 2], c.vbf[t % 2], c.zab[t % 2], c.zbb[t % 2]
            r0 = g * 512 + t * 128
            pv = c.ps()
            _proj_tm(B, hT, t, 1024, 512, pv)
            sub = c.cfg.get("sub", 3)
            if sub & 1:
                S.op("act", lambda e: e.activation(out=vf.ap, in_=pv.ap, func=AF.Copy), reads=[pv], writes=[vf])
            if sub & 2:
                S.op("dve", lambda e: e.tensor_copy(out=vbf.ap, in_=pv.ap), reads=[pv], writes=[vbf])
            if sec & 16:
                S.dma("pool", c.v_prompt[ei, r0:r0 + 128, :], vf.ap, reads=[vf])
            if sec & 32:
                S.dma("pool", c.v_s[r0:r0 + 128, :], vbf.ap, reads=[vbf], writes=[S.db("v_s")])
            if not (sec & 64):
                continue
            pz = c.ps()
            _proj_tm(B, hT, t, 1536, 512, pz)
            S.op("act", lambda e: e.activation(out=zab.ap, in_=pz.ap, func=AF.Silu), reads=[pz], writes=[zab])
            S.dma("pool", c.za_s[r0:r0 + 128, :], zab.ap, reads=[zab], writes=[S.db("za_s")])
            pz2 = c.ps()
            _proj_tm(B, hT, t, 3584, 512, pz2)
            S.op("act", lambda e: e.activation(out=zbb.ap, in_=pz2.ap, func=AF.Silu), reads=[pz2], writes=[zbb])
            S.dma("pool", c.zb_s[r0:r0 + 128, :], zbb.ap, reads=[zbb], writes=[S.db("zb_s")])
            if not (sec & 8):
                continue
            pa = c.ps()
            _proj_tm(B, hT, t, 4096, 8, pa)
            S.op("dve", lambda e, t=t: e.tensor_tensor(out=gbt.ap[:, t, 0:4], in0=pa.ap[:, 0:4], in1=c.dtb.ap, op=ALU.add),
                 reads=[pa, c.dtb], writes=[gbt])
            S.op("dve", lambda e, t=t: e.tensor_copy(out=gbt.ap[:, t, 4:8], in_=pa.ap[:, 4:8]), reads=[pa], writes=[gbt])
        if not (sec & 8):
            continue
        S.op("act", lambda e: e.activation(out=gbt.ap[:, :, 0:4], in_=gbt.ap[:, :, 0:4], func=AF.Exp), reads=[gbt], writes=[gbt])
        S.op("act", lambda e: e.activation(out=gbt.ap[:, :, 0:4], in_=gbt.ap[:, :, 0:4], func=AF.Ln, bias=1.0), reads=[gbt], writes=[gbt])
        for t in range(4):
            S.op("dve", lambda e, t=t: e.tensor_tensor(out=gbt.ap[:, t, 0:4], in0=gbt.ap[:, t, 0:4], in1=c.negA.ap, op=ALU.mult),
                 reads=[gbt, c.negA], writes=[gbt])
        S.op("act", lambda e: e.activation(out=gbt.ap[:, :, 4:8], in_=gbt.ap[:, :, 4:8], func=AF.Sigmoid), reads=[gbt], writes=[gbt])
        rr = lambda ap: ap[g * 512:(g + 1) * 512, :].rearrange("(t p) d -> p t d", p=128)
        S.dma("pool", rr(c.gb_s), gbt.ap, reads=[gbt], writes=[S.db("gb_s")], allow_slow_non_contiguous=True)


def even_attn(B, li, ei):
    c, S = B, B.S
    NT = c.TP // 128
    G = c.TP // 512
    step = 0
    c.arena_reset()
    c.KT = [c.ar([128, c.TP], BF16, "KT0")]
    c.QT = [c.ar([128, c.TP], BF16, "QT0")]
    c.Vx = [c.ar([128, NT, 136], BF16, "Vx0")]
    S.op("pool", lambda e: e.memset(c.Vx[0].ap[:, :, 128:136], 1.0), writes=[c.Vx[0]])
    c.pT = [[c.ar([128, 512], BF16, f"pT{a}{m}") for m in range(2)] for a in range(2)]
    c.zat = [c.ar([128, 4, 128], BF16, f"zat{i}") for i in range(2)]
    c.ea = [c.ar([128, 128], F32, f"ea{i}") for i in range(6)]
    c.nea = 0
    c.ug = [c.ar([128, 128], BF16, f"ug{i}") for i in range(4)]
    c.nug = 0
    for h in range(4):
        KT, QT, Vx = c.KT[0], c.QT[0], c.Vx[0]
        S.dma("sp", KT.ap, c.kT_s[h], reads=[S.db("kT_s", h)], writes=[KT])
        S.dma("sp", QT.ap, c.qT_s[h], reads=[S.db("qT_s", h)], writes=[QT])
        S.dma("sp", Vx.ap[:, :, 0:128], c.v_s[:, h * 128:(h + 1) * 128].rearrange("(t p) d -> p t d", p=128),
              reads=[S.db("v_s")], writes=[Vx])
        for qg in range(G):
            zat = c.zat[qg % 2]
            S.dma("sp", zat.ap, c.za_s[qg * 512:(qg + 1) * 512, h * 128:(h + 1) * 128].rearrange("(t p) d -> p t d", p=128),
                  reads=[S.db("za_s")], writes=[zat])

            def O(m, i):
                idx = m * 4 + i
                return c.psb[4 + idx // 3], (idx % 3) * 129
            for bk in (4, 5, 6):
                S.op("pe", lambda e, bk=bk: e.matmul(c.psb[bk].ap[:, 0:387], lhsT=c.zeros_b.ap[:, 0:128], rhs=c.zeros_b.ap[:, 0:387],
                                                     start=True, stop=True, skip_group_check=True), reads=[c.zeros_b], writes=[c.psb[bk]])
            for kt in range(4 * qg + 4):
                i0 = max(0, kt - 4 * qg)
                pS = [c.psb[2 * (step % 2) + m] for m in range(2)]
                pT = c.pT[step % 2]
                step += 1
                for m in range(2):
                    S.op("pe", lambda e, m=m: e.matmul(pS[m].ap[:, i0 * 128:512], lhsT=KT.ap[m * 64:(m + 1) * 64, kt * 128:(kt + 1) * 128],
                                                       rhs=QT.ap[m * 64:(m + 1) * 64, qg * 512 + i0 * 128:(qg + 1) * 512],
                                                       start=True, stop=True), reads=[KT, QT], writes=[pS[m]])
                    S.op("act", lambda e, m=m: e.activation(out=pT[m].ap[:, i0 * 128:512], in_=pS[m].ap[:, i0 * 128:512], func=AF.Exp,
                                                            scale=0.125, bias=c.crow.ap[:, h:h + 1]),
                         reads=[pS[m], c.crow], writes=[pT[m]])
                    for i in range(i0, 4):
                        qt = 4 * qg + i
                        if kt == qt or kt == qt - 1:
                            tt = 0 if kt == qt else 1
                            S.op("dve", lambda e, m=m, i=i, tt=tt: e.tensor_tensor(
                                out=pT[m].ap[:, i * 128:(i + 1) * 128], in0=pT[m].ap[:, i * 128:(i + 1) * 128],
                                in1=c.EB.ap[:, h, tt, :], op=ALU.mult), reads=[pT[m], c.EB], writes=[pT[m]])
                for i in range(i0, 4):
                    qt = 4 * qg + i
                    for m in range(2):
                        ob, off = O(m, i)
                        S.op("pe", lambda e, m=m, i=i, ob=ob, off=off: e.matmul(
                            ob.ap[:, off:off + 129], lhsT=pT[m].ap[:, i * 128:(i + 1) * 128], rhs=Vx.ap[:, kt, 0:129],
                            start=False, stop=(kt == qt), skip_group_check=True), reads=[pT[m], Vx], writes=[ob])
                    if kt == qt:
                        (o1b, o1), (o2b, o2) = O(0, i), O(1, i)
                        st = c.st[c.nst % 4]
                        c.nst += 1
                        ta, oa, t2 = c.ea[c.nea % 6], c.ea[(c.nea + 1) % 6], c.ea[(c.nea + 2) % 6]
                        c.nea += 3
                        S.op("dve", lambda e: e.reciprocal(out=st.ap[:, 0:1], in_=o1b.ap[:, o1 + 128:o1 + 129]), reads=[o1b], writes=[st])
                        S.op("dve", lambda e: e.reciprocal(out=st.ap[:, 1:2], in_=o2b.ap[:, o2 + 128:o2 + 129]), reads=[o2b], writes=[st])
                        S.op("dve", lambda e: e.tensor_tensor(out=st.ap[:, 1:2], in0=st.ap[:, 1:2], in1=c.nlam.ap, op=ALU.mult),
                             reads=[st, c.nlam], writes=[st])
                        S.op("act", lambda e: e.activation(out=ta.ap, in_=o1b.ap[:, o1:o1 + 128], func=AF.Copy, scale=st.ap[:, 0:1]),
                             reads=[o1b, st], writes=[ta])
                        S.op("dve", lambda e: e.scalar_tensor_tensor(out=oa.ap, in0=o2b.ap[:, o2:o2 + 128], scalar=st.ap[:, 1:2], in1=ta.ap,
                                                                      op0=ALU.mult, op1=ALU.add), reads=[o2b, st, ta], writes=[oa])
                        S.op("act", lambda e: e.activation(out=t2.ap, in_=oa.ap, func=AF.Square, accum_out=st.ap[:, 2:3]),
                             reads=[oa], writes=[t2, st])
                        S.op("dve", lambda e: e.tensor_scalar(out=st.ap[:, 2:3], in0=st.ap[:, 2:3], scalar1=1.0 / 128, scalar2=EPS,
                                                              op0=ALU.mult, op1=ALU.add), reads=[st], writes=[st])
                        _rsq(B, st, st.ap[:, 2:3])
                        S.op("dve", lambda e: e.scalar_tensor_tensor(out=t2.ap, in0=oa.ap, scalar=st.ap[:, 2:3], in1=c.sw_row.ap,
                                                                      op0=ALU.mult, op1=ALU.mult), reads=[oa, st, c.sw_row], writes=[t2])
                        ug = c.ug[c.nug % 4]
                        c.nug += 1
                        S.op("pool", lambda e, i=i: e.tensor_tensor(out=ug.ap, in0=t2.ap, in1=zat.ap[:, i, :], op=ALU.mult),
                             reads=[t2, zat], writes=[ug])
                        S.dma("pool", c.u_s[qt * 128:(qt + 1) * 128, h * 128:(h + 1) * 128], ug.ap, reads=[ug],
                              writes=[S.db("u_s", qt)])


def even_out(B, li, ei):
    c, S = B, B.S
    NT = c.TP // 128
    last = (li == len(c.layers) - 1)
    c.arena_reset()
    c.ut = [c.ar([128, D], BF16, f"ut{i}") for i in range(2)]
    c.uTt = [c.ar([128, 8, 128], BF16, f"uTt{i}") for i in range(2)]
    c.xt = [c.ar([128, 1, D], F32, f"xt{i}") for i in range(2)]
    for t in range(NT):
        ut, uTt, xt = c.ut[t % 2], c.uTt[t % 2], c.xt[t % 2]
        S.dma("sp", ut.ap, c.u_s[t * 128:(t + 1) * 128, :], reads=[S.db("u_s", t)], writes=[ut])
        src = c.x_prompt if li == 0 else c.xres
        S.dma("sp", xt.ap[:, 0, :], src[t * 128:(t + 1) * 128, :], reads=[S.db("xres", t)], writes=[xt])
        pst = c.ps()
        pv = pst.ap.bitcast(BF16).rearrange("p (k t) -> p k t", k=8)
        for cb in range(8):
            S.op("pe", lambda e, cb=cb: e.transpose(out=pv[:, cb, :], in_=ut.ap[:, cb * 128:(cb + 1) * 128], identity=c.ident_b),
                 reads=[ut, c.cbf], writes=[pst], sig=(cb == 7))
        S.op("act", lambda e: e.activation(out=uTt.ap, in_=pv, func=AF.Copy), reads=[pst], writes=[uTt])
        for half in range(2):
            py = c.ps()
            for cb in range(8):
                S.op("pe", lambda e, cb=cb: e.matmul(py.ap, lhsT=uTt.ap[:, cb, :], rhs=c.wout.ap[:, cb, half * 512:(half + 1) * 512],
                                                     start=(cb == 0), stop=(cb == 7)), reads=[uTt, c.wout], writes=[py], sig=(cb == 7))
            S.op("dve", lambda e: e.tensor_tensor(out=xt.ap[:, 0, half * 512:(half + 1) * 512], in0=py.ap,
                                                  in1=xt.ap[:, 0, half * 512:(half + 1) * 512], op=ALU.add), reads=[py, xt], writes=[xt])
        dst = c.y_prompt if last else c.xres
        S.dma("pool", dst[t * 128:(t + 1) * 128, :], xt.ap[:, 0, :], reads=[xt], writes=[S.db("xres", t)])


def even_gdn(B, li, ei):
    c, S = B, B.S
    NT = c.TP // 128
    c.arena_reset()
    A = lambda n: c.ar([128, 128], F32, n)
    Sst = [[A(f"S{h}_{i}") for i in range(2)] for h in range(4)]
    names = ("qT", "kT", "vT", "kbg", "kg", "vb", "dg", "tmp", "dLs", "dTs", "EG", "qgT", "M0", "M1", "N0", "N1", "P0", "P1",
             "attnT", "wTn", "vnew", "o1", "o2")
    W = [{n: A(f"{n}{hs}") for n in names} for hs in range(2)]
    gbl = [c.ar([128, 8], F32, f"gbl{i}") for i in range(2)]
    gcs = [c.ar([128, 24], F32, f"gcs{i}") for i in range(2)]
    zbt = [c.ar([128, 512], BF16, f"zbt{i}") for i in range(2)]
    ub = [c.ar([128, 128], BF16, f"ub{i}") for i in range(4)]
    nub = 0
    for h in range(4):
        S.op("pool", lambda e, h=h: e.memset(Sst[h][0].ap, 0.0), writes=[Sst[h][0]])
    negL = c.cstb.ap[:, 896:1024]
    negU = c.negm_s
    cp = lambda dst, src_ap, srcb, **kw: S.op("act", lambda e: e.activation(out=dst.ap, in_=src_ap, func=AF.Copy, **kw),
                                              reads=srcb, writes=[dst])
    for tt in range(NT):
        gb, gc, zb = gbl[tt % 2], gcs[tt % 2], zbt[tt % 2]
        r0 = tt * 128
        S.dma("sp", gb.ap, c.gb_s[r0:r0 + 128, :], reads=[S.db("gb_s")], writes=[gb])
        S.dma("sp", zb.ap, c.zb_s[r0:r0 + 128, :], reads=[S.db("zb_s")], writes=[zb])
        pg = c.ps()
        S.op("pe", lambda e: e.matmul(pg.ap[:, 0:4], lhsT=c.tri, rhs=gb.ap[:, 0:4], start=True, stop=True), reads=[c.cstb, gb], writes=[pg])
        S.op("pe", lambda e: e.matmul(pg.ap[:, 4:8], lhsT=c.ones, rhs=gb.ap[:, 0:4], start=True, stop=True), reads=[c.cstb, gb], writes=[pg])
        S.op("dve", lambda e: e.tensor_copy(out=gc.ap[:, 0:4], in_=pg.ap[:, 0:4]), reads=[pg], writes=[gc])
        S.op("dve", lambda e: e.tensor_scalar(out=gc.ap[:, 4:8], in0=gc.ap[:, 0:4], scalar1=-1.0, scalar2=None, op0=ALU.mult), reads=[gc], writes=[gc])
        S.op("dve", lambda e: e.tensor_tensor(out=gc.ap[:, 12:16], in0=pg.ap[:, 4:8], in1=gc.ap[:, 0:4], op=ALU.subtract), reads=[pg, gc], writes=[gc])
        S.op("dve", lambda e: e.tensor_copy(out=gc.ap[:, 16:20], in_=pg.ap[:, 4:8]), reads=[pg], writes=[gc])
        S.op("act", lambda e: e.activation(out=gc.ap[:, 8:12], in_=gc.ap[:, 0:4], func=AF.Exp), reads=[gc], writes=[gc])
        S.op("act", lambda e: e.activation(out=gc.ap[:, 12:20], in_=gc.ap[:, 12:20], func=AF.Exp), reads=[gc], writes=[gc])
        S.op("dve", lambda e: e.tensor_tensor(out=gc.ap[:, 20:24], in0=gb.ap[:, 4:8], in1=gc.ap[:, 8:12], op=ALU.mult), reads=[gb, gc], writes=[gc])
        for h in range(4):
            w = W[h % 2]
            Sc, Sn = Sst[h][tt % 2], Sst[h][(tt + 1) % 2]
            col = lambda k: gc.ap[:, k * 4 + h:k * 4 + h + 1]
            S.dma("sp", w["qT"].ap, c.g_s[h, :, r0:r0 + 128], reads=[S.db("g_s", h)], writes=[w["qT"]])
            S.dma("sp", w["kT"].ap, c.g_s[4 + h, :, r0:r0 + 128], reads=[S.db("g_s", 4 + h)], writes=[w["kT"]])
            S.dma("sp", w["vT"].ap, c.g_s[8 + h, :, r0:r0 + 128], reads=[S.db("g_s", 8 + h)], writes=[w["vT"]])
            pk = c.ps()
            S.op("pe", lambda e: e.transpose(out=pk.ap[:, 0:128], in_=w["kT"].ap, identity=c.ident), reads=[w["kT"], c.cstb], writes=[pk])
            S.op("pe", lambda e: e.transpose(out=pk.ap[:, 128:256], in_=w["vT"].ap, identity=c.ident), reads=[w["vT"], c.cstb], writes=[pk])
            cp(w["kbg"], pk.ap[:, 0:128], [pk, gc], scale=col(5))
            cp(w["kg"], pk.ap[:, 0:128], [pk, gc], scale=col(3))
            cp(w["vb"], pk.ap[:, 128:256], [pk, gb], scale=gb.ap[:, 4 + h:5 + h])
            S.op("dve", lambda e: e.tensor_scalar(out=w["dg"].ap, in0=c.ident, scalar1=col(0), scalar2=None, op0=ALU.mult),
                 reads=[c.cstb, gc], writes=[w["dg"]])
            pr = c.ps()
            S.op("pe", lambda e: e.matmul(pr.ap[:, 0:128], lhsT=c.ones, rhs=w["dg"].ap, start=True, stop=True), reads=[c.cstb, w["dg"]], writes=[pr])
            S.op("dve", lambda e: e.scalar_tensor_tensor(out=w["tmp"].ap, in0=pr.ap[:, 0:128], scalar=-1.0, in1=negL, op0=ALU.mult, op1=ALU.add),
                 reads=[pr, c.cstb], writes=[w["tmp"]])
            S.op("act", lambda e: e.activation(out=w["dLs"].ap, in_=w["tmp"].ap, func=AF.Exp, bias=col(0)), reads=[w["tmp"], gc], writes=[w["dLs"]])
            S.op("dve", lambda e: e.tensor_tensor(out=w["tmp"].ap, in0=pr.ap[:, 0:128], in1=negU, op=ALU.add), reads=[pr, c.cstb, w["dLs"]], writes=[w["tmp"]])
            S.op("act", lambda e: e.activation(out=w["dTs"].ap, in_=w["tmp"].ap, func=AF.Exp, bias=col(1)), reads=[w["tmp"], gc], writes=[w["dTs"]])
            S.op("act", lambda e: e.activation(out=w["EG"].ap, in_=pr.ap[:, 0:128], func=AF.Exp), reads=[pr], writes=[w["EG"]])
            S.op("pool", lambda e: e.tensor_tensor(out=w["qgT"].ap, in0=w["qT"].ap, in1=w["EG"].ap, op=ALU.mult), reads=[w["qT"], w["EG"]], writes=[w["qgT"]])
            S.op("pool", lambda e: e.tensor_tensor(out=w["dTs"].ap, in0=w["dTs"].ap, in1=c.ident, op=ALU.add), reads=[w["dTs"], c.cstb], writes=[w["dTs"]])
            pkk = c.ps()
            S.op("pe", lambda e: e.matmul(pkk.ap[:, 0:128], lhsT=w["kT"].ap, rhs=w["kT"].ap, start=True, stop=True), reads=[w["kT"]], writes=[pkk])
            S.op("pe", lambda e: e.matmul(pkk.ap[:, 128:256], lhsT=w["kT"].ap, rhs=w["qT"].ap, start=True, stop=True), reads=[w["kT"], w["qT"]], writes=[pkk])
            S.op("dve", lambda e: e.scalar_tensor_tensor(out=w["M0"].ap, in0=pkk.ap[:, 0:128], scalar=gb.ap[:, 4 + h:5 + h], in1=w["dLs"].ap,
                                                          op0=ALU.mult, op1=ALU.mult), reads=[pkk, gb, w["dLs"]], writes=[w["M0"]])
            S.op("dve", lambda e: e.tensor_tensor(out=w["attnT"].ap, in0=pkk.ap[:, 128:256], in1=w["dTs"].ap, op=ALU.mult),
                 reads=[pkk, w["dTs"]], writes=[w["attnT"]])
            pn = c.ps()
            S.op("pe", lambda e: e.transpose(out=pn.ap[:, 0:128], in_=w["M0"].ap, identity=c.ident), reads=[w["M0"], c.cstb], writes=[pn])
            cp(w["N0"], pn.ap[:, 0:128], [pn])
            S.op("dve", lambda e: e.tensor_tensor(out=w["P0"].ap, in0=c.ident, in1=pn.ap[:, 0:128], op=ALU.subtract), reads=[pn, c.cstb], writes=[w["P0"]])
            Mc, Nc, Pc = "M0", "N0", "P0"
            for lvl in range(1, 7):
                Mn, Nn, Pn = ("M1", "N1", "P1") if Mc == "M0" else ("M0", "N0", "P0")
                pm = c.ps()
                S.op("pe", lambda e, Mc=Mc, Nc=Nc: e.matmul(pm.ap[:, 0:128], lhsT=w[Nc].ap, rhs=w[Mc].ap, start=True, stop=True),
                     reads=[w[Nc], w[Mc]], writes=[pm])
                if lvl < 6:
                    S.op("pe", lambda e, Mc=Mc, Nc=Nc: e.matmul(pm.ap[:, 128:256], lhsT=w[Mc].ap, rhs=w[Nc].ap, start=True, stop=True),
                         reads=[w[Nc], w[Mc]], writes=[pm])
                cp(w[Mn], pm.ap[:, 0:128], [pm])
                if lvl < 6:
                    S.op("dve", lambda e, Nn=Nn: e.tensor_copy(out=w[Nn].ap, in_=pm.ap[:, 128:256]), reads=[pm], writes=[w[Nn]])
                pp = c.ps()
                S.op("pe", lambda e, Pc=Pc: e.matmul(pp.ap[:, 0:128], lhsT=c.ident, rhs=w[Pc].ap, start=True, stop=False),
                     reads=[w[Pc], c.cstb], writes=[pp], sig=False)
                S.op("pe", lambda e, Pc=Pc, Mn=Mn: e.matmul(pp.ap[:, 0:128], lhsT=w[Mn].ap, rhs=w[Pc].ap, start=False, stop=True),
                     reads=[w[Pc], w[Mn]], writes=[pp])
                cp(w[Pn], pp.ap[:, 0:128], [pp])
                Mc, Nc, Pc = Mn, Nn, Pn
            TT = w[Pc]
            pw = c.ps()
            S.op("pe", lambda e: e.matmul(pw.ap[:, 0:128], lhsT=w["kbg"].ap, rhs=TT.ap, start=True, stop=True), reads=[w["kbg"], TT], writes=[pw])
            S.op("act", lambda e: e.activation(out=w["wTn"].ap, in_=pw.ap[:, 0:128], func=AF.Copy, scale=-1.0), reads=[pw], writes=[w["wTn"]])
            pvn = c.ps()
            S.op("pe", lambda e: e.matmul(pvn.ap[:, 0:128], lhsT=TT.ap, rhs=w["vb"].ap, start=True, stop=False), reads=[TT, w["vb"]], writes=[pvn], sig=False)
            S.op("pe", lambda e: e.matmul(pvn.ap[:, 0:128], lhsT=w["wTn"].ap, rhs=Sc.ap, start=False, stop=True), reads=[w["wTn"], Sc], writes=[pvn])
            cp(w["vnew"], pvn.ap[:, 0:128], [pvn])
            po = c.ps()
            S.op("pe", lambda e: e.matmul(po.ap[:, 0:128], lhsT=w["qgT"].ap, rhs=Sc.ap, start=True, stop=False), reads=[w["qgT"], Sc], writes=[po], sig=False)
            S.op("pe", lambda e: e.matmul(po.ap[:, 0:128], lhsT=w["attnT"].ap, rhs=w["vnew"].ap, start=False, stop=True),
                 reads=[w["attnT"], w["vnew"]], writes=[po])
            S.op("pe", lambda e: e.matmul(po.ap[:, 128:256], lhsT=w["kg"].ap, rhs=w["vnew"].ap, start=True, stop=True),
                 reads=[w["kg"], w["vnew"]], writes=[po])
            S.op("dve", lambda e: e.scalar_tensor_tensor(out=Sn.ap, in0=Sc.ap, scalar=col(4), in1=po.ap[:, 128:256], op0=ALU.mult, op1=ALU.add),
                 reads=[Sc, gc, po], writes=[Sn])
            st = c.st[c.nst % 4]
            c.nst += 1
            S.op("act", lambda e: e.activation(out=w["o1"].ap, in_=po.ap[:, 0:128], func=AF.Square, accum_out=st.ap[:, 0:1]),
                 reads=[po], writes=[w["o1"], st])
            S.op("dve", lambda e: e.tensor_scalar(out=st.ap[:, 0:1], in0=st.ap[:, 0:1], scalar1=1.0 / 128, scalar2=EPS, op0=ALU.mult, op1=ALU.add),
                 reads=[st], writes=[st])
            _rsq(B, st, st.ap[:, 0:1])
            S.op("dve", lambda e: e.scalar_tensor_tensor(out=w["o2"].ap, in0=po.ap[:, 0:128], scalar=st.ap[:, 0:1], in1=c.gnw_row.ap,
                                                          op0=ALU.mult, op1=ALU.mult), reads=[po, st, c.gnw_row], writes=[w["o2"]])
            u = ub[nub % 4]
            nub += 1
            S.op("pool", lambda e, h=h: e.tensor_tensor(out=u.ap, in0=w["o2"].ap, in1=zb.ap[:, h * 128:(h + 1) * 128], op=ALU.mult),
                 reads=[w["o2"], zb], writes=[u])
            S.dma("pool", c.u_s[r0:r0 + 128, 512 + h * 128:512 + (h + 1) * 128], u.ap, reads=[u], writes=[S.db("u_s", tt)])
            if tt == NT - 1:
                S.dma("pool", c.gdn_prompt[ei, h], Sn.ap, reads=[Sn])


def prompt_even(B, li, ei):
    c = B
    c.load_w(c.win, c.w_in_even[ei], P_EVEN, c.wstg)
    c.load_w(c.wout, c.w_out_even[ei], D, c.wstg)
    upto = c.cfg.get("upto", 9)
    if upto >= 1:
        even_params(B, li, ei)
    if upto >= 2:
        even_proj(B, li, ei)
    if upto < 3:
        return
    if c.cfg.get("attn", True):
        even_attn(B, li, ei)
    if c.cfg.get("gdn", True):
        even_gdn(B, li, ei)
    even_out(B, li, ei)


def decode_all(B):
    c, S = B, B.S
    NS = c.NS
    NPG = c.NPAGES
    c.arena_reset()
    A = c.ar
    xs = A([NS, D], F32, "xs")
    S.dma("sp", xs.ap, c.x_sample, writes=[xs])
    hbd = A([NS, D], BF16, "hbd")
    jk = A([NS, D], BF16, "jkd")
    hTd = A([128, 8, NS], BF16, "hTd")
    ssd = A([NS, 2], F32, "ssd")
    pcol = c.cstb.ap[:, 1536:1537]
    sel = c.cstb.ap[0:4, 1024:1536]

    def evac(dst_ap, src_ap, srcb, dstb, eng="act", func=AF.Copy, **kw):
        S.op("act", lambda e: e.activation(out=dst_ap, in_=src_ap, func=func, **kw), reads=srcb, writes=dstb)

    def normT(li):
        S.dma("sp", c.normw.ap, c.norm_w[li:li + 1, :].partition_broadcast(128), writes=[c.normw])
        S.op("act", lambda e: e.activation(out=jk.ap, in_=xs.ap, func=AF.Square, accum_out=ssd.ap[:, 0:1]), reads=[xs], writes=[jk, ssd])
        S.op("dve", lambda e: e.tensor_scalar(out=ssd.ap[:, 0:1], in0=ssd.ap[:, 0:1], scalar1=1.0 / D, scalar2=EPS, op0=ALU.mult, op1=ALU.add),
             reads=[ssd], writes=[ssd])
        _rsq(B, ssd, ssd.ap[:, 0:1])
        S.op("dve", lambda e: e.scalar_tensor_tensor(out=hbd.ap, in0=xs.ap, scalar=ssd.ap[:, 0:1], in1=c.normw.ap[0:NS, :],
                                                      op0=ALU.mult, op1=ALU.mult), reads=[xs, ssd, c.normw], writes=[hbd])
        pst = c.ps()
        pv = pst.ap.bitcast(BF16)[:, 0:8 * NS].rearrange("p (k t) -> p k t", k=8)
        for kc in range(8):
            S.op("pe", lambda e, kc=kc: e.transpose(out=pv[:, kc, :], in_=hbd.ap[:, kc * 128:(kc + 1) * 128], identity=c.ident_b[0:NS, 0:NS]),
                 reads=[hbd, c.cbf], writes=[pst], sig=(kc == 7))
        evac(hTd.ap, pv, [pst], [hTd])

    def proj(pst, slot, col0, m=128):
        for kc in range(8):
            S.op("pe", lambda e, kc=kc: e.matmul(pst.ap[0:m, slot * NS:(slot + 1) * NS], lhsT=c.win.ap[:, kc, col0:col0 + m], rhs=hTd.ap[:, kc, :],
                                                 start=(kc == 0), stop=(kc == 7)), reads=[c.win, hTd], writes=[pst], sig=(kc == 7))

    def tr_out(src_ap, srcb, nblk, dst_dram_ap, tag):
        to = A([NS, nblk * 128], F32, "to_" + tag)
        for b0 in range(0, nblk, 4):
            nb = min(4, nblk - b0)
            pst = c.ps()
            for b in range(nb):
                S.op("pe", lambda e, b=b: e.transpose(out=pst.ap[0:NS, b * 128:(b + 1) * 128], in_=src_ap[:, b0 + b, :], identity=c.ident),
                     reads=srcb + [c.cstb], writes=[pst], sig=(b == nb - 1))
            evac(to.ap[:, b0 * 128:(b0 + nb) * 128], pst.ap[0:NS, 0:nb * 128], [pst], [to])
        S.dma("pool", dst_dram_ap, to.ap, reads=[to])

    def tr_in(dst, src_dram_ap, nblk, tag):
        ti = A([NS, nblk * 128], F32, "ti_" + tag)
        S.dma("sp", ti.ap, src_dram_ap, writes=[ti])
        for b0 in range(0, nblk, 16):
            nb = min(16, nblk - b0)
            pst = c.ps()
            for b in range(nb):
                S.op("pe", lambda e, b=b: e.transpose(out=pst.ap[:, b * NS:(b + 1) * NS], in_=ti.ap[:, (b0 + b) * 128:(b0 + b + 1) * 128],
                                                      identity=c.ident[0:NS, 0:NS]), reads=[ti, c.cstb], writes=[pst], sig=(b == nb - 1))
            evac(dst.ap[:, b0:b0 + nb, :], pst.ap[:, 0:nb * NS].rearrange("p (b s) -> p b s", b=nb), [pst], [dst])

    def colsum_bc(dst, src, ncol, lhsT=None):
        pst = c.ps()
        S.op("pe", lambda e: e.matmul(pst.ap[:, 0:ncol], lhsT=c.ones if lhsT is None else lhsT, rhs=src.ap, start=True, stop=True),
             reads=[src, c.cstb], writes=[pst])
        return pst

    def outproj(uT):
        for half in range(2):
            py = c.ps()
            for cb in range(8):
                S.op("pe", lambda e, cb=cb: e.matmul(py.ap[0:NS, :], lhsT=uT.ap[:, cb, :], rhs=c.wout.ap[:, cb, half * 512:(half + 1) * 512],
                                                     start=(cb == 0), stop=(cb == 7)), reads=[uT, c.wout], writes=[py], sig=(cb == 7))
            S.op("dve", lambda e: e.tensor_tensor(out=xs.ap[:, half * 512:(half + 1) * 512], in0=py.ap[0:NS, :],
                                                  in1=xs.ap[:, half * 512:(half + 1) * 512], op=ALU.add), reads=[py, xs], writes=[xs])

    amark = c.aoff
    ei = oi = 0
    for li, ch in enumerate(c.layers):
        c.arena_reset()
        c.aoff = amark
        uT = A([128, 8, NS], BF16, "uTd")
        if ch == "o":
            c.load_w(c.win, c.w_in_odd[oi], P_ODD, c.wstg)
            c.load_w(c.wout, c.w_out_odd[oi], D, c.wstg)
            for j in range(3):
                S.dma("sp", c.scw.ap[:, :, j], c.sc_conv_w[oi, j].rearrange("(c p) -> p c", p=128), writes=[c.scw], allow_slow_non_contiguous=True)
            normT(li)
            P4 = A([128, 32, NS], F32, "P4")
            for b0 in (0, 16):
                pst = c.ps()
                for b in range(16):
                    proj(pst, b, (b0 + b) * 128)
                evac(P4.ap[:, b0:b0 + 16, :], pst.ap[:, 0:16 * NS].rearrange("p (b s) -> p b s", b=16), [pst], [P4])
            st = A([128, 16, NS], F32, "scst")
            tr_in(st, c.state_shortconv[oi].rearrange("s j d -> s (j d)"), 16, "sc")
            chh = A([128, 8, NS], F32, "chh")
            cv = A([128, 8, NS], F32, "cv")
            S.op("dve", lambda e: e.tensor_tensor(out=chh.ap, in0=P4.ap[:, 8:16, :], in1=P4.ap[:, 16:24, :], op=ALU.mult), reads=[P4], writes=[chh])
            for cb in range(8):
                S.op("dve", lambda e, cb=cb: e.tensor_scalar(out=cv.ap[:, cb, :], in0=st.ap[:, cb, :], scalar1=c.scw.ap[:, cb, 0:1], scalar2=None,
                                                             op0=ALU.mult), reads=[st, c.scw], writes=[cv])
                S.op("dve", lambda e, cb=cb: e.scalar_tensor_tensor(out=cv.ap[:, cb, :], in0=st.ap[:, 8 + cb, :], scalar=c.scw.ap[:, cb, 1:2],
                                                                    in1=cv.ap[:, cb, :], op0=ALU.mult, op1=ALU.add), reads=[st, c.scw, cv], writes=[cv])
                S.op("dve", lambda e, cb=cb: e.scalar_tensor_tensor(out=cv.ap[:, cb, :], in0=chh.ap[:, cb, :], scalar=c.scw.ap[:, cb, 2:3],
                                                                    in1=cv.ap[:, cb, :], op0=ALU.mult, op1=ALU.add), reads=[chh, c.scw, cv], writes=[cv])
            sz = A([128, 8, NS], F32, "sz")
            evac(sz.ap, P4.ap[:, 24:32, :], [P4], [sz], func=AF.Silu)
            S.op("dve", lambda e: e.tensor_tensor(out=cv.ap, in0=cv.ap, in1=P4.ap[:, 0:8, :], op=ALU.mult), reads=[cv, P4], writes=[cv])
            S.op("dve", lambda e: e.tensor_tensor(out=uT.ap, in0=cv.ap, in1=sz.ap, op=ALU.mult), reads=[cv, sz], writes=[uT])
            S.dma("pool", c.sc_sample[oi, :, 0, :], c.state_shortconv[oi, :, 1, :])
            tr_out(chh.ap, [chh], 8, c.sc_sample[oi, :, 1, :], "sc")
            outproj(uT)
            oi += 1
            continue
        lambda_init = 0.8 - 0.6 * math.exp(-0.3 * li)
        c.load_w(c.win, c.w_in_even[ei], P_EVEN, c.wstg)
        c.load_w(c.wout, c.w_out_even[ei], D, c.wstg)
        even_params(B, li, ei)
        fmp = A([128, 8], F32, "fmp")
        S.dma("sp", fmp.ap[:, 0:1], c.subln_w[ei].rearrange("(p o) -> p o", o=1), writes=[fmp])
        S.dma("sp", fmp.ap[:, 1:2], c.gdn_norm_w[ei].rearrange("(p o) -> p o", o=1), writes=[fmp])
        S.dma("sp", fmp.ap[:, 2:6], c.rel_table[0:1, :].partition_broadcast(128), writes=[fmp])
        S.op("dve", lambda e: e.tensor_scalar(out=fmp.ap[:, 0:1], in0=fmp.ap[:, 0:1], scalar1=1.0 - lambda_init, scalar2=None, op0=ALU.mult),
             reads=[fmp], writes=[fmp])
        evac(fmp.ap[:, 2:6], fmp.ap[:, 2:6], [fmp], [fmp], func=AF.Exp)
        ab4 = A([4, 2], F32, "ab4")
        S.dma("sp", ab4.ap[:, 0:1], c.gdn_a_log[ei].rearrange("(p o) -> p o", o=1), writes=[ab4])
        S.dma("sp", ab4.ap[:, 1:2], c.gdn_dt_bias[ei].rearrange("(p o) -> p o", o=1), writes=[ab4])
        evac(ab4.ap[:, 0:1], ab4.ap[:, 0:1], [ab4], [ab4], func=AF.Exp)
        S.op("dve", lambda e: e.tensor_scalar(out=ab4.ap[:, 0:1], in0=ab4.ap[:, 0:1], scalar1=-1.0, scalar2=None, op0=ALU.mult), reads=[ab4], writes=[ab4])
        eb = A([128, 2, 8], F32, "ebd")
        ohd = A([33, 128], F32, "ohd")
        S.dma("sp", ohd.ap, c.oh[:, 2 * 128 * 128:2 * 128 * 128 + 128], writes=[ohd])
        pb_ = c.ps()
        S.op("pe", lambda e: e.matmul(pb_.ap[:, 0:4], lhsT=ohd.ap[0:32, :], rhs=c.relx.ap[0:32, :], start=True, stop=True), reads=[ohd, c.relx], writes=[pb_])
        for m in range(2):
            evac(eb.ap[:, 1, :].rearrange("p (h m) -> p h m", m=2)[:, :, m], pb_.ap[:, 0:4], [pb_], [eb], func=AF.Exp)
            evac(eb.ap[:, 0, :].rearrange("p (h m) -> p h m", m=2)[:, :, m], c.crow.ap, [c.crow], [eb], func=AF.Exp)
        normT(li)
        PQ = A([128, 8, NS], F32, "PQ")
        PVZ = A([128, 12, NS], F32, "PVZ")
        PG = A([128, 12, NS], F32, "PGd")
        pst = c.ps()
        for b in range(8):
            proj(pst, b, b * 128)
        evac(PQ.ap, pst.ap[:, 0:8 * NS].rearrange("p (b s) -> p b s", b=8), [pst], [PQ])
        pst = c.ps()
        for b in range(8):
            proj(pst, b, 1024 + b * 128)
        for b in range(4):
            proj(pst, 8 + b, 3584 + b * 128)
        evac(PVZ.ap, pst.ap[:, 0:12 * NS].rearrange("p (b s) -> p b s", b=12), [pst], [PVZ])
        pst = c.ps()
        for b in range(12):
            proj(pst, b, 2048 + b * 128)
        evac(PG.ap, pst.ap[:, 0:12 * NS].rearrange("p (b s) -> p b s", b=12), [pst], [PG])
        ga = A([4, 2, NS], F32, "ga")
        pst = c.ps()
        proj(pst, 0, 4096, 4)
        proj(pst, 1, 4100, 4)
        evac(ga.ap[:, 0, :], pst.ap[0:4, 0:NS], [pst, ab4], [ga], func=AF.Exp, bias=ab4.ap[:, 1:2])
        evac(ga.ap[:, 1, :], pst.ap[0:4, NS:2 * NS], [pst], [ga], func=AF.Sigmoid)
        evac(ga.ap[:, 0, :], ga.ap[:, 0, :], [ga], [ga], func=AF.Ln, bias=1.0)
        S.op("dve", lambda e: e.tensor_scalar(out=ga.ap[:, 0, :], in0=ga.ap[:, 0, :], scalar1=ab4.ap[:, 0:1], scalar2=None, op0=ALU.mult),
             reads=[ga, ab4], writes=[ga])
        EGB = A([128, 2, 4, NS], F32, "EGB")
        pst = c.ps()
        for t in range(2):
            for h in range(4):
                S.op("pe", lambda e, t=t, h=h: e.matmul(pst.ap[:, (t * 4 + h) * NS:(t * 4 + h + 1) * NS], lhsT=sel[:, h * 128:(h + 1) * 128],
                                                        rhs=ga.ap[:, t, :], start=True, stop=True), reads=[ga, c.cstb], writes=[pst])
        evac(EGB.ap[:, 0, :, :], pst.ap[:, 0:4 * NS].rearrange("p (h s) -> p h s", h=4), [pst], [EGB], func=AF.Exp)
        evac(EGB.ap[:, 1, :, :], pst.ap[:, 4 * NS:8 * NS].rearrange("p (h s) -> p h s", h=4), [pst], [EGB])
        sq = A([128, 8 * NS], BF16, "sqd")
        rs = A([128, 8, NS], F32, "rsd")
        evac(sq.ap, PQ.ap.rearrange("p b s -> p (b s)"), [PQ], [sq], func=AF.Square)
        p2 = c.ps()
        S.op("pe", lambda e: e.matmul(p2.ap[:, 0:8 * NS], lhsT=c.bones_b, rhs=sq.ap, start=True, stop=True), reads=[c.cbf, sq], writes=[p2])
        S.op("dve", lambda e: e.tensor_scalar(out=rs.ap.rearrange("p b s -> p (b s)"), in0=p2.ap[:, 0:8 * NS], scalar1=1.0, scalar2=64 * EPS,
                                              op0=ALU.mult, op1=ALU.add), reads=[p2], writes=[rs])
        _rsq(B, rs, rs.ap.rearrange("p b s -> p (b s)"))
        for t, wv in ((0, c.qnw8), (1, c.knw8)):
            S.op("dve", lambda e, t=t, wv=wv: e.scalar_tensor_tensor(out=PQ.ap[:, 4 * t:4 * t + 4, :], in0=PQ.ap[:, 4 * t:4 * t + 4, :], scalar=wv.ap[:, 0:1],
                                                                      in1=rs.ap[:, 4 * t:4 * t + 4, :], op0=ALU.mult, op1=ALU.mult),
                 reads=[PQ, wv, rs], writes=[PQ])
        tr_out(PQ.ap[:, 4:8, :], [PQ], 4, c.k_sample[ei], "k")
        tr_out(PVZ.ap[:, 0:4, :], [PVZ], 4, c.v_sample[ei], "v")
        prod = A([128, 4, NS], F32, "prodn")
        S.op("dve", lambda e: e.tensor_tensor(out=prod.ap, in0=PQ.ap[:, 0:4, :], in1=PQ.ap[:, 4:8, :], op=ALU.mult), reads=[PQ], writes=[prod])
        pnew = A([128, 2, 4, NS], F32, "pnew")
        for m in range(2):
            pst = c.ps()
            S.op("pe", lambda e, m=m: e.matmul(pst.ap[:, 0:4 * NS], lhsT=c.hones[:, m * 128:(m + 1) * 128], rhs=prod.ap.rearrange("p h s -> p (h s)"),
                                               start=True, stop=True), reads=[prod, c.cstb], writes=[pst])
            evac(pnew.ap[:, m, :, :], pst.ap[:, 0:4 * NS].rearrange("p (h s) -> p h s", h=4), [pst], [pnew], func=AF.Exp, scale=0.125)
            for h in range(4):
                S.op("dve", lambda e, m=m, h=h: e.tensor_scalar(out=pnew.ap[:, m, h, :], in0=pnew.ap[:, m, h, :], scalar1=fmp.ap[:, 2 + h:3 + h], scalar2=None,
                                                                op0=ALU.mult), reads=[pnew, fmp], writes=[pnew])
        OA = A([128, 4, NS], F32, "OAd")
        ptb = [A([128, NPG], I32, f"ptb{i}") for i in range(2)]
        idx = [A([128, NPG], I32, f"idx{i}") for i in range(2)]
        Kp = [A([128, 512], F32, f"Kp{i}") for i in range(2)]
        Vp = [A([128, 512], F32, f"Vp{i}") for i in range(2)]
        qrow = [A([128, 512], F32, f"qrow{i}") for i in range(2)]
        dq = [A([128, 128], F32, f"dq{i}") for i in range(2)]
        prd = [A([128, 512], F32, f"prd{i}") for i in range(2)]
        s8 = [A([128, 8], F32, f"s8{i}") for i in range(2)]
        fin = [A([128, 40], F32, f"fin{i}") for i in range(2)]
        zero32 = A([128, 128], F32, "zero32")
        S.op("pool", lambda e: e.memset(zero32.ap, 0.0), writes=[zero32])
        npg = 0
        for s_ in range(NS):
            pt, ix, qr = ptb[s_ % 2], idx[s_ % 2], qrow[s_ % 2]
            S.dma("sp", pt.ap, c.page_table[s_:s_ + 1, :].partition_broadcast(128), writes=[pt])
            S.op("dve", lambda e: e.tensor_scalar(out=ix.ap, in0=pt.ap, scalar1=128.0, scalar2=pcol, op0=ALU.mult, op1=ALU.add),
                 reads=[pt, c.cstb], writes=[ix])
            if ei > 0:
                S.op("dve", lambda e: e.tensor_scalar(out=ix.ap, in0=ix.ap, scalar1=float(ei * c.NPOOL * 128), scalar2=None, op0=ALU.add),
                     reads=[ix], writes=[ix])
            pq_ = c.ps()
            for h in range(4):
                d_ = dq[h % 2]
                S.op("dve", lambda e, h=h: e.tensor_scalar(out=d_.ap, in0=c.ident, scalar1=PQ.ap[:, h, s_:s_ + 1], scalar2=None, op0=ALU.mult),
                     reads=[PQ, c.cstb], writes=[d_])
                S.op("pe", lambda e, h=h: e.matmul(pq_.ap[:, h * 128:(h + 1) * 128], lhsT=c.ones, rhs=d_.ap, start=True, stop=True),
                     reads=[d_, c.cstb], writes=[pq_])
            evac(qr.ap, pq_.ap, [pq_], [qr])
            po_ = c.ps()
            S.op("pe", lambda e: e.matmul(po_.ap[:, 0:16], lhsT=zero32.ap, rhs=zero32.ap[:, 0:16], start=True, stop=True, skip_group_check=True),
                 reads=[zero32], writes=[po_])
            for j in range(NPG):
                kp, vp, pr_, s8_ = Kp[npg % 2], Vp[npg % 2], prd[npg % 2], s8[npg % 2]
                npg += 1
                S.idma(kp.ap, c.cache_k.rearrange("e r d -> (e r) d"), ix.ap[:, j:j + 1], reads=[ix], writes=[kp])
                S.idma(vp.ap, c.cache_v.rearrange("e r d -> (e r) d"), ix.ap[:, j:j + 1], reads=[ix], writes=[vp])
                S.op("dve", lambda e: e.tensor_tensor(out=pr_.ap, in0=kp.ap, in1=qr.ap, op=ALU.mult), reads=[kp, qr], writes=[pr_])
                S.op("dve", lambda e: e.tensor_reduce(out=s8_.ap, in_=pr_.ap.rearrange("p (g d) -> p g d", d=64), axis=AX.X, op=ALU.add),
                     reads=[pr_], writes=[s8_])
                evac(s8_.ap, s8_.ap, [s8_], [s8_], func=AF.Exp, scale=0.125)
                S.op("dve", lambda e, j=j: e.tensor_tensor(out=s8_.ap, in0=s8_.ap, in1=eb.ap[:, 1 if j == NPG - 1 else 0, :], op=ALU.mult),
                     reads=[s8_, eb], writes=[s8_])
                for h in range(4):
                    S.op("pe", lambda e, h=h: e.matmul(po_.ap[:, 2 * h:2 * h + 2], lhsT=vp.ap[:, h * 128:(h + 1) * 128], rhs=s8_.ap[:, 2 * h:2 * h + 2],
                                                       start=False, stop=False, skip_group_check=True), reads=[vp, s8_], writes=[po_], sig=False)
                S.op("pe", lambda e: e.matmul(po_.ap[:, 8:16], lhsT=c.ones, rhs=s8_.ap, start=False, stop=(j == NPG - 1), skip_group_check=True),
                     reads=[s8_, c.cstb], writes=[po_])
            f = fin[s_ % 2]
            fo = f.ap[:, 0:8].rearrange("p (h m) -> p h m", m=2)
            fl = f.ap[:, 8:16].rearrange("p (h m) -> p h m", m=2)
            S.op("dve", lambda e: e.tensor_copy(out=f.ap[:, 0:16], in_=po_.ap[:, 0:16]), reads=[po_], writes=[f])
            for m in range(2):
                S.op("dve", lambda e, m=m: e.tensor_tensor(out=f.ap[:, 16 + 4 * m:20 + 4 * m], in0=pnew.ap[:, m, :, s_], in1=PVZ.ap[:, 0:4, s_], op=ALU.mult),
                     reads=[pnew, PVZ], writes=[f])
                S.op("dve", lambda e, m=m: e.tensor_tensor(out=fo[:, :, m], in0=fo[:, :, m], in1=f.ap[:, 16 + 4 * m:20 + 4 * m], op=ALU.add), reads=[f], writes=[f])
                S.op("dve", lambda e, m=m: e.tensor_tensor(out=fl[:, :, m], in0=fl[:, :, m], in1=pnew.ap[:, m, :, s_], op=ALU.add), reads=[f, pnew], writes=[f])
            S.op("dve", lambda e: e.reciprocal(out=f.ap[:, 8:16], in_=f.ap[:, 8:16]), reads=[f], writes=[f])
            S.op("dve", lambda e: e.tensor_tensor(out=f.ap[:, 0:8], in0=f.ap[:, 0:8], in1=f.ap[:, 8:16], op=ALU.mult), reads=[f], writes=[f])
            S.op("dve", lambda e: e.scalar_tensor_tensor(out=OA.ap[:, :, s_], in0=fo[:, :, 1], scalar=c.nlam.ap[:, 0:1], in1=fo[:, :, 0],
                                                          op0=ALU.mult, op1=ALU.add), reads=[f, c.nlam], writes=[OA])
        sq2 = A([128, 4 * NS], F32, "sq2")
        OAf = OA.ap.rearrange("p h s -> p (h s)")
        S.op("dve", lambda e: e.tensor_tensor(out=sq2.ap, in0=OAf, in1=OAf, op=ALU.mult), reads=[OA], writes=[sq2])
        pst = colsum_bc(None, sq2, 4 * NS)
        S.op("dve", lambda e: e.tensor_scalar(out=sq2.ap, in0=pst.ap[:, 0:4 * NS], scalar1=1.0 / 128, scalar2=EPS, op0=ALU.mult, op1=ALU.add), reads=[pst], writes=[sq2])
        _rsq(B, sq2, sq2.ap)
        sza = A([128, 8, NS], F32, "sza")
        evac(sza.ap, PVZ.ap[:, 4:12, :], [PVZ], [sza], func=AF.Silu)
        S.op("dve", lambda e: e.scalar_tensor_tensor(out=OAf, in0=OAf, scalar=fmp.ap[:, 0:1], in1=sq2.ap, op0=ALU.mult, op1=ALU.mult), reads=[OA, fmp, sq2], writes=[OA])
        S.op("dve", lambda e: e.tensor_tensor(out=uT.ap[:, 0:4, :], in0=OA.ap, in1=sza.ap[:, 0:4, :], op=ALU.mult), reads=[OA, sza], writes=[uT])
        cst3 = A([128, 36, NS], F32, "cst3")
        tr_in(cst3, c.state_gdn_conv[ei].rearrange("s i d -> s (i d)"), 36, "gc")
        cbd = A([128, 12, NS], F32, "cbd")
        for j in range(12):
            S.op("dve", lambda e, j=j: e.tensor_scalar(out=cbd.ap[:, j, :], in0=PG.ap[:, j, :], scalar1=c.gcw.ap[:, j, 3:4], scalar2=None, op0=ALU.mult),
                 reads=[PG, c.gcw], writes=[cbd])
            for i in range(3):
                S.op("dve", lambda e, j=j, i=i: e.scalar_tensor_tensor(out=cbd.ap[:, j, :], in0=cst3.ap[:, i * 12 + j, :], scalar=c.gcw.ap[:, j, i:i + 1],
                                                                       in1=cbd.ap[:, j, :], op0=ALU.mult, op1=ALU.add), reads=[cst3, c.gcw, cbd], writes=[cbd])
        for i in range(2):
            S.dma("pool", c.gdn_conv_sample[ei, :, i, :], c.state_gdn_conv[ei, :, i + 1, :])
        tr_out(PG.ap, [PG], 12, c.gdn_conv_sample[ei, :, 2, :], "gcs")
        evac(cbd.ap, cbd.ap, [cbd], [cbd], func=AF.Silu)
        sq3 = A([128, 8 * NS], F32, "sq3")
        cb8 = cbd.ap[:, 0:8, :].rearrange("p b s -> p (b s)")
        S.op("dve", lambda e: e.tensor_tensor(out=sq3.ap, in0=cb8, in1=cb8, op=ALU.mult), reads=[cbd], writes=[sq3])
        pst = colsum_bc(None, sq3, 8 * NS)
        S.op("dve", lambda e: e.tensor_scalar(out=sq3.ap, in0=pst.ap[:, 0:8 * NS], scalar1=1.0, scalar2=EPS, op0=ALU.mult, op1=ALU.add), reads=[pst], writes=[sq3])
        _rsq(B, sq3, sq3.ap)
        S.op("dve", lambda e: e.tensor_tensor(out=cb8, in0=cb8, in1=sq3.ap, op=ALU.mult), reads=[cbd, sq3], writes=[cbd])
        S.op("dve", lambda e: e.tensor_scalar(out=cbd.ap[:, 0:4, :], in0=cbd.ap[:, 0:4, :], scalar1=128 ** -0.5, scalar2=None, op0=ALU.mult), reads=[cbd], writes=[cbd])
        qd, kd, vd = cbd.ap[:, 0:4, :], cbd.ap[:, 4:8, :], cbd.ap[:, 8:12, :]
        KQ = A([128, 4, NS, 2], F32, "KQd")
        S.op("dve", lambda e: e.tensor_copy(out=KQ.ap[:, :, :, 0], in_=kd), reads=[cbd], writes=[KQ])
        S.op("dve", lambda e: e.tensor_copy(out=KQ.ap[:, :, :, 1], in_=qd), reads=[cbd], writes=[KQ])
        Sb = [A([128, 4, 128], F32, f"Sb{i}") for i in range(2)]
        psk = c.ps()
        SKQ = A([128, 4, NS, 2], F32, "SKQ")
        for s_ in range(NS):
            sb_ = Sb[s_ % 2]
            S.dma("sp", sb_.ap, c.state_gdn[ei, s_].rearrange("h k v -> k h v"), writes=[sb_])
            for h in range(4):
                col = (h * NS + s_) * 2
                S.op("pe", lambda e, h=h, col=col: e.matmul(psk.ap[:, col:col + 2], lhsT=sb_.ap[:, h, :], rhs=KQ.ap[:, h, s_, :], start=True, stop=True),
                     reads=[sb_, KQ], writes=[psk])
        evac(SKQ.ap.rearrange("p h s t -> p (h s t)"), psk.ap[:, 0:8 * NS], [psk], [SKQ])
        EG, BET = EGB.ap[:, 0, :, :], EGB.ap[:, 1, :, :]
        vn = A([128, 4, NS], F32, "vnd")
        od = A([128, 4, NS], F32, "odd")
        S.op("dve", lambda e: e.tensor_tensor(out=vn.ap, in0=SKQ.ap[:, :, :, 0], in1=EG, op=ALU.mult), reads=[SKQ, EGB], writes=[vn])
        S.op("dve", lambda e: e.tensor_tensor(out=vn.ap, in0=vd, in1=vn.ap, op=ALU.subtract), reads=[cbd, vn], writes=[vn])
        S.op("dve", lambda e: e.tensor_tensor(out=vn.ap, in0=vn.ap, in1=BET, op=ALU.mult), reads=[vn, EGB], writes=[vn])
        qk = A([128, 4 * NS], F32, "qkd")
        S.op("dve", lambda e: e.tensor_tensor(out=qk.ap.rearrange("p (h s) -> p h s", h=4), in0=qd, in1=kd, op=ALU.mult), reads=[cbd], writes=[qk])
        pst = colsum_bc(None, qk, 4 * NS)
        S.op("dve", lambda e: e.tensor_tensor(out=od.ap, in0=pst.ap[:, 0:4 * NS].rearrange("p (h s) -> p h s", h=4), in1=vn.ap, op=ALU.mult), reads=[pst, vn], writes=[od])
        S.op("dve", lambda e: e.tensor_tensor(out=qk.ap.rearrange("p (h s) -> p h s", h=4), in0=SKQ.ap[:, :, :, 1], in1=EG, op=ALU.mult), reads=[SKQ, EGB], writes=[qk])
        S.op("dve", lambda e: e.tensor_tensor(out=od.ap, in0=od.ap, in1=qk.ap.rearrange("p (h s) -> p h s", h=4), op=ALU.add), reads=[od, qk], writes=[od])
        dg = [A([128, 128], F32, f"dgd{i}") for i in range(2)]
        tmpS = [A([128, 128], F32, f"tmpS{i}") for i in range(2)]
        So = [A([128, 4, 128], F32, f"So{i}") for i in range(2)]
        n_ = 0
        for s_ in range(NS):
            sb_ = Sb[s_ % 2]
            so = So[s_ % 2]
            S.dma("sp", sb_.ap, c.state_gdn[ei, s_].rearrange("h k v -> k h v"), writes=[sb_])
            for h in range(4):
                d_, t_ = dg[n_ % 2], tmpS[n_ % 2]
                n_ += 1
                S.op("dve", lambda e, h=h: e.tensor_scalar(out=d_.ap, in0=c.ident, scalar1=vn.ap[:, h, s_:s_ + 1], scalar2=None, op0=ALU.mult),
                     reads=[vn, c.cstb], writes=[d_])
                pvb = c.ps()
                S.op("pe", lambda e: e.matmul(pvb.ap[:, 0:128], lhsT=c.ones, rhs=d_.ap, start=True, stop=True), reads=[d_, c.cstb], writes=[pvb])
                S.op("dve", lambda e, h=h: e.tensor_scalar(out=t_.ap, in0=sb_.ap[:, h, :], scalar1=EGB.ap[:, 0, h, s_:s_ + 1], scalar2=None, op0=ALU.mult),
                     reads=[sb_, EGB], writes=[t_])
                S.op("dve", lambda e, h=h: e.scalar_tensor_tensor(out=so.ap[:, h, :], in0=pvb.ap[:, 0:128], scalar=cbd.ap[:, 4 + h, s_:s_ + 1], in1=t_.ap,
                                                                  op0=ALU.mult, op1=ALU.add), reads=[pvb, cbd, t_], writes=[so])
            S.dma("pool", c.gdn_sample[ei, s_].rearrange("h k v -> k h v"), so.ap, reads=[so])
        odf = od.ap.rearrange("p h s -> p (h s)")
        S.op("dve", lambda e: e.tensor_tensor(out=qk.ap, in0=odf, in1=odf, op=ALU.mult), reads=[od], writes=[qk])
        pst = colsum_bc(None, qk, 4 * NS)
        S.op("dve", lambda e: e.tensor_scalar(out=qk.ap, in0=pst.ap[:, 0:4 * NS], scalar1=1.0 / 128, scalar2=EPS, op0=ALU.mult, op1=ALU.add), reads=[pst], writes=[qk])
        _rsq(B, qk, qk.ap)
        S.op("dve", lambda e: e.scalar_tensor_tensor(out=odf, in0=odf, scalar=fmp.ap[:, 1:2], in1=qk.ap, op0=ALU.mult, op1=ALU.mult), reads=[od, fmp, qk], writes=[od])
        S.op("dve", lambda e: e.tensor_tensor(out=uT.ap[:, 4:8, :], in0=od.ap, in1=sza.ap[:, 4:8, :], op=ALU.mult), reads=[od, sza], writes=[uT])
        outproj(uT)
        ei += 1
    S.dma("pool", c.y_sample, xs.ap, reads=[xs])


def build(cfg):
    B = Builder(cfg)
    B.declare()
    B.setup()
    if B.do_prompt:
        prompt_alloc(B)
        if B.NE > 0:
            even_alloc(B)
        ei = oi = 0
        for li, ch in enumerate(B.layers):
            if ch == "e":
                prompt_even(B, li, ei)
                ei += 1
            else:
                prompt_odd(B, li, oi)
                oi += 1
    if B.do_decode:
        if not B.do_prompt:
            _prompt_common(B)
            B.scw = B.sb([128, 8, 3], F32, "scw")
            if B.NE > 0:
                even_alloc(B)
        decode_all(B)
    if cfg.get("dbg"):
        B.S.barrier()
        B.S.dma("sp", B.dbg_u, B.u_s)
    B.S.finish()
    return B


def core_inputs(inp, core, cfg, cst, oh, batch=None):
    TP, NS = cfg["TP"], cfg["NS"]
    layers = cfg["layers"]
    NE = max(sum(1 for ch in layers if ch == "e"), 1)
    NO = max(sum(1 for ch in layers if ch == "o"), 1)
    b = core if batch is None else batch
    f = np.ascontiguousarray
    m = {}
    m["x_prompt"] = f(inp["x_prompt"][b].reshape(TP, D))
    m["x_sample"] = f(inp["x_sample"][core * NS:(core + 1) * NS, 0])
    npool = inp["cache_k"].shape[1]
    m["cache_k"] = inp["cache_k"].reshape(-1, npool * 128, 512)[:NE]
    m["cache_v"] = inp["cache_v"].reshape(-1, npool * 128, 512)[:NE]
    m["page_table"] = f(inp["page_table"][core * NS:(core + 1) * NS])
    m["state_gdn"] = f(inp["state_gdn"][:NE, core * NS:(core + 1) * NS])
    m["state_gdn_conv"] = f(inp["state_gdn_conv"][:NE, core * NS:(core + 1) * NS])
    m["state_shortconv"] = f(inp["state_shortconv"][:NO, core * NS:(core + 1) * NS])
    for n in ("norm_w", "rel_table", "w_in_even", "w_out_even", "qn_w", "kn_w", "lam_q1", "lam_k1", "lam_q2", "lam_k2",
              "subln_w", "gdn_conv_w", "gdn_a_log", "gdn_dt_bias", "gdn_norm_w", "w_in_odd", "sc_conv_w", "w_out_odd"):
        m[n] = inp[n]
    m["norm_w"] = inp["norm_w"][:len(layers)]
    for n in ("w_in_even", "w_out_even", "qn_w", "kn_w", "lam_q1", "lam_k1", "lam_q2", "lam_k2", "subln_w", "gdn_conv_w",
              "gdn_a_log", "gdn_dt_bias", "gdn_norm_w"):
        m[n] = inp[n][:NE]
    for n in ("w_in_odd", "sc_conv_w", "w_out_odd"):
        m[n] = inp[n][:NO]
    m["cst"] = cst
    m["onehot"] = oh
    return m


NCORES = 4


def kernel(**inputs):
    inp = {k: np.asarray(v) for k, v in inputs.items()}
    nb, tp = inp["x_prompt"].shape[0], inp["x_prompt"].shape[1]
    nsamp = inp["x_sample"].shape[0]
    assert nb == NCORES and nsamp % NCORES == 0
    NS = nsamp // NCORES
    cfg = dict(TP=tp, NS=NS, NPOOL=inp["cache_k"].shape[1], NPAGES=inp["page_table"].shape[1], layers="eoeo")
    B = build(cfg)
    cst, oh = make_consts()
    in_maps = [core_inputs(inp, c, cfg, cst, oh) for c in range(NCORES)]
    res = run_bass_kernel_spmd(B.nc, in_maps, core_ids=list(range(NCORES)))
    r = res.results
    NE, NO = 2, 2
    st = lambda n, ax: np.stack([np.asarray(r[c][n]) for c in range(NCORES)], axis=ax)
    ct = lambda n, ax: np.concatenate([np.asarray(r[c][n]) for c in range(NCORES)], axis=ax)
    y_prompt = st("y_prompt", 0).reshape(nb, tp, D)
    y_sample = ct("y_sample", 0).reshape(nsamp, 1, D)
    k_prompt = st("k_prompt", 1).reshape(NE, nb, tp, H_A, DA)
    v_prompt = st("v_prompt", 1).reshape(NE, nb, tp, H_A, DA)
    gdn_prompt = st("gdn_prompt", 1).reshape(NE, nb, H_B, 128, 128)
    gdn_conv_prompt = st("gdn_conv_prompt", 1).reshape(NE, nb, 3, QKV_B)
    sc_prompt = st("sc_prompt", 1).reshape(NO, nb, 2, D)
    k_sample = ct("k_sample", 1).reshape(NE, nsamp, 1, H_A, DA)
    v_sample = ct("v_sample", 1).reshape(NE, nsamp, 1, H_A, DA)
    gdn_sample = ct("gdn_sample", 1).reshape(NE, nsamp, H_B, 128, 128)
    gdn_conv_sample = ct("gdn_conv_sample", 1).reshape(NE, nsamp, 3, QKV_B)
    sc_sample = ct("sc_sample", 1).reshape(NO, nsamp, 2, D)
    return tuple(np.ascontiguousarray(a, dtype=np.float32) for a in (
        y_prompt, y_sample, k_prompt, v_prompt, gdn_prompt, gdn_conv_prompt, sc_prompt,
        k_sample, v_sample, gdn_sample, gdn_conv_sample, sc_sample))
```

```python
import math
import numpy as np
import concourse.bass as bass
import concourse.mybir as mybir
from concourse.bass_utils import run_bass_kernel_spmd

F32, BF16, I32 = mybir.dt.float32, mybir.dt.bfloat16, mybir.dt.int32
AF = mybir.ActivationFunctionType
ALU = mybir.AluOpType
AX = mybir.AxisListType

D = 1024
H_A = 4
DA = 128
H_B = 4
QKV_B = 1536
P_EVEN = 4104
P_ODD = 4096
EPS = 1e-6
NEG = -30000.0
NUM_BUCKETS = 32


def t5_bucket_np(n):
    n = np.maximum(n, 0)
    nf = np.maximum(n, 1).astype(np.float32)
    large = 16 + (np.log(nf / np.float32(16)) / np.float32(math.log(8.0)) * 16).astype(np.int32)
    large = np.minimum(large, 31)
    return np.where(n < 16, n, large)


class Buf:
    __slots__ = ("ap", "w", "r", "name", "psum")

    def __init__(self, ap, name="", psum=False):
        self.ap = ap
        self.w = None
        self.r = {}
        self.name = name
        self.psum = psum

    def __getitem__(self, k):
        return self.ap[k]


class Sched:
    LIMIT = 30000
    NDS = 20

    def __init__(self, nc):
        self.nc = nc
        self.E = {"pe": nc.tensor, "dve": nc.vector, "act": nc.scalar, "pool": nc.gpsimd, "sp": nc.sync}
        self.sem = {}
        self.cnt = {}
        self.nsem = 0
        self.waited = {e: {} for e in self.E}
        self.pending = {e: False for e in self.E}
        for e in self.E:
            self._newsem(e)
        self.dsems = {}
        self.dnext = {}
        for q in ("sp", "pool", "act"):
            self.dsems[q] = [[self._alloc(f"d_{q}_{i}"), 0] for i in range(self.NDS)]
            self.dnext[q] = 0
        self.dbufs = {}
        self.nins = 0

    def _alloc(self, name):
        self.nsem += 1
        return (self.nc.alloc_semaphore(name), name)

    def _newsem(self, e):
        self.sem[e] = self._alloc(f"c_{e}_{self.nsem}")
        self.cnt[e] = 0

    def _wait(self, e, tok):
        sem, val = tok
        if e == "pe" and sem[1] == self.sem["pe"][1]:
            return
        w = self.waited[e]
        if w.get(sem[1], 0) >= val:
            return
        self.E[e].wait_ge(sem[0], val)
        self.nins += 1
        w[sem[1]] = val

    def _deps(self, e, reads, writes):
        for b in reads:
            if b.w is not None:
                self._wait(e, b.w)
            if b.psum:
                for s, tok in b.r.items():
                    self._wait(e, tok)
        for b in writes:
            if b.w is not None:
                self._wait(e, b.w)
            for s, tok in b.r.items():
                self._wait(e, tok)

    def _mark(self, tok, reads, writes):
        for b in reads:
            b.r[tok[0][1]] = tok
        for b in writes:
            b.w = tok
            b.r = {}

    def op(self, e, fn, reads=(), writes=(), sig=True):
        self._deps(e, reads, writes)
        ins = fn(self.E[e])
        self.nins += 1
        if sig:
            if self.cnt[e] >= self.LIMIT:
                self._newsem(e)
            self.cnt[e] += 1
            ins.then_inc(self.sem[e][0], 1)
            tok = (self.sem[e], self.cnt[e])
            self.pending[e] = False
        else:
            assert self.cnt[e] < self.LIMIT - 1
            tok = (self.sem[e], self.cnt[e] + 1)
            self.pending[e] = True
        self._mark(tok, reads, writes)
        return ins

    def dma(self, q, out, in_, reads=(), writes=(), **kw):
        self._deps(q, reads, writes)
        k = self.dnext[q]
        self.dnext[q] = (k + 1) % self.NDS
        ent = self.dsems[q][k]
        if ent[1] > 0:
            self._wait(q, (ent[0], ent[1]))
        if ent[1] + 16 > 60000:
            ent[0] = self._alloc(f"d_{q}_{k}_{self.nsem}")
            ent[1] = 0
        ins = self.E[q].dma_start(out=out, in_=in_, **kw)
        self.nins += 1
        ent[1] += 16
        ins.then_inc(ent[0][0], 16)
        tok = (ent[0], ent[1])
        self._mark(tok, reads, writes)
        return ins

    def idma(self, out, in_, idx_ap, reads=(), writes=()):
        q = "pool"
        self._deps(q, reads, writes)
        k = self.dnext[q]
        self.dnext[q] = (k + 1) % self.NDS
        ent = self.dsems[q][k]
        if ent[1] > 0:
            self._wait(q, (ent[0], ent[1]))
        ins = self.E[q].indirect_dma_start(out=out, out_offset=None, in_=in_,
                                           in_offset=bass.IndirectOffsetOnAxis(ap=idx_ap, axis=0))
        self.nins += 1
        ent[1] += 16
        ins.then_inc(ent[0][0], 16)
        self._mark((ent[0], ent[1]), reads, writes)
        return ins

    def barrier(self):
        toks = []
        for q in self.dsems:
            for ent in self.dsems[q]:
                if ent[1] > 0:
                    toks.append((ent[0], ent[1]))
        for f in self.E:
            assert not self.pending[f]
            if self.cnt[f] > 0:
                toks.append((self.sem[f], self.cnt[f]))
        for e in self.E:
            for tok in toks:
                self._wait(e, tok)

    def db(self, name, key=0):
        k = (name, key)
        if k not in self.dbufs:
            self.dbufs[k] = Buf(None, f"{name}:{key}")
        return self.dbufs[k]

    def finish(self):
        for q in self.dsems:
            for ent in self.dsems[q]:
                if ent[1] > 0:
                    self._wait("sp", (ent[0], ent[1]))
        for e in self.E:
            assert not self.pending[e], e
            if self.cnt[e] > 0:
                self._wait("sp", (self.sem[e], self.cnt[e]))


class Builder:
    def __init__(self, cfg):
        self.cfg = cfg
        self.TP = cfg["TP"]
        self.NS = cfg["NS"]
        self.NPOOL = cfg["NPOOL"]
        self.NPAGES = cfg["NPAGES"]
        self.layers = cfg["layers"]
        self.NE = sum(1 for c in self.layers if c == "e")
        self.NO = sum(1 for c in self.layers if c == "o")
        self.do_prompt = cfg.get("prompt", True)
        self.do_decode = cfg.get("decode", True)
        self.nc = bass.Bass("TRN2", target_bir_lowering=False)
        self.S = Sched(self.nc)
        self.nsb = 0
        self.psn = 0

    def sb(self, shape, dt=F32, name=None):
        self.nsb += 1
        h = self.nc.alloc_sbuf_tensor(name or f"sb{self.nsb}", list(shape), dt)
        return Buf(h.ap(), name or f"sb{self.nsb}")

    def arena_reset(self):
        if not hasattr(self, "arena"):
            self.ARW = 22528
            self.arena = self.nc.alloc_sbuf_tensor("arena", [128, self.ARW], F32).ap()
        self.S.barrier()
        self.aoff = 0

    def ar(self, shape, dt=F32, name=None):
        n = int(np.prod(shape[1:]))
        words = n if dt in (F32, I32) else (n + 1) // 2
        words = (words + 7) // 8 * 8
        assert self.aoff + words <= self.ARW, (self.aoff, words, name)
        ap = self.arena[0:shape[0], self.aoff:self.aoff + words]
        self.aoff += words
        if dt != F32:
            ap = ap.bitcast(dt)
        ap = ap[:, 0:n]
        if len(shape) == 3:
            ap = ap.rearrange("p (a b) -> p a b", a=shape[1])
        elif len(shape) == 4:
            ap = ap.rearrange("p (a b c) -> p a b c", a=shape[1], b=shape[2])
        self.nsb += 1
        return Buf(ap, name or f"ar{self.nsb}")

    def dram(self, name, shape, dt=F32, kind="Internal"):
        return self.nc.dram_tensor(name, list(shape), dt, kind=kind).ap()

    def ps(self):
        b = self.psb[self.psn % 8]
        self.psn += 1
        return b

    def declare(self):
        c = self
        TP, NS, NE, NO = self.TP, self.NS, max(self.NE, 1), self.NO
        i = lambda n, s, dt=F32: c.dram(n, s, dt, "ExternalInput")
        o = lambda n, s, dt=F32: c.dram(n, s, dt, "ExternalOutput")
        c.x_prompt = i("x_prompt", [TP, D])
        c.x_sample = i("x_sample", [NS, D])
        c.cache_k = i("cache_k", [NE, self.NPOOL * 128, 512])
        c.cache_v = i("cache_v", [NE, self.NPOOL * 128, 512])
        c.page_table = i("page_table", [NS, self.NPAGES], I32)
        c.state_gdn = i("state_gdn", [NE, NS, H_B, 128, 128])
        c.state_gdn_conv = i("state_gdn_conv", [NE, NS, 3, QKV_B])
        c.state_shortconv = i("state_shortconv", [max(NO, 1), NS, 2, D])
        c.norm_w = i("norm_w", [len(self.layers), D])
        c.rel_table = i("rel_table", [32, 4])
        c.w_in_even = i("w_in_even", [NE, D, P_EVEN])
        c.w_out_even = i("w_out_even", [NE, D, D])
        for n in ("qn_w", "kn_w", "lam_q1", "lam_k1", "lam_q2", "lam_k2"):
            setattr(c, n, i(n, [NE, 64]))
        c.subln_w = i("subln_w", [NE, 128])
        c.gdn_conv_w = i("gdn_conv_w", [NE, 4, QKV_B])
        c.gdn_a_log = i("gdn_a_log", [NE, 4])
        c.gdn_dt_bias = i("gdn_dt_bias", [NE, 4])
        c.gdn_norm_w = i("gdn_norm_w", [NE, 128])
        c.w_in_odd = i("w_in_odd", [max(NO, 1), D, P_ODD])
        c.sc_conv_w = i("sc_conv_w", [max(NO, 1), 3, D])
        c.w_out_odd = i("w_out_odd", [max(NO, 1), D, D])
        c.cst = i("cst", [128, CST_W])
        c.oh = i("onehot", [33, OH_W])
        c.y_prompt = o("y_prompt", [TP, D])
        c.y_sample = o("y_sample", [NS, D])
        c.k_prompt = o("k_prompt", [NE, TP, 512])
        c.v_prompt = o("v_prompt", [NE, TP, 512])
        c.gdn_prompt = o("gdn_prompt", [NE, H_B, 128, 128])
        c.gdn_conv_prompt = o("gdn_conv_prompt", [NE, 3, QKV_B])
        c.sc_prompt = o("sc_prompt", [max(NO, 1), 2, D])
        c.k_sample = o("k_sample", [NE, NS, 512])
        c.v_sample = o("v_sample", [NE, NS, 512])
        c.gdn_sample = o("gdn_sample", [NE, NS, H_B, 128, 128])
        c.gdn_conv_sample = o("gdn_conv_sample", [NE, NS, 3, QKV_B])
        c.sc_sample = o("sc_sample", [max(NO, 1), NS, 2, D])
        c.xres = c.dram("xres", [TP, D])
        c.qT_s = c.dram("qT_s", [4, 128, TP], BF16)
        c.kT_s = c.dram("kT_s", [4, 128, TP], BF16)
        c.v_s = c.dram("v_s", [TP, 512], BF16)
        c.za_s = c.dram("za_s", [TP, 512], BF16)
        c.zb_s = c.dram("zb_s", [TP, 512], BF16)
        c.gb_s = c.dram("gb_s", [TP, 8])
        c.g_s = c.dram("g_s", [12, 128, TP])
        c.u_s = c.dram("u_s", [TP, D], BF16)
        c.bias_s = c.dram("bias_s", [4, 2 * 128 * 128])
        if c.cfg.get("dbg"):
            c.dbg_u = c.dram("dbg_u", [TP, D], BF16, "ExternalOutput")

    def setup(self):
        c, S = self, self.S
        c.psb = []
        for k in range(8):
            h = self.nc.alloc_psum_tensor(f"psb{k}", [128, 512], F32)
            c.psb.append(Buf(h.ap(), f"psb{k}", psum=True))
        c.cstb = c.sb([128, CST_W], F32, "cstb")
        S.dma("sp", c.cstb.ap, c.cst, writes=[c.cstb])
        c.ident = c.cstb.ap[:, 0:128]
        c.bones = c.cstb.ap[:, 128:256]
        c.ones = c.cstb.ap[:, 256:384]
        c.tri = c.cstb.ap[:, 384:512]
        c.negm_s = c.cstb.ap[:, 512:640]
        c.hones = c.cstb.ap[:, 640:896]
        c.cbf = c.sb([128, 384], BF16, "cbf")
        S.op("dve", lambda e: e.tensor_copy(out=c.cbf.ap, in_=c.cstb.ap[:, 0:384]), reads=[c.cstb], writes=[c.cbf])
        c.zeros_b = c.sb([128, 512], BF16, "zeros_b")
        S.op("pool", lambda e: e.memset(c.zeros_b.ap, 0.0), writes=[c.zeros_b])
        c.ident_b = c.cbf.ap[:, 0:128]
        c.bones_b = c.cbf.ap[:, 128:256]
        c.ones_b = c.cbf.ap[:, 256:384]

    def bcast_rows(self, dst_buf, src_ap, ncols):
        self.S.dma("sp", dst_buf.ap, src_ap.partition_broadcast(128), writes=[dst_buf])

    def load_w(self, dst, w_ap, ncol, stg):
        S = self.S
        CH = 1026
        i = 0
        for kc in range(8):
            for c0 in range(0, ncol, CH):
                cw = min(CH, ncol - c0)
                st = stg[i % 2]
                S.dma("sp", st.ap[:, 0:cw], w_ap[kc * 128:(kc + 1) * 128, c0:c0 + cw], writes=[st])
                eng = ("dve", "pool", "act")[i % 3]
                if eng == "act":
                    S.op("act", lambda e, st=st, kc=kc, c0=c0, cw=cw: e.activation(
                        out=dst.ap[:, kc, c0:c0 + cw], in_=st.ap[:, 0:cw], func=AF.Copy), reads=[st], writes=[dst])
                else:
                    S.op(eng, lambda e, st=st, kc=kc, c0=c0, cw=cw: e.tensor_copy(
                        out=dst.ap[:, kc, c0:c0 + cw], in_=st.ap[:, 0:cw]), reads=[st], writes=[dst])
                i += 1


CST_W = 1024 + 512 + 1
OH_W = 2 * 128 * 128 + 128


def make_consts():
    cst = np.zeros((128, CST_W), np.float32)
    cst[:, 0:128] = np.eye(128)
    cst[0:64, 128:192] = 1
    cst[64:128, 192:256] = 1
    cst[:, 256:384] = 1
    j = np.arange(128)[:, None]
    i = np.arange(128)[None, :]
    cst[:, 384:512] = (j <= i)
    cst[:, 512:640] = np.where(i > j, 0.0, NEG)
    cst[0:64, 640:768] = 1
    cst[64:128, 768:896] = 1
    cst[:, 896:1024] = np.where(j > i, 0.0, NEG)
    for hh in range(4):
        cst[hh, 1024 + hh * 128:1024 + (hh + 1) * 128] = 1
    cst[:, 1536] = np.arange(128)
    oh = np.zeros((33, 2, 128, 128), np.float32)
    k = np.arange(128)[:, None]
    q = np.arange(128)[None, :]
    for t, off in enumerate((0, 128)):
        n = q - k + off
        bk = t5_bucket_np(n)
        for b in range(32):
            oh[b, t] = (bk == b) & (n >= 0)
        oh[32, t] = (n < 0)
    ohd = np.zeros((33, 128), np.float32)
    bk = t5_bucket_np(128 - np.arange(128))
    for b in range(32):
        ohd[b] = (bk == b)
    return cst, np.concatenate([oh.reshape(33, 2 * 128 * 128), ohd], axis=1)


def _prompt_common(B):
    c, S = B, B.S
    c.ss = [c.sb([128, 1], F32, f"ss{i}") for i in range(4)]
    c.normw = c.sb([128, D], F32, "normw")
    c.wstg = [c.sb([128, 1026], F32, f"wstg{i}") for i in range(2)]
    c.win = c.sb([128, 8, 4224], BF16, "win")
    c.wout = c.sb([128, 8, D], BF16, "wout")
    c.mhalf = c.sb([128, 512], F32, "mhalf")
    S.op("pool", lambda e: e.memset(c.mhalf.ap, -0.5), writes=[c.mhalf])
    c.nss = 0


def _rstd(B, dst, src_ap, src_bufs, scale, bias, ncol):
    S = B.S
    S.op("dve", lambda e: e.tensor_scalar(out=dst.ap[:, 0:ncol], in0=src_ap, scalar1=scale, scalar2=bias,
                                          op0=ALU.mult, op1=ALU.add), reads=src_bufs, writes=[dst])
    _rsq(B, dst, dst.ap[:, 0:ncol])


def _rsq(B, buf, ap, extra=()):
    S = B.S
    S.op("act", lambda e: e.activation(out=ap, in_=ap, func=AF.Sqrt), reads=[buf] + list(extra), writes=[buf])
    S.op("dve", lambda e: e.reciprocal(out=ap, in_=ap), reads=[buf], writes=[buf])


def _norm_T(B, xg, li, hT):
    c, S = B, B.S
    for t in range(4):
        ss = c.ss[c.nss % 4]
        c.nss += 1
        S.op("act", lambda e: e.activation(out=c.junk.ap, in_=xg.ap[:, t, :], func=AF.Square, accum_out=ss.ap),
             reads=[xg], writes=[c.junk, ss])
        _rstd(B, ss, ss.ap, [ss], 1.0 / D, EPS, 1)
        hb = c.hb[t % 2]
        S.op("dve", lambda e: e.scalar_tensor_tensor(out=hb.ap, in0=xg.ap[:, t, :], scalar=ss.ap[:, 0:1],
                                                      in1=c.normw.ap, op0=ALU.mult, op1=ALU.mult),
             reads=[xg, ss, c.normw], writes=[hb])
        pst = c.ps()
        pv = pst.ap.bitcast(BF16).rearrange("p (k t) -> p k t", k=8)
        for kc in range(8):
            S.op("pe", lambda e, kc=kc: e.transpose(out=pv[:, kc, :], in_=hb.ap[:, kc * 128:(kc + 1) * 128],
                                                    identity=c.ident_b), reads=[hb, c.cbf], writes=[pst], sig=(kc == 7))
        S.op("act", lambda e: e.activation(out=hT.ap[:, :, t * 128:(t + 1) * 128], in_=pv, func=AF.Copy),
             reads=[pst], writes=[hT])


def _proj_fm(B, hT, col0, pst):
    c, S = B, B.S
    for kc in range(8):
        S.op("pe", lambda e, kc=kc: e.matmul(pst.ap, lhsT=c.win.ap[:, kc, col0:col0 + 128], rhs=hT.ap[:, kc, :],
                                             start=(kc == 0), stop=(kc == 7)),
             reads=[c.win, hT], writes=[pst], sig=(kc == 7))


def _out_proj_store(B, uT_ap, uT_bufs, xg, g, last):
    c, S = B, B.S
    for t in range(4):
        for half in range(2):
            pst = c.ps()
            for cb in range(8):
                S.op("pe", lambda e, cb=cb: e.matmul(pst.ap, lhsT=uT_ap(cb, t), rhs=c.wout.ap[:, cb, half * 512:(half + 1) * 512],
                                                     start=(cb == 0), stop=(cb == 7)),
                     reads=uT_bufs + [c.wout], writes=[pst], sig=(cb == 7))
            S.op("dve", lambda e: e.tensor_tensor(out=xg.ap[:, t, half * 512:(half + 1) * 512], in0=pst.ap,
                                                  in1=xg.ap[:, t, half * 512:(half + 1) * 512], op=ALU.add),
                 reads=[pst, xg], writes=[xg])
    dst = c.y_prompt if last else c.xres
    S.dma("pool", dst[g * 512:(g + 1) * 512, :].rearrange("(t p) d -> p t d", p=128), xg.ap,
          reads=[xg], writes=_xk(B, g))


def _xk(B, g):
    return [B.S.db("xres", 4 * g + i) for i in range(4)]


def _proj_bufs(B, li):
    c, S = B, B.S
    c.arena_reset()
    c.xg = [c.ar([128, 4, D], F32, "xg0")]
    c.hT = [c.ar([128, 8, 512], BF16, f"hT{i}") for i in range(2)]
    c.hb = [c.ar([128, D], BF16, f"hb{i}") for i in range(2)]
    c.junk = c.ar([128, D], BF16, "junk")
    S.dma("sp", c.normw.ap, c.norm_w[li:li + 1, :].partition_broadcast(128), writes=[c.normw])


def _load_xg(B, li, g):
    c, S = B, B.S
    xg = c.xg[g % len(c.xg)]
    src = c.x_prompt if li == 0 else c.xres
    S.dma("sp", xg.ap, src[g * 512:(g + 1) * 512, :].rearrange("(t p) d -> p t d", p=128),
          reads=_xk(B, g), writes=[xg])
    return xg


def prompt_odd(B, li, oi):
    c, S = B, B.S
    G = c.TP // 512
    last = (li == len(c.layers) - 1)
    c.load_w(c.win, c.w_in_odd[oi], P_ODD, c.wstg)
    c.load_w(c.wout, c.w_out_odd[oi], D, c.wstg)
    scw = c.scw
    for j in range(3):
        S.dma("sp", scw.ap[:, :, j], c.sc_conv_w[oi, j].rearrange("(c p) -> p c", p=128), writes=[scw],
              allow_slow_non_contiguous=True)
    _proj_bufs(B, li)
    c.CHw = [c.ar([128, 514], F32, f"CHw{i}") for i in range(2)]
    c.ocar = c.ar([128, 8, 2], F32, "ocar")
    c.otmp = [c.ar([128, 512], F32, f"otmp{i}") for i in range(3)]
    c.uT = [c.ar([128, 8, 512], BF16, f"uT{i}") for i in range(1)]
    S.op("pool", lambda e: e.memset(c.ocar.ap, 0.0), writes=[c.ocar])
    for g in range(G):
        xg = _load_xg(B, li, g)
        hT = c.hT[g % 2]
        _norm_T(B, xg, li, hT)
        uT = c.uT[0]
        for cb in range(8):
            p_b, p_c, p_h, p_z = c.ps(), c.ps(), c.ps(), c.ps()
            _proj_fm(B, hT, 0 * D + cb * 128, p_b)
            _proj_fm(B, hT, 1 * D + cb * 128, p_c)
            _proj_fm(B, hT, 2 * D + cb * 128, p_h)
            _proj_fm(B, hT, 3 * D + cb * 128, p_z)
            CH = c.CHw[cb % 2]
            t0, t1, t2 = c.otmp[0], c.otmp[1], c.otmp[2]
            S.op("pool", lambda e, cb=cb: e.tensor_copy(out=CH.ap[:, 0:2], in_=c.ocar.ap[:, cb, :]), reads=[c.ocar], writes=[CH])
            S.op("act", lambda e: e.activation(out=t0.ap, in_=p_h.ap, func=AF.Copy), reads=[p_h], writes=[t0])
            S.op("dve", lambda e: e.tensor_tensor(out=CH.ap[:, 2:514], in0=p_c.ap, in1=t0.ap, op=ALU.mult),
                 reads=[p_c, t0], writes=[CH])
            S.op("dve", lambda e: e.tensor_scalar(out=t1.ap, in0=CH.ap[:, 0:512], scalar1=scw.ap[:, cb, 0:1], scalar2=None,
                                                  op0=ALU.mult), reads=[CH, scw], writes=[t1])
            for j in (1, 2):
                S.op("dve", lambda e, j=j: e.scalar_tensor_tensor(out=t1.ap, in0=CH.ap[:, j:j + 512], scalar=scw.ap[:, cb, j:j + 1],
                                                                  in1=t1.ap, op0=ALU.mult, op1=ALU.add),
                     reads=[CH, scw, t1], writes=[t1])
            S.op("act", lambda e: e.activation(out=t2.ap, in_=p_z.ap, func=AF.Silu), reads=[p_z], writes=[t2])
            S.op("dve", lambda e: e.tensor_tensor(out=t1.ap, in0=p_b.ap, in1=t1.ap, op=ALU.mult), reads=[p_b, t1], writes=[t1])
            S.op("pool", lambda e, cb=cb: e.tensor_tensor(out=uT.ap[:, cb, :], in0=t1.ap, in1=t2.ap, op=ALU.mult),
                 reads=[t1, t2], writes=[uT])
            if g == G - 1:
                S.dma("pool", c.sc_prompt[oi, :, cb * 128:(cb + 1) * 128].rearrange("j p -> p j"), CH.ap[:, 512:514],
                      reads=[CH], allow_slow_non_contiguous=True)
            else:
                S.op("pool", lambda e, cb=cb: e.tensor_copy(out=c.ocar.ap[:, cb, :], in_=CH.ap[:, 512:514]), reads=[CH], writes=[c.ocar])
        _out_proj_store(B, lambda cb, t: uT.ap[:, cb, t * 128:(t + 1) * 128], [uT], xg, g, last)


def prompt_alloc(B):
    c = B
    _prompt_common(B)
    c.scw = c.sb([128, 8, 3], F32, "scw")


def _proj_tm(B, hT, t, col0, ncol, pst):
    c, S = B, B.S
    for kc in range(8):
        S.op("pe", lambda e, kc=kc: e.matmul(pst.ap[:, 0:ncol], lhsT=hT.ap[:, kc, t * 128:(t + 1) * 128],
                                             rhs=c.win.ap[:, kc, col0:col0 + ncol], start=(kc == 0), stop=(kc == 7)),
             reads=[hT, c.win], writes=[pst], sig=(kc == 7))


def even_alloc(B):
    c, S = B, B.S
    c.qnw8 = c.sb([128, 1], F32, "qnw8")
    c.knw8 = c.sb([128, 1], F32, "knw8")
    c.gcw = c.sb([128, 12, 4], F32, "gcw")
    c.dtb = c.sb([128, 4], F32, "dtb")
    c.negA = c.sb([128, 4], F32, "negA")
    c.crow = c.sb([128, 4], F32, "crow")
    c.ncrow = c.sb([128, 4], F32, "ncrow")
    c.lamv = c.sb([128, 4, 64], F32, "lamv")
    c.lams = c.sb([128, 4], F32, "lams")
    c.nlam = c.sb([128, 1], F32, "nlam")
    c.sw_row = c.sb([128, 128], F32, "sw_row")
    c.gnw_row = c.sb([128, 128], F32, "gnw_row")
    c.EB = c.sb([128, 4, 2, 128], F32, "EB")
    c.relx = c.sb([33, 4], F32, "relx")
    c.st = [c.sb([128, 4], F32, f"st{i}") for i in range(4)]
    c.nst = 0
    c.arena_reset()
    c.oht = [c.ar([33, 512], F32, f"oht{i}") for i in range(2)]
    c.b4 = [c.ar([4, 512], F32, f"b4{i}") for i in range(2)]
    S.op("pool", lambda e: e.memset(c.relx.ap[32:33, :], NEG), writes=[c.relx])
    S.dma("sp", c.relx.ap[0:32, :], c.rel_table, writes=[c.relx])
    S.dma("sp", c.crow.ap, c.rel_table[31:32, :].partition_broadcast(128), writes=[c.crow])
    S.op("dve", lambda e: e.tensor_scalar(out=c.ncrow.ap, in0=c.crow.ap, scalar1=-1.0, scalar2=None, op0=ALU.mult),
         reads=[c.crow], writes=[c.ncrow])
    for ch in range(2 * 128 * 128 // 512):
        oht = c.oht[ch % 2]
        S.dma("sp", oht.ap, c.oh[:, ch * 512:(ch + 1) * 512], writes=[oht])
        pst = c.ps()
        S.op("pe", lambda e: e.matmul(pst.ap[0:4, :], lhsT=c.relx.ap, rhs=oht.ap, start=True, stop=True),
             reads=[c.relx, oht], writes=[pst])
        b4 = c.b4[ch % 2]
        S.op("act", lambda e: e.activation(out=b4.ap, in_=pst.ap[0:4, :], func=AF.Copy), reads=[pst], writes=[b4])
        S.dma("pool", c.bias_s[:, ch * 512:(ch + 1) * 512], b4.ap, reads=[b4], writes=[S.db("bias_s")])
    for h in range(4):
        for t in range(2):
            S.dma("sp", c.EB.ap[:, h, t, :], c.bias_s[h, t * 16384:(t + 1) * 16384].rearrange("(k q) -> k q", q=128),
                  reads=[S.db("bias_s")], writes=[c.EB])
    for h in range(4):
        S.op("act", lambda e, h=h: e.activation(out=c.EB.ap[:, h, :, :], in_=c.EB.ap[:, h, :, :], func=AF.Exp,
                                                bias=c.ncrow.ap[:, h:h + 1]), reads=[c.EB, c.ncrow], writes=[c.EB])


def even_params(B, li, ei):
    c, S = B, B.S
    lambda_init = 0.8 - 0.6 * math.exp(-0.3 * li)
    for dst, src in ((c.qnw8, c.qn_w), (c.knw8, c.kn_w)):
        for hf in range(2):
            S.dma("sp", dst.ap[hf * 64:(hf + 1) * 64, :], src[ei].rearrange("(p o) -> p o", o=1), writes=[dst])
        S.op("dve", lambda e, dst=dst: e.tensor_scalar(out=dst.ap, in0=dst.ap, scalar1=8.0, scalar2=None, op0=ALU.mult),
             reads=[dst], writes=[dst])
    for i in range(4):
        S.dma("sp", c.gcw.ap[:, :, i], c.gdn_conv_w[ei, i].rearrange("(j p) -> p j", p=128), writes=[c.gcw],
              allow_slow_non_contiguous=True)
    S.dma("sp", c.dtb.ap, c.gdn_dt_bias[ei:ei + 1, :].partition_broadcast(128), writes=[c.dtb])
    S.dma("sp", c.negA.ap, c.gdn_a_log[ei:ei + 1, :].partition_broadcast(128), writes=[c.negA])
    S.op("act", lambda e: e.activation(out=c.negA.ap, in_=c.negA.ap, func=AF.Exp), reads=[c.negA], writes=[c.negA])
    S.op("dve", lambda e: e.tensor_scalar(out=c.negA.ap, in0=c.negA.ap, scalar1=-1.0, scalar2=None, op0=ALU.mult),
         reads=[c.negA], writes=[c.negA])
    for i, src in enumerate((c.lam_q1, c.lam_k1, c.lam_q2, c.lam_k2)):
        S.dma("sp", c.lamv.ap[:, i, :], src[ei:ei + 1, :].partition_broadcast(128), writes=[c.lamv])
    for i in range(2):
        S.op("dve", lambda e, i=i: e.tensor_tensor(out=c.lamv.ap[:, 2 * i, :], in0=c.lamv.ap[:, 2 * i, :],
                                                   in1=c.lamv.ap[:, 2 * i + 1, :], op=ALU.mult), reads=[c.lamv], writes=[c.lamv])
        S.op("dve", lambda e, i=i: e.tensor_reduce(out=c.lams.ap[:, i:i + 1], in_=c.lamv.ap[:, 2 * i, :], axis=AX.X, op=ALU.add),
             reads=[c.lamv], writes=[c.lams])
    S.op("act", lambda e: e.activation(out=c.lams.ap[:, 0:2], in_=c.lams.ap[:, 0:2], func=AF.Exp), reads=[c.lams], writes=[c.lams])
    S.op("dve", lambda e: e.tensor_tensor(out=c.lams.ap[:, 2:3], in0=c.lams.ap[:, 1:2], in1=c.lams.ap[:, 0:1], op=ALU.subtract),
         reads=[c.lams], writes=[c.lams])
    S.op("dve", lambda e: e.tensor_scalar(out=c.nlam.ap, in0=c.lams.ap[:, 2:3], scalar1=-lambda_init, scalar2=None, op0=ALU.add),
         reads=[c.lams], writes=[c.nlam])
    S.dma("sp", c.sw_row.ap, c.subln_w[ei:ei + 1, :].partition_broadcast(128), writes=[c.sw_row])
    S.op("dve", lambda e: e.tensor_scalar(out=c.sw_row.ap, in0=c.sw_row.ap, scalar1=1.0 - lambda_init, scalar2=None, op0=ALU.mult),
         reads=[c.sw_row], writes=[c.sw_row])
    S.dma("sp", c.gnw_row.ap, c.gdn_norm_w[ei:ei + 1, :].partition_broadcast(128), writes=[c.gnw_row])


def even_proj(B, li, ei):
    c, S = B, B.S
    G = c.TP // 512
    _proj_bufs(B, li)
    c.sqb = [c.ar([128, 512], BF16, f"sqb{i}") for i in range(2)]
    c.rsb = [c.ar([128, 512], F32, f"rsb{i}") for i in range(2)]
    c.qob = [c.ar([128, 512], BF16, f"qob{i}") for i in range(2)]
    c.kfb = [c.ar([128, 512], F32, f"kfb{i}") for i in range(2)]
    c.kob = [c.ar([128, 512], BF16, f"kob{i}") for i in range(2)]
    c.kout = [c.ar([128, 4, 128], F32, f"kout{i}") for i in range(2)]
    c.vf = [c.ar([128, 512], F32, f"vf{i}") for i in range(2)]
    c.vbf = [c.ar([128, 512], BF16, f"vbf{i}") for i in range(2)]
    c.zab = [c.ar([128, 512], BF16, f"zab{i}") for i in range(2)]
    c.zbb = [c.ar([128, 512], BF16, f"zbb{i}") for i in range(2)]
    c.gbt = [c.ar([128, 4, 8], F32, f"gbt{i}") for i in range(2)]
    c.CBw = [c.ar([128, 515], F32, f"CBw{i}") for i in range(2)]
    c.gcar = c.ar([128, 12, 3], F32, "gcar")
    c.gtmp = [c.ar([128, 512], F32, f"gtmp{i}") for i in range(4)]
    c.gout = [c.ar([128, 512], F32, f"gout{i}") for i in range(2)]
    S.op("pool", lambda e: e.memset(c.gcar.ap, 0.0), writes=[c.gcar])
    nb = 0
    for g in range(G):
        xg = _load_xg(B, li, g)
        hT = c.hT[g % 2]
        _norm_T(B, xg, li, hT)
        sec = c.cfg.get("sec", 127)
        for kind in (("q", "k") if sec & 1 else ()):
            for h in range(4):
                pq = c.ps()
                _proj_fm(B, hT, (0 if kind == "q" else 512) + h * 128, pq)
                sq = c.sqb[nb % 2]
                rs = c.rsb[nb % 2]
                nb += 1
                S.op("act", lambda e: e.activation(out=sq.ap, in_=pq.ap, func=AF.Square), reads=[pq], writes=[sq])
                p2 = c.ps()
                S.op("pe", lambda e: e.matmul(p2.ap, lhsT=c.bones_b, rhs=sq.ap, start=True, stop=True),
                     reads=[c.cbf, sq], writes=[p2])
                _rstd(B, rs, p2.ap, [p2], 1.0, 64 * EPS, 512)
                if kind == "q":
                    qo = c.qob[h % 2]
                    S.op("dve", lambda e: e.scalar_tensor_tensor(out=qo.ap, in0=pq.ap, scalar=c.qnw8.ap[:, 0:1], in1=rs.ap,
                                                                  op0=ALU.mult, op1=ALU.mult), reads=[pq, c.qnw8, rs], writes=[qo])
                    S.dma("pool", c.qT_s[h, :, g * 512:(g + 1) * 512], qo.ap, reads=[qo], writes=[S.db("qT_s", h)])
                else:
                    kf = c.kfb[h % 2]
                    ko = c.kob[h % 2]
                    S.op("dve", lambda e: e.scalar_tensor_tensor(out=kf.ap, in0=pq.ap, scalar=c.knw8.ap[:, 0:1], in1=rs.ap,
                                                                  op0=ALU.mult, op1=ALU.mult), reads=[pq, c.knw8, rs], writes=[kf])
                    S.op("pool", lambda e: e.tensor_copy(out=ko.ap, in_=kf.ap), reads=[kf], writes=[ko])
                    S.dma("pool", c.kT_s[h, :, g * 512:(g + 1) * 512], ko.ap, reads=[ko], writes=[S.db("kT_s", h)])
                    pt = c.ps()
                    for t in range(4):
                        S.op("pe", lambda e, t=t: e.transpose(out=pt.ap[:, t * 128:(t + 1) * 128], in_=kf.ap[:, t * 128:(t + 1) * 128],
                                                              identity=c.ident), reads=[kf, c.cstb], writes=[pt], sig=(t == 3))
                    kout = c.kout[h % 2]
                    S.op("act", lambda e: e.activation(out=kout.ap, in_=pt.ap.rearrange("p (t d) -> p t d", t=4), func=AF.Copy),
                         reads=[pt], writes=[kout])
                    S.dma("pool", c.k_prompt[ei, g * 512:(g + 1) * 512, h * 128:(h + 1) * 128].rearrange("(t p) d -> p t d", p=128),
                          kout.ap, reads=[kout])
        for j in (range(12) if sec & 2 else ()):
            pb = c.ps()
            _proj_fm(B, hT, 2048 + j * 128, pb)
            CB = c.CBw[j % 2]
            S.op("pool", lambda e, j=j: e.tensor_copy(out=CB.ap[:, 0:3], in_=c.gcar.ap[:, j, :]), reads=[c.gcar], writes=[CB])
            S.op("act", lambda e: e.activation(out=CB.ap[:, 3:515], in_=pb.ap, func=AF.Copy), reads=[pb], writes=[CB])
            acc = c.gtmp[(2 * j) % 4]
            sl = c.gtmp[(2 * j + 1) % 4]
            S.op("dve", lambda e: e.tensor_scalar(out=acc.ap, in0=CB.ap[:, 0:512], scalar1=c.gcw.ap[:, j, 0:1], scalar2=None,
                                                  op0=ALU.mult), reads=[CB, c.gcw], writes=[acc])
            for i in (1, 2, 3):
                S.op("dve", lambda e, i=i: e.scalar_tensor_tensor(out=acc.ap, in0=CB.ap[:, i:i + 512], scalar=c.gcw.ap[:, j, i:i + 1],
                                                                  in1=acc.ap, op0=ALU.mult, op1=ALU.add),
                     reads=[CB, c.gcw, acc], writes=[acc])
            S.op("act", lambda e: e.activation(out=sl.ap, in_=acc.ap, func=AF.Silu), reads=[acc], writes=[sl])
            go = c.gout[j % 2]
            if j < 8:
                sq = c.sqb[nb % 2]
                rs = c.rsb[nb % 2]
                nb += 1
                S.op("act", lambda e: e.activation(out=sq.ap, in_=sl.ap, func=AF.Square), reads=[sl], writes=[sq])
                p2 = c.ps()
                S.op("pe", lambda e: e.matmul(p2.ap, lhsT=c.ones_b, rhs=sq.ap, start=True, stop=True),
                     reads=[c.cbf, sq], writes=[p2])
                _rstd(B, rs, p2.ap, [p2], 1.0, EPS, 512)
                scl = (128 ** -0.5) if j < 4 else 1.0
                S.op("dve", lambda e: e.scalar_tensor_tensor(out=go.ap, in0=sl.ap, scalar=scl, in1=rs.ap, op0=ALU.mult, op1=ALU.mult),
                     reads=[sl, rs], writes=[go])
            else:
                S.op("pool", lambda e: e.tensor_copy(out=go.ap, in_=sl.ap), reads=[sl], writes=[go])
            S.dma("pool", c.g_s[j, :, g * 512:(g + 1) * 512], go.ap, reads=[go], writes=[S.db("g_s", j)])
            if g == G - 1:
                S.dma("pool", c.gdn_conv_prompt[ei, :, j * 128:(j + 1) * 128].rearrange("i p -> p i"), CB.ap[:, 512:515],
                      reads=[CB], allow_slow_non_contiguous=True)
            else:
                S.op("pool", lambda e, j=j: e.tensor_copy(out=c.gcar.ap[:, j, :], in_=CB.ap[:, 512:515]), reads=[CB], writes=[c.gcar])
        gbt = c.gbt[g % 2]
        if not (sec & 4):
            continue
        for t in range(4):
            vf, vbf, zab, zbb = c.vf[t % 2], c.vbf[t % 2], c.zab[t % 2], c.zbb[t % 2]
            r0 = g * 512 + t * 128
            pv = c.ps()
            _proj_tm(B, hT, t, 1024, 512, pv)
            sub = c.cfg.get("sub", 3)
            if sub & 1:
                S.op("act", lambda e: e.activation(out=vf.ap, in_=pv.ap, func=AF.Copy), reads=[pv], writes=[vf])
            if sub & 2:
                S.op("dve", lambda e: e.tensor_copy(out=vbf.ap, in_=pv.ap), reads=[pv], writes=[vbf])
            if sec & 16:
                S.dma("pool", c.v_prompt[ei, r0:r0 + 128, :], vf.ap, reads=[vf])
            if sec & 32:
                S.dma("pool", c.v_s[r0:r0 + 128, :], vbf.ap, reads=[vbf], writes=[S.db("v_s")])
            if not (sec & 64):
                continue
            pz = c.ps()
            _proj_tm(B, hT, t, 1536, 512, pz)
            S.op("act", lambda e: e.activation(out=zab.ap, in_=pz.ap, func=AF.Silu), reads=[pz], writes=[zab])
            S.dma("pool", c.za_s[r0:r0 + 128, :], zab.ap, reads=[zab], writes=[S.db("za_s")])
            pz2 = c.ps()
            _proj_tm(B, hT, t, 3584, 512, pz2)
            S.op("act", lambda e: e.activation(out=zbb.ap, in_=pz2.ap, func=AF.Silu), reads=[pz2], writes=[zbb])
            S.dma("pool", c.zb_s[r0:r0 + 128, :], zbb.ap, reads=[zbb], writes=[S.db("zb_s")])
            if not (sec & 8):
                continue
            pa = c.ps()
            _proj_tm(B, hT, t, 4096, 8, pa)
            S.op("dve", lambda e, t=t: e.tensor_tensor(out=gbt.ap[:, t, 0:4], in0=pa.ap[:, 0:4], in1=c.dtb.ap, op=ALU.add),
                 reads=[pa, c.dtb], writes=[gbt])
            S.op("dve", lambda e, t=t: e.tensor_copy(out=gbt.ap[:, t, 4:8], in_=pa.ap[:, 4:8]), reads=[pa], writes=[gbt])
        if not (sec & 8):
            continue
        S.op("act", lambda e: e.activation(out=gbt.ap[:, :, 0:4], in_=gbt.ap[:, :, 0:4], func=AF.Exp), reads=[gbt], writes=[gbt])
        S.op("act", lambda e: e.activation(out=gbt.ap[:, :, 0:4], in_=gbt.ap[:, :, 0:4], func=AF.Ln, bias=1.0), reads=[gbt], writes=[gbt])
        for t in range(4):
            S.op("dve", lambda e, t=t: e.tensor_tensor(out=gbt.ap[:, t, 0:4], in0=gbt.ap[:, t, 0:4], in1=c.negA.ap, op=ALU.mult),
                 reads=[gbt, c.negA], writes=[gbt])
        S.op("act", lambda e: e.activation(out=gbt.ap[:, :, 4:8], in_=gbt.ap[:, :, 4:8], func=AF.Sigmoid), reads=[gbt], writes=[gbt])
        rr = lambda ap: ap[g * 512:(g + 1) * 512, :].rearrange("(t p) d -> p t d", p=128)
        S.dma("pool", rr(c.gb_s), gbt.ap, reads=[gbt], writes=[S.db("gb_s")], allow_slow_non_contiguous=True)


def even_attn(B, li, ei):
    c, S = B, B.S
    NT = c.TP // 128
    G = c.TP // 512
    c.arena_reset()
    KT = c.ar([128, c.TP], BF16, "KT0")
    QT = c.ar([128, c.TP], BF16, "QT0")
    Vx = c.ar([128, NT, 136], BF16, "Vx0")
    S.op("pool", lambda e: e.memset(Vx.ap[:, :, 128:136], 1.0), writes=[Vx])
    pTs = [[c.ar([128, 512], BF16, f"pT{a}{m}") for m in range(2)] for a in range(2)]
    zat = c.ar([128, NT, 128], BF16, "zat")
    OAall = c.ar([128, NT, 128], F32, "OAall")
    ssall = c.ar([128, NT], F32, "ssall")
    ea = [c.ar([128, 128], F32, f"ea{i}") for i in range(4)]
    nea = 0
    ug = [c.ar([128, 128], BF16, f"ug{i}") for i in range(4)]
    sts = [c.ar([128, 4], F32, f"sta{i}") for i in range(4)]
    nst = 0
    step = 0

    def O(m, i):
        idx = m * 4 + i
        return c.psb[4 + idx // 3], (idx % 3) * 129

    for h in range(4):
        S.dma("sp", KT.ap, c.kT_s[h], reads=[S.db("kT_s", h)], writes=[KT])
        S.dma("sp", QT.ap, c.qT_s[h], reads=[S.db("qT_s", h)], writes=[QT])
        S.dma("sp", Vx.ap[:, :, 0:128], c.v_s[:, h * 128:(h + 1) * 128].rearrange("(t p) d -> p t d", p=128),
              reads=[S.db("v_s")], writes=[Vx])
        S.dma("sp", zat.ap, c.za_s[:, h * 128:(h + 1) * 128].rearrange("(t p) d -> p t d", p=128), reads=[S.db("za_s")], writes=[zat])
        for qg in range(G):
            nkt = 4 * qg + 4
            for bk in (4, 5, 6):
                S.op("pe", lambda e, bk=bk: e.matmul(c.psb[bk].ap[:, 0:387], lhsT=c.zeros_b.ap[:, 0:128], rhs=c.zeros_b.ap[:, 0:387],
                                                     start=True, stop=True, skip_group_check=True), reads=[c.zeros_b], writes=[c.psb[bk]])

            def emitS(kt, stp):
                i0 = max(0, kt - 4 * qg)
                for m in range(2):
                    pS = c.psb[2 * (stp % 2) + m]
                    S.op("pe", lambda e, m=m, pS=pS: e.matmul(pS.ap[:, i0 * 128:512], lhsT=KT.ap[m * 64:(m + 1) * 64, kt * 128:(kt + 1) * 128],
                                                              rhs=QT.ap[m * 64:(m + 1) * 64, qg * 512 + i0 * 128:(qg + 1) * 512],
                                                              start=True, stop=True), reads=[KT, QT], writes=[pS])
            emitS(0, step)
            for kt in range(nkt):
                i0 = max(0, kt - 4 * qg)
                pT = pTs[step % 2]
                if kt + 1 < nkt:
                    emitS(kt + 1, step + 1)
                for m in range(2):
                    pS = c.psb[2 * (step % 2) + m]
                    S.op("act", lambda e, m=m, pS=pS: e.activation(out=pT[m].ap[:, i0 * 128:512], in_=pS.ap[:, i0 * 128:512], func=AF.Exp,
                                                                   scale=0.125, bias=c.crow.ap[:, h:h + 1]),
                         reads=[pS, c.crow], writes=[pT[m]])
                    for i in range(i0, 4):
                        qt = 4 * qg + i
                        if kt == qt or kt == qt - 1:
                            tt = 0 if kt == qt else 1
                            S.op("dve", lambda e, m=m, i=i, tt=tt: e.tensor_tensor(
                                out=pT[m].ap[:, i * 128:(i + 1) * 128], in0=pT[m].ap[:, i * 128:(i + 1) * 128],
                                in1=c.EB.ap[:, h, tt, :], op=ALU.mult), reads=[pT[m], c.EB], writes=[pT[m]])
                step += 1
                for i in range(i0, 4):
                    qt = 4 * qg + i
                    for m in range(2):
                        ob, off = O(m, i)
                        S.op("pe", lambda e, m=m, i=i, ob=ob, off=off: e.matmul(
                            ob.ap[:, off:off + 129], lhsT=pT[m].ap[:, i * 128:(i + 1) * 128], rhs=Vx.ap[:, kt, 0:129],
                            start=False, stop=(kt == qt), skip_group_check=True), reads=[pT[m], Vx], writes=[ob])
                    if kt == qt:
                        (o1b, o1), (o2b, o2) = O(0, i), O(1, i)
                        st = sts[nst % 4]
                        nst += 1
                        ta = ea[nea % 4]
                        nea += 1
                        S.op("dve", lambda e: e.reciprocal(out=st.ap[:, 0:1], in_=o1b.ap[:, o1 + 128:o1 + 129]), reads=[o1b], writes=[st])
                        S.op("dve", lambda e: e.reciprocal(out=st.ap[:, 1:2], in_=o2b.ap[:, o2 + 128:o2 + 129]), reads=[o2b], writes=[st])
                        S.op("dve", lambda e: e.tensor_tensor(out=st.ap[:, 1:2], in0=st.ap[:, 1:2], in1=c.nlam.ap, op=ALU.mult),
                             reads=[st, c.nlam], writes=[st])
                        S.op("dve", lambda e: e.tensor_scalar(out=ta.ap, in0=o1b.ap[:, o1:o1 + 128], scalar1=st.ap[:, 0:1], scalar2=None, op0=ALU.mult),
                             reads=[o1b, st], writes=[ta])
                        S.op("dve", lambda e, qt=qt: e.scalar_tensor_tensor(out=OAall.ap[:, qt, :], in0=o2b.ap[:, o2:o2 + 128], scalar=st.ap[:, 1:2], in1=ta.ap,
                                                                             op0=ALU.mult, op1=ALU.add), reads=[o2b, st, ta], writes=[OAall])
                        S.op("act", lambda e, qt=qt: e.activation(out=ta.ap, in_=OAall.ap[:, qt, :], func=AF.Square, accum_out=ssall.ap[:, qt:qt + 1]),
                             reads=[OAall], writes=[ta, ssall])
        S.op("dve", lambda e: e.tensor_scalar(out=ssall.ap, in0=ssall.ap, scalar1=1.0 / 128, scalar2=EPS, op0=ALU.mult, op1=ALU.add),
             reads=[ssall], writes=[ssall])
        _rsq(B, ssall, ssall.ap)
        for qt in range(NT):
            t2 = ea[nea % 4]
            nea += 1
            u = ug[qt % 4]
            S.op("dve", lambda e: e.scalar_tensor_tensor(out=t2.ap, in0=OAall.ap[:, qt, :], scalar=ssall.ap[:, qt:qt + 1], in1=c.sw_row.ap,
                                                          op0=ALU.mult, op1=ALU.mult), reads=[OAall, ssall, c.sw_row], writes=[t2])
            S.op("pool", lambda e: e.tensor_tensor(out=u.ap, in0=t2.ap, in1=zat.ap[:, qt, :], op=ALU.mult), reads=[t2, zat], writes=[u])
            S.dma("pool", c.u_s[qt * 128:(qt + 1) * 128, h * 128:(h + 1) * 128], u.ap, reads=[u], writes=[S.db("u_s", qt)])


def even_out(B, li, ei):
    c, S = B, B.S
    NT = c.TP // 128
    last = (li == len(c.layers) - 1)
    c.arena_reset()
    c.ut = [c.ar([128, D], BF16, f"ut{i}") for i in range(2)]
    c.uTt = [c.ar([128, 8, 128], BF16, f"uTt{i}") for i in range(2)]
    c.xt = [c.ar([128, 1, D], F32, f"xt{i}") for i in range(2)]
    for t in range(NT):
        ut, uTt, xt = c.ut[t % 2], c.uTt[t % 2], c.xt[t % 2]
        S.dma("sp", ut.ap, c.u_s[t * 128:(t + 1) * 128, :], reads=[S.db("u_s", t)], writes=[ut])
        src = c.x_prompt if li == 0 else c.xres
        S.dma("sp", xt.ap[:, 0, :], src[t * 128:(t + 1) * 128, :], reads=[S.db("xres", t)], writes=[xt])
        pst = c.ps()
        pv = pst.ap.bitcast(BF16).rearrange("p (k t) -> p k t", k=8)
        for cb in range(8):
            S.op("pe", lambda e, cb=cb: e.transpose(out=pv[:, cb, :], in_=ut.ap[:, cb * 128:(cb + 1) * 128], identity=c.ident_b),
                 reads=[ut, c.cbf], writes=[pst], sig=(cb == 7))
        S.op("act", lambda e: e.activation(out=uTt.ap, in_=pv, func=AF.Copy), reads=[pst], writes=[uTt])
        for half in range(2):
            py = c.ps()
            for cb in range(8):
                S.op("pe", lambda e, cb=cb: e.matmul(py.ap, lhsT=uTt.ap[:, cb, :], rhs=c.wout.ap[:, cb, half * 512:(half + 1) * 512],
                                                     start=(cb == 0), stop=(cb == 7)), reads=[uTt, c.wout], writes=[py], sig=(cb == 7))
            S.op("dve", lambda e: e.tensor_tensor(out=xt.ap[:, 0, half * 512:(half + 1) * 512], in0=py.ap,
                                                  in1=xt.ap[:, 0, half * 512:(half + 1) * 512], op=ALU.add), reads=[py, xt], writes=[xt])
        dst = c.y_prompt if last else c.xres
        S.dma("pool", dst[t * 128:(t + 1) * 128, :], xt.ap[:, 0, :], reads=[xt], writes=[S.db("xres", t)])


def even_gdn(B, li, ei):
    c, S = B, B.S
    NT = c.TP // 128
    c.arena_reset()
    A = lambda n: c.ar([128, 128], F32, n)
    Sst = [[A(f"S{h}_{i}") for i in range(2)] for h in range(4)]
    names = ("qT", "kT", "vT", "kbg", "kg", "vb", "dg", "tmp", "dLs", "dTs", "EG", "qgT", "M0", "M1", "N0", "N1", "P0", "P1",
             "attnT", "wTn", "vnew", "o1", "o2")
    W = [{n: A(f"{n}{hs}") for n in names} for hs in range(4)]
    gbl = [c.ar([128, 8], F32, f"gbl{i}") for i in range(2)]
    gcs = [c.ar([128, 24], F32, f"gcs{i}") for i in range(2)]
    zbt = [c.ar([128, 512], BF16, f"zbt{i}") for i in range(2)]
    ub = [c.ar([128, 128], BF16, f"ub{i}") for i in range(4)]
    sts = [c.ar([128, 4], F32, f"stg{i}") for i in range(2)]
    for h in range(4):
        S.op("pool", lambda e, h=h: e.memset(Sst[h][0].ap, 0.0), writes=[Sst[h][0]])
    negL = c.cstb.ap[:, 896:1024]
    negU = c.negm_s
    cp = lambda dst, src_ap, srcb, **kw: S.op("act", lambda e: e.activation(out=dst.ap, in_=src_ap, func=AF.Copy, **kw),
                                              reads=srcb, writes=[dst])
    for tt in range(NT):
        gb, gc, zb, st = gbl[tt % 2], gcs[tt % 2], zbt[tt % 2], sts[tt % 2]
        r0 = tt * 128
        S.dma("sp", gb.ap, c.gb_s[r0:r0 + 128, :], reads=[S.db("gb_s")], writes=[gb])
        S.dma("sp", zb.ap, c.zb_s[r0:r0 + 128, :], reads=[S.db("zb_s")], writes=[zb])
        for h in range(4):
            w = W[h]
            S.dma("sp", w["qT"].ap, c.g_s[h, :, r0:r0 + 128], reads=[S.db("g_s", h)], writes=[w["qT"]])
            S.dma("sp", w["kT"].ap, c.g_s[4 + h, :, r0:r0 + 128], reads=[S.db("g_s", 4 + h)], writes=[w["kT"]])
            S.dma("sp", w["vT"].ap, c.g_s[8 + h, :, r0:r0 + 128], reads=[S.db("g_s", 8 + h)], writes=[w["vT"]])
        pg = c.ps()
        S.op("pe", lambda e: e.matmul(pg.ap[:, 0:4], lhsT=c.tri, rhs=gb.ap[:, 0:4], start=True, stop=True), reads=[c.cstb, gb], writes=[pg])
        S.op("pe", lambda e: e.matmul(pg.ap[:, 4:8], lhsT=c.ones, rhs=gb.ap[:, 0:4], start=True, stop=True), reads=[c.cstb, gb], writes=[pg])
        S.op("dve", lambda e: e.tensor_copy(out=gc.ap[:, 0:4], in_=pg.ap[:, 0:4]), reads=[pg], writes=[gc])
        S.op("dve", lambda e: e.tensor_scalar(out=gc.ap[:, 4:8], in0=gc.ap[:, 0:4], scalar1=-1.0, scalar2=None, op0=ALU.mult), reads=[gc], writes=[gc])
        S.op("dve", lambda e: e.tensor_tensor(out=gc.ap[:, 12:16], in0=pg.ap[:, 4:8], in1=gc.ap[:, 0:4], op=ALU.subtract), reads=[pg, gc], writes=[gc])
        S.op("dve", lambda e: e.tensor_copy(out=gc.ap[:, 16:20], in_=pg.ap[:, 4:8]), reads=[pg], writes=[gc])
        S.op("act", lambda e: e.activation(out=gc.ap[:, 8:12], in_=gc.ap[:, 0:4], func=AF.Exp), reads=[gc], writes=[gc])
        S.op("act", lambda e: e.activation(out=gc.ap[:, 12:20], in_=gc.ap[:, 12:20], func=AF.Exp), reads=[gc], writes=[gc])
        S.op("dve", lambda e: e.tensor_tensor(out=gc.ap[:, 20:24], in0=gb.ap[:, 4:8], in1=gc.ap[:, 8:12], op=ALU.mult), reads=[gb, gc], writes=[gc])
        col = lambda k, h: gc.ap[:, k * 4 + h:k * 4 + h + 1]
        for h in range(4):
            w = W[h]
            pk = c.ps()
            S.op("pe", lambda e: e.transpose(out=pk.ap[:, 0:128], in_=w["kT"].ap, identity=c.ident), reads=[w["kT"], c.cstb], writes=[pk])
            S.op("pe", lambda e: e.transpose(out=pk.ap[:, 128:256], in_=w["vT"].ap, identity=c.ident), reads=[w["vT"], c.cstb], writes=[pk])
            cp(w["kbg"], pk.ap[:, 0:128], [pk, gc], scale=col(5, h))
            cp(w["kg"], pk.ap[:, 0:128], [pk, gc], scale=col(3, h))
            cp(w["vb"], pk.ap[:, 128:256], [pk, gb], scale=gb.ap[:, 4 + h:5 + h])
            S.op("dve", lambda e: e.tensor_scalar(out=w["dg"].ap, in0=c.ident, scalar1=col(0, h), scalar2=None, op0=ALU.mult),
                 reads=[c.cstb, gc], writes=[w["dg"]])
            pr = c.ps()
            S.op("pe", lambda e: e.matmul(pr.ap[:, 0:128], lhsT=c.ones, rhs=w["dg"].ap, start=True, stop=True), reads=[c.cstb, w["dg"]], writes=[pr])
            S.op("dve", lambda e: e.scalar_tensor_tensor(out=w["tmp"].ap, in0=pr.ap[:, 0:128], scalar=-1.0, in1=negL, op0=ALU.mult, op1=ALU.add),
                 reads=[pr, c.cstb], writes=[w["tmp"]])
            S.op("dve", lambda e: e.tensor_tensor(out=w["dTs"].ap, in0=pr.ap[:, 0:128], in1=negU, op=ALU.add), reads=[pr, c.cstb], writes=[w["dTs"]])
            S.op("act", lambda e: e.activation(out=w["EG"].ap, in_=pr.ap[:, 0:128], func=AF.Exp), reads=[pr], writes=[w["EG"]])
            S.op("act", lambda e: e.activation(out=w["dLs"].ap, in_=w["tmp"].ap, func=AF.Exp, bias=col(0, h)), reads=[w["tmp"], gc], writes=[w["dLs"]])
            S.op("act", lambda e: e.activation(out=w["dTs"].ap, in_=w["dTs"].ap, func=AF.Exp, bias=col(1, h)), reads=[w["dTs"], gc], writes=[w["dTs"]])
            S.op("pool", lambda e: e.tensor_tensor(out=w["qgT"].ap, in0=w["qT"].ap, in1=w["EG"].ap, op=ALU.mult), reads=[w["qT"], w["EG"]], writes=[w["qgT"]])
            S.op("pool", lambda e: e.tensor_tensor(out=w["dTs"].ap, in0=w["dTs"].ap, in1=c.ident, op=ALU.add), reads=[w["dTs"], c.cstb], writes=[w["dTs"]])
        for h in range(4):
            w = W[h]
            pkk = c.ps()
            S.op("pe", lambda e: e.matmul(pkk.ap[:, 0:128], lhsT=w["kT"].ap, rhs=w["kT"].ap, start=True, stop=True), reads=[w["kT"]], writes=[pkk])
            S.op("pe", lambda e: e.matmul(pkk.ap[:, 128:256], lhsT=w["kT"].ap, rhs=w["qT"].ap, start=True, stop=True), reads=[w["kT"], w["qT"]], writes=[pkk])
            S.op("dve", lambda e: e.scalar_tensor_tensor(out=w["M0"].ap, in0=pkk.ap[:, 0:128], scalar=gb.ap[:, 4 + h:5 + h], in1=w["dLs"].ap,
                                                          op0=ALU.mult, op1=ALU.mult), reads=[pkk, gb, w["dLs"]], writes=[w["M0"]])
            S.op("dve", lambda e: e.tensor_tensor(out=w["attnT"].ap, in0=pkk.ap[:, 128:256], in1=w["dTs"].ap, op=ALU.mult),
                 reads=[pkk, w["dTs"]], writes=[w["attnT"]])
            pn = c.ps()
            S.op("pe", lambda e: e.transpose(out=pn.ap[:, 0:128], in_=w["M0"].ap, identity=c.ident), reads=[w["M0"], c.cstb], writes=[pn])
            cp(w["N0"], pn.ap[:, 0:128], [pn])
            S.op("dve", lambda e: e.tensor_tensor(out=w["P0"].ap, in0=c.ident, in1=pn.ap[:, 0:128], op=ALU.subtract), reads=[pn, c.cstb], writes=[w["P0"]])
        Mc, Nc, Pc = "M0", "N0", "P0"
        for lvl in range(1, 7):
            Mn, Nn, Pn = ("M1", "N1", "P1") if Mc == "M0" else ("M0", "N0", "P0")
            for h in range(4):
                w = W[h]
                pm = c.ps()
                S.op("pe", lambda e: e.matmul(pm.ap[:, 0:128], lhsT=w[Nc].ap, rhs=w[Mc].ap, start=True, stop=True),
                     reads=[w[Nc], w[Mc]], writes=[pm])
                if lvl < 6:
                    S.op("pe", lambda e: e.matmul(pm.ap[:, 128:256], lhsT=w[Mc].ap, rhs=w[Nc].ap, start=True, stop=True),
                         reads=[w[Nc], w[Mc]], writes=[pm])
                cp(w[Mn], pm.ap[:, 0:128], [pm])
                if lvl < 6:
                    S.op("dve", lambda e: e.tensor_copy(out=w[Nn].ap, in_=pm.ap[:, 128:256]), reads=[pm], writes=[w[Nn]])
            for h in range(4):
                w = W[h]
                pp = c.ps()
                S.op("pe", lambda e: e.matmul(pp.ap[:, 0:128], lhsT=c.ident, rhs=w[Pc].ap, start=True, stop=False),
                     reads=[w[Pc], c.cstb], writes=[pp], sig=False)
                S.op("pe", lambda e: e.matmul(pp.ap[:, 0:128], lhsT=w[Mn].ap, rhs=w[Pc].ap, start=False, stop=True),
                     reads=[w[Pc], w[Mn]], writes=[pp])
                cp(w[Pn], pp.ap[:, 0:128], [pp])
            Mc, Nc, Pc = Mn, Nn, Pn
        for h in range(4):
            w = W[h]
            Sc, Sn = Sst[h][tt % 2], Sst[h][(tt + 1) % 2]
            TT = w[Pc]
            pw = c.ps()
            S.op("pe", lambda e: e.matmul(pw.ap[:, 0:128], lhsT=w["kbg"].ap, rhs=TT.ap, start=True, stop=True), reads=[w["kbg"], TT], writes=[pw])
            S.op("act", lambda e: e.activation(out=w["wTn"].ap, in_=pw.ap[:, 0:128], func=AF.Copy, scale=-1.0), reads=[pw], writes=[w["wTn"]])
            pvn = c.ps()
            S.op("pe", lambda e: e.matmul(pvn.ap[:, 0:128], lhsT=TT.ap, rhs=w["vb"].ap, start=True, stop=False), reads=[TT, w["vb"]], writes=[pvn], sig=False)
            S.op("pe", lambda e: e.matmul(pvn.ap[:, 0:128], lhsT=w["wTn"].ap, rhs=Sc.ap, start=False, stop=True), reads=[w["wTn"], Sc], writes=[pvn])
            cp(w["vnew"], pvn.ap[:, 0:128], [pvn])
            po = c.ps()
            S.op("pe", lambda e: e.matmul(po.ap[:, 0:128], lhsT=w["qgT"].ap, rhs=Sc.ap, start=True, stop=False), reads=[w["qgT"], Sc], writes=[po], sig=False)
            S.op("pe", lambda e: e.matmul(po.ap[:, 0:128], lhsT=w["attnT"].ap, rhs=w["vnew"].ap, start=False, stop=True),
                 reads=[w["attnT"], w["vnew"]], writes=[po])
            S.op("pe", lambda e: e.matmul(po.ap[:, 128:256], lhsT=w["kg"].ap, rhs=w["vnew"].ap, start=True, stop=True),
                 reads=[w["kg"], w["vnew"]], writes=[po])
            S.op("dve", lambda e: e.scalar_tensor_tensor(out=Sn.ap, in0=Sc.ap, scalar=col(4, h), in1=po.ap[:, 128:256], op0=ALU.mult, op1=ALU.add),
                 reads=[Sc, gc, po], writes=[Sn])
            cp(w["o1"], po.ap[:, 0:128], [po])
            S.op("act", lambda e: e.activation(out=w["o2"].ap, in_=w["o1"].ap, func=AF.Square, accum_out=st.ap[:, h:h + 1]),
                 reads=[w["o1"]], writes=[w["o2"], st])
            if tt == NT - 1:
                S.dma("pool", c.gdn_prompt[ei, h], Sn.ap, reads=[Sn])
        S.op("dve", lambda e: e.tensor_scalar(out=st.ap, in0=st.ap, scalar1=1.0 / 128, scalar2=EPS, op0=ALU.mult, op1=ALU.add), reads=[st], writes=[st])
        _rsq(B, st, st.ap)
        for h in range(4):
            w = W[h]
            S.op("dve", lambda e: e.scalar_tensor_tensor(out=w["o2"].ap, in0=w["o1"].ap, scalar=st.ap[:, h:h + 1], in1=c.gnw_row.ap,
                                                          op0=ALU.mult, op1=ALU.mult), reads=[w["o1"], st, c.gnw_row], writes=[w["o2"]])
            u = ub[h]
            S.op("pool", lambda e: e.tensor_tensor(out=u.ap, in0=w["o2"].ap, in1=zb.ap[:, h * 128:(h + 1) * 128], op=ALU.mult),
                 reads=[w["o2"], zb], writes=[u])
            S.dma("pool", c.u_s[r0:r0 + 128, 512 + h * 128:512 + (h + 1) * 128], u.ap, reads=[u], writes=[S.db("u_s", tt)])


def prompt_even(B, li, ei):
    c = B
    c.load_w(c.win, c.w_in_even[ei], P_EVEN, c.wstg)
    c.load_w(c.wout, c.w_out_even[ei], D, c.wstg)
    upto = c.cfg.get("upto", 9)
    if upto >= 1:
        even_params(B, li, ei)
    if upto >= 2:
        even_proj(B, li, ei)
    if upto < 3:
        return
    if c.cfg.get("attn", True):
        even_attn(B, li, ei)
    if c.cfg.get("gdn", True):
        even_gdn(B, li, ei)
    even_out(B, li, ei)


def decode_all(B):
    c, S = B, B.S
    NS = c.NS
    NPG = c.NPAGES
    c.arena_reset()
    A = c.ar
    xs = A([NS, D], F32, "xs")
    S.dma("sp", xs.ap, c.x_sample, writes=[xs])
    hbd = A([NS, D], BF16, "hbd")
    jk = A([NS, D], BF16, "jkd")
    hTd = A([128, 8, NS], BF16, "hTd")
    ssd = A([NS, 2], F32, "ssd")
    pcol = c.cstb.ap[:, 1536:1537]
    sel = c.cstb.ap[0:4, 1024:1536]

    def evac(dst_ap, src_ap, srcb, dstb, eng="act", func=AF.Copy, **kw):
        S.op("act", lambda e: e.activation(out=dst_ap, in_=src_ap, func=func, **kw), reads=srcb, writes=dstb)

    def normT(li):
        S.dma("sp", c.normw.ap, c.norm_w[li:li + 1, :].partition_broadcast(128), writes=[c.normw])
        S.op("act", lambda e: e.activation(out=jk.ap, in_=xs.ap, func=AF.Square, accum_out=ssd.ap[:, 0:1]), reads=[xs], writes=[jk, ssd])
        S.op("dve", lambda e: e.tensor_scalar(out=ssd.ap[:, 0:1], in0=ssd.ap[:, 0:1], scalar1=1.0 / D, scalar2=EPS, op0=ALU.mult, op1=ALU.add),
             reads=[ssd], writes=[ssd])
        _rsq(B, ssd, ssd.ap[:, 0:1])
        S.op("dve", lambda e: e.scalar_tensor_tensor(out=hbd.ap, in0=xs.ap, scalar=ssd.ap[:, 0:1], in1=c.normw.ap[0:NS, :],
                                                      op0=ALU.mult, op1=ALU.mult), reads=[xs, ssd, c.normw], writes=[hbd])
        pst = c.ps()
        pv = pst.ap.bitcast(BF16)[:, 0:8 * NS].rearrange("p (k t) -> p k t", k=8)
        for kc in range(8):
            S.op("pe", lambda e, kc=kc: e.transpose(out=pv[:, kc, :], in_=hbd.ap[:, kc * 128:(kc + 1) * 128], identity=c.ident_b[0:NS, 0:NS]),
                 reads=[hbd, c.cbf], writes=[pst], sig=(kc == 7))
        evac(hTd.ap, pv, [pst], [hTd])

    def proj(pst, slot, col0, m=128):
        for kc in range(8):
            S.op("pe", lambda e, kc=kc: e.matmul(pst.ap[0:m, slot * NS:(slot + 1) * NS], lhsT=c.win.ap[:, kc, col0:col0 + m], rhs=hTd.ap[:, kc, :],
                                                 start=(kc == 0), stop=(kc == 7)), reads=[c.win, hTd], writes=[pst], sig=(kc == 7))

    def tr_out(src_ap, srcb, nblk, dst_dram_ap, tag):
        to = A([NS, nblk * 128], F32, "to_" + tag)
        for b0 in range(0, nblk, 4):
            nb = min(4, nblk - b0)
            pst = c.ps()
            for b in range(nb):
                S.op("pe", lambda e, b=b: e.transpose(out=pst.ap[0:NS, b * 128:(b + 1) * 128], in_=src_ap[:, b0 + b, :], identity=c.ident),
                     reads=srcb + [c.cstb], writes=[pst], sig=(b == nb - 1))
            evac(to.ap[:, b0 * 128:(b0 + nb) * 128], pst.ap[0:NS, 0:nb * 128], [pst], [to])
        S.dma("pool", dst_dram_ap, to.ap, reads=[to])

    def tr_in(dst, src_dram_ap, nblk, tag):
        ti = A([NS, nblk * 128], F32, "ti_" + tag)
        S.dma("sp", ti.ap, src_dram_ap, writes=[ti])
        for b0 in range(0, nblk, 16):
            nb = min(16, nblk - b0)
            pst = c.ps()
            for b in range(nb):
                S.op("pe", lambda e, b=b: e.transpose(out=pst.ap[:, b * NS:(b + 1) * NS], in_=ti.ap[:, (b0 + b) * 128:(b0 + b + 1) * 128],
                                                      identity=c.ident[0:NS, 0:NS]), reads=[ti, c.cstb], writes=[pst], sig=(b == nb - 1))
            evac(dst.ap[:, b0:b0 + nb, :], pst.ap[:, 0:nb * NS].rearrange("p (b s) -> p b s", b=nb), [pst], [dst])

    def colsum_bc(dst, src, ncol, lhsT=None):
        pst = c.ps()
        S.op("pe", lambda e: e.matmul(pst.ap[:, 0:ncol], lhsT=c.ones if lhsT is None else lhsT, rhs=src.ap, start=True, stop=True),
             reads=[src, c.cstb], writes=[pst])
        return pst

    def outproj(uT):
        for half in range(2):
            py = c.ps()
            for cb in range(8):
                S.op("pe", lambda e, cb=cb: e.matmul(py.ap[0:NS, :], lhsT=uT.ap[:, cb, :], rhs=c.wout.ap[:, cb, half * 512:(half + 1) * 512],
                                                     start=(cb == 0), stop=(cb == 7)), reads=[uT, c.wout], writes=[py], sig=(cb == 7))
            S.op("dve", lambda e: e.tensor_tensor(out=xs.ap[:, half * 512:(half + 1) * 512], in0=py.ap[0:NS, :],
                                                  in1=xs.ap[:, half * 512:(half + 1) * 512], op=ALU.add), reads=[py, xs], writes=[xs])

    amark = c.aoff
    ei = oi = 0
    for li, ch in enumerate(c.layers):
        c.arena_reset()
        c.aoff = amark
        uT = A([128, 8, NS], BF16, "uTd")
        if ch == "o":
            c.load_w(c.win, c.w_in_odd[oi], P_ODD, c.wstg)
            c.load_w(c.wout, c.w_out_odd[oi], D, c.wstg)
            for j in range(3):
                S.dma("sp", c.scw.ap[:, :, j], c.sc_conv_w[oi, j].rearrange("(c p) -> p c", p=128), writes=[c.scw], allow_slow_non_contiguous=True)
            normT(li)
            P4 = A([128, 32, NS], F32, "P4")
            for b0 in (0, 16):
                pst = c.ps()
                for b in range(16):
                    proj(pst, b, (b0 + b) * 128)
                evac(P4.ap[:, b0:b0 + 16, :], pst.ap[:, 0:16 * NS].rearrange("p (b s) -> p b s", b=16), [pst], [P4])
            st = A([128, 16, NS], F32, "scst")
            tr_in(st, c.state_shortconv[oi].rearrange("s j d -> s (j d)"), 16, "sc")
            chh = A([128, 8, NS], F32, "chh")
            cv = A([128, 8, NS], F32, "cv")
            S.op("dve", lambda e: e.tensor_tensor(out=chh.ap, in0=P4.ap[:, 8:16, :], in1=P4.ap[:, 16:24, :], op=ALU.mult), reads=[P4], writes=[chh])
            for cb in range(8):
                S.op("dve", lambda e, cb=cb: e.tensor_scalar(out=cv.ap[:, cb, :], in0=st.ap[:, cb, :], scalar1=c.scw.ap[:, cb, 0:1], scalar2=None,
                                                             op0=ALU.mult), reads=[st, c.scw], writes=[cv])
                S.op("dve", lambda e, cb=cb: e.scalar_tensor_tensor(out=cv.ap[:, cb, :], in0=st.ap[:, 8 + cb, :], scalar=c.scw.ap[:, cb, 1:2],
                                                                    in1=cv.ap[:, cb, :], op0=ALU.mult, op1=ALU.add), reads=[st, c.scw, cv], writes=[cv])
                S.op("dve", lambda e, cb=cb: e.scalar_tensor_tensor(out=cv.ap[:, cb, :], in0=chh.ap[:, cb, :], scalar=c.scw.ap[:, cb, 2:3],
                                                                    in1=cv.ap[:, cb, :], op0=ALU.mult, op1=ALU.add), reads=[chh, c.scw, cv], writes=[cv])
            sz = A([128, 8, NS], F32, "sz")
            evac(sz.ap, P4.ap[:, 24:32, :], [P4], [sz], func=AF.Silu)
            S.op("dve", lambda e: e.tensor_tensor(out=cv.ap, in0=cv.ap, in1=P4.ap[:, 0:8, :], op=ALU.mult), reads=[cv, P4], writes=[cv])
            S.op("dve", lambda e: e.tensor_tensor(out=uT.ap, in0=cv.ap, in1=sz.ap, op=ALU.mult), reads=[cv, sz], writes=[uT])
            S.dma("pool", c.sc_sample[oi, :, 0, :], c.state_shortconv[oi, :, 1, :])
            tr_out(chh.ap, [chh], 8, c.sc_sample[oi, :, 1, :], "sc")
            outproj(uT)
            oi += 1
            continue
        lambda_init = 0.8 - 0.6 * math.exp(-0.3 * li)
        c.load_w(c.win, c.w_in_even[ei], P_EVEN, c.wstg)
        c.load_w(c.wout, c.w_out_even[ei], D, c.wstg)
        even_params(B, li, ei)
        fmp = A([128, 8], F32, "fmp")
        S.dma("sp", fmp.ap[:, 0:1], c.subln_w[ei].rearrange("(p o) -> p o", o=1), writes=[fmp])
        S.dma("sp", fmp.ap[:, 1:2], c.gdn_norm_w[ei].rearrange("(p o) -> p o", o=1), writes=[fmp])
        S.dma("sp", fmp.ap[:, 2:6], c.rel_table[0:1, :].partition_broadcast(128), writes=[fmp])
        S.op("dve", lambda e: e.tensor_scalar(out=fmp.ap[:, 0:1], in0=fmp.ap[:, 0:1], scalar1=1.0 - lambda_init, scalar2=None, op0=ALU.mult),
             reads=[fmp], writes=[fmp])
        evac(fmp.ap[:, 2:6], fmp.ap[:, 2:6], [fmp], [fmp], func=AF.Exp)
        ab4 = A([4, 2], F32, "ab4")
        S.dma("sp", ab4.ap[:, 0:1], c.gdn_a_log[ei].rearrange("(p o) -> p o", o=1), writes=[ab4])
        S.dma("sp", ab4.ap[:, 1:2], c.gdn_dt_bias[ei].rearrange("(p o) -> p o", o=1), writes=[ab4])
        evac(ab4.ap[:, 0:1], ab4.ap[:, 0:1], [ab4], [ab4], func=AF.Exp)
        S.op("dve", lambda e: e.tensor_scalar(out=ab4.ap[:, 0:1], in0=ab4.ap[:, 0:1], scalar1=-1.0, scalar2=None, op0=ALU.mult), reads=[ab4], writes=[ab4])
        eb = A([128, 2, 8], F32, "ebd")
        ohd = A([33, 128], F32, "ohd")
        S.dma("sp", ohd.ap, c.oh[:, 2 * 128 * 128:2 * 128 * 128 + 128], writes=[ohd])
        pb_ = c.ps()
        S.op("pe", lambda e: e.matmul(pb_.ap[:, 0:4], lhsT=ohd.ap[0:32, :], rhs=c.relx.ap[0:32, :], start=True, stop=True), reads=[ohd, c.relx], writes=[pb_])
        for m in range(2):
            evac(eb.ap[:, 1, :].rearrange("p (h m) -> p h m", m=2)[:, :, m], pb_.ap[:, 0:4], [pb_], [eb], func=AF.Exp)
            evac(eb.ap[:, 0, :].rearrange("p (h m) -> p h m", m=2)[:, :, m], c.crow.ap, [c.crow], [eb], func=AF.Exp)
        normT(li)
        PQ = A([128, 8, NS], F32, "PQ")
        PVZ = A([128, 12, NS], F32, "PVZ")
        PG = A([128, 12, NS], F32, "PGd")
        pst = c.ps()
        for b in range(8):
            proj(pst, b, b * 128)
        evac(PQ.ap, pst.ap[:, 0:8 * NS].rearrange("p (b s) -> p b s", b=8), [pst], [PQ])
        pst = c.ps()
        for b in range(8):
            proj(pst, b, 1024 + b * 128)
        for b in range(4):
            proj(pst, 8 + b, 3584 + b * 128)
        evac(PVZ.ap, pst.ap[:, 0:12 * NS].rearrange("p (b s) -> p b s", b=12), [pst], [PVZ])
        pst = c.ps()
        for b in range(12):
            proj(pst, b, 2048 + b * 128)
        evac(PG.ap, pst.ap[:, 0:12 * NS].rearrange("p (b s) -> p b s", b=12), [pst], [PG])
        ga = A([4, 2, NS], F32, "ga")
        pst = c.ps()
        proj(pst, 0, 4096, 4)
        proj(pst, 1, 4100, 4)
        evac(ga.ap[:, 0, :], pst.ap[0:4, 0:NS], [pst, ab4], [ga], func=AF.Exp, bias=ab4.ap[:, 1:2])
        evac(ga.ap[:, 1, :], pst.ap[0:4, NS:2 * NS], [pst], [ga], func=AF.Sigmoid)
        evac(ga.ap[:, 0, :], ga.ap[:, 0, :], [ga], [ga], func=AF.Ln, bias=1.0)
        S.op("dve", lambda e: e.tensor_scalar(out=ga.ap[:, 0, :], in0=ga.ap[:, 0, :], scalar1=ab4.ap[:, 0:1], scalar2=None, op0=ALU.mult),
             reads=[ga, ab4], writes=[ga])
        EGB = A([128, 2, 4, NS], F32, "EGB")
        pst = c.ps()
        for t in range(2):
            for h in range(4):
                S.op("pe", lambda e, t=t, h=h: e.matmul(pst.ap[:, (t * 4 + h) * NS:(t * 4 + h + 1) * NS], lhsT=sel[:, h * 128:(h + 1) * 128],
                                                        rhs=ga.ap[:, t, :], start=True, stop=True), reads=[ga, c.cstb], writes=[pst])
        evac(EGB.ap[:, 0, :, :], pst.ap[:, 0:4 * NS].rearrange("p (h s) -> p h s", h=4), [pst], [EGB], func=AF.Exp)
        evac(EGB.ap[:, 1, :, :], pst.ap[:, 4 * NS:8 * NS].rearrange("p (h s) -> p h s", h=4), [pst], [EGB])
        sq = A([128, 8 * NS], BF16, "sqd")
        rs = A([128, 8, NS], F32, "rsd")
        evac(sq.ap, PQ.ap.rearrange("p b s -> p (b s)"), [PQ], [sq], func=AF.Square)
        p2 = c.ps()
        S.op("pe", lambda e: e.matmul(p2.ap[:, 0:8 * NS], lhsT=c.bones_b, rhs=sq.ap, start=True, stop=True), reads=[c.cbf, sq], writes=[p2])
        S.op("dve", lambda e: e.tensor_scalar(out=rs.ap.rearrange("p b s -> p (b s)"), in0=p2.ap[:, 0:8 * NS], scalar1=1.0, scalar2=64 * EPS,
                                              op0=ALU.mult, op1=ALU.add), reads=[p2], writes=[rs])
        _rsq(B, rs, rs.ap.rearrange("p b s -> p (b s)"))
        for t, wv in ((0, c.qnw8), (1, c.knw8)):
            S.op("dve", lambda e, t=t, wv=wv: e.scalar_tensor_tensor(out=PQ.ap[:, 4 * t:4 * t + 4, :], in0=PQ.ap[:, 4 * t:4 * t + 4, :], scalar=wv.ap[:, 0:1],
                                                                      in1=rs.ap[:, 4 * t:4 * t + 4, :], op0=ALU.mult, op1=ALU.mult),
                 reads=[PQ, wv, rs], writes=[PQ])
        tr_out(PQ.ap[:, 4:8, :], [PQ], 4, c.k_sample[ei], "k")
        tr_out(PVZ.ap[:, 0:4, :], [PVZ], 4, c.v_sample[ei], "v")
        prod = A([128, 4, NS], F32, "prodn")
        S.op("dve", lambda e: e.tensor_tensor(out=prod.ap, in0=PQ.ap[:, 0:4, :], in1=PQ.ap[:, 4:8, :], op=ALU.mult), reads=[PQ], writes=[prod])
        pnew = A([128, 2, 4, NS], F32, "pnew")
        for m in range(2):
            pst = c.ps()
            S.op("pe", lambda e, m=m: e.matmul(pst.ap[:, 0:4 * NS], lhsT=c.hones[:, m * 128:(m + 1) * 128], rhs=prod.ap.rearrange("p h s -> p (h s)"),
                                               start=True, stop=True), reads=[prod, c.cstb], writes=[pst])
            evac(pnew.ap[:, m, :, :], pst.ap[:, 0:4 * NS].rearrange("p (h s) -> p h s", h=4), [pst], [pnew], func=AF.Exp, scale=0.125)
            for h in range(4):
                S.op("dve", lambda e, m=m, h=h: e.tensor_scalar(out=pnew.ap[:, m, h, :], in0=pnew.ap[:, m, h, :], scalar1=fmp.ap[:, 2 + h:3 + h], scalar2=None,
                                                                op0=ALU.mult), reads=[pnew, fmp], writes=[pnew])
        OA = A([128, 4, NS], F32, "OAd")
        ptb = [A([128, NPG], I32, f"ptb{i}") for i in range(2)]
        idx = [A([128, NPG], I32, f"idx{i}") for i in range(2)]
        Kp = [A([128, 512], F32, f"Kp{i}") for i in range(2)]
        Vp = [A([128, 512], F32, f"Vp{i}") for i in range(2)]
        qrow = [A([128, 512], F32, f"qrow{i}") for i in range(2)]
        dq = [A([128, 128], F32, f"dq{i}") for i in range(2)]
        prd = [A([128, 512], F32, "prd0")] * 2
        s8 = [A([128, 8], F32, f"s8{i}") for i in range(2)]
        s8b = [A([128, 8], BF16, f"s8b{i}") for i in range(2)]
        Vpb = [A([128, 512], BF16, f"Vpb{i}") for i in range(2)]
        fin = [A([128, 40], F32, f"fin{i}") for i in range(2)]
        npg = 0
        for s_ in range(NS):
            pt, ix, qr = ptb[s_ % 2], idx[s_ % 2], qrow[s_ % 2]
            S.dma("sp", pt.ap, c.page_table[s_:s_ + 1, :].partition_broadcast(128), writes=[pt])
            S.op("dve", lambda e: e.tensor_scalar(out=ix.ap, in0=pt.ap, scalar1=128.0, scalar2=pcol, op0=ALU.mult, op1=ALU.add),
                 reads=[pt, c.cstb], writes=[ix])
            if ei > 0:
                S.op("dve", lambda e: e.tensor_scalar(out=ix.ap, in0=ix.ap, scalar1=float(ei * c.NPOOL * 128), scalar2=None, op0=ALU.add),
                     reads=[ix], writes=[ix])
            pq_ = c.ps()
            for h in range(4):
                d_ = dq[h % 2]
                S.op("dve", lambda e, h=h: e.tensor_scalar(out=d_.ap, in0=c.ident, scalar1=PQ.ap[:, h, s_:s_ + 1], scalar2=None, op0=ALU.mult),
                     reads=[PQ, c.cstb], writes=[d_])
                S.op("pe", lambda e, h=h: e.matmul(pq_.ap[:, h * 128:(h + 1) * 128], lhsT=c.ones, rhs=d_.ap, start=True, stop=True),
                     reads=[d_, c.cstb], writes=[pq_])
            evac(qr.ap, pq_.ap, [pq_], [qr])
            po_ = c.ps()
            S.op("pe", lambda e: e.matmul(po_.ap[:, 0:16], lhsT=c.zeros_b.ap[:, 0:128], rhs=c.zeros_b.ap[:, 0:16], start=True, stop=True, skip_group_check=True),
                 reads=[c.zeros_b], writes=[po_])
            for j in range(NPG):
                kp, vp, pr_, s8_ = Kp[npg % 2], Vp[npg % 2], prd[npg % 2], s8[npg % 2]
                npg += 1
                npg_ = npg - 1
                S.idma(kp.ap, c.cache_k.rearrange("e r d -> (e r) d"), ix.ap[:, j:j + 1], reads=[ix], writes=[kp])
                S.idma(vp.ap, c.cache_v.rearrange("e r d -> (e r) d"), ix.ap[:, j:j + 1], reads=[ix], writes=[vp])
                S.op("dve", lambda e: e.tensor_tensor(out=pr_.ap, in0=kp.ap, in1=qr.ap, op=ALU.mult), reads=[kp, qr], writes=[pr_])
                S.op("dve", lambda e: e.tensor_reduce(out=s8_.ap, in_=pr_.ap.rearrange("p (g d) -> p g d", d=64), axis=AX.X, op=ALU.add),
                     reads=[pr_], writes=[s8_])
                evac(s8_.ap, s8_.ap, [s8_], [s8_], func=AF.Exp, scale=0.125)
                S.op("dve", lambda e, j=j: e.tensor_tensor(out=s8_.ap, in0=s8_.ap, in1=eb.ap[:, 1 if j == NPG - 1 else 0, :], op=ALU.mult),
                     reads=[s8_, eb], writes=[s8_])
                vb_, p8b = Vpb[npg_ % 2], s8b[npg_ % 2]
                S.op("act", lambda e: e.activation(out=vb_.ap, in_=vp.ap, func=AF.Copy), reads=[vp], writes=[vb_])
                S.op("dve", lambda e: e.tensor_copy(out=p8b.ap, in_=s8_.ap), reads=[s8_], writes=[p8b])
                for h in range(4):
                    S.op("pe", lambda e, h=h: e.matmul(po_.ap[:, 2 * h:2 * h + 2], lhsT=vb_.ap[:, h * 128:(h + 1) * 128], rhs=p8b.ap[:, 2 * h:2 * h + 2],
                                                       start=False, stop=False, skip_group_check=True), reads=[vb_, p8b], writes=[po_], sig=False)
                S.op("pe", lambda e: e.matmul(po_.ap[:, 8:16], lhsT=c.ones_b, rhs=p8b.ap, start=False, stop=(j == NPG - 1), skip_group_check=True),
                     reads=[p8b, c.cbf], writes=[po_])
            f = fin[s_ % 2]
            fo = f.ap[:, 0:8].rearrange("p (h m) -> p h m", m=2)
            fl = f.ap[:, 8:16].rearrange("p (h m) -> p h m", m=2)
            S.op("dve", lambda e: e.tensor_copy(out=f.ap[:, 0:16], in_=po_.ap[:, 0:16]), reads=[po_], writes=[f])
            for m in range(2):
                S.op("dve", lambda e, m=m: e.tensor_tensor(out=f.ap[:, 16 + 4 * m:20 + 4 * m], in0=pnew.ap[:, m, :, s_], in1=PVZ.ap[:, 0:4, s_], op=ALU.mult),
                     reads=[pnew, PVZ], writes=[f])
                S.op("dve", lambda e, m=m: e.tensor_tensor(out=fo[:, :, m], in0=fo[:, :, m], in1=f.ap[:, 16 + 4 * m:20 + 4 * m], op=ALU.add), reads=[f], writes=[f])
                S.op("dve", lambda e, m=m: e.tensor_tensor(out=fl[:, :, m], in0=fl[:, :, m], in1=pnew.ap[:, m, :, s_], op=ALU.add), reads=[f, pnew], writes=[f])
            S.op("dve", lambda e: e.reciprocal(out=f.ap[:, 8:16], in_=f.ap[:, 8:16]), reads=[f], writes=[f])
            S.op("dve", lambda e: e.tensor_tensor(out=f.ap[:, 0:8], in0=f.ap[:, 0:8], in1=f.ap[:, 8:16], op=ALU.mult), reads=[f], writes=[f])
            S.op("dve", lambda e: e.scalar_tensor_tensor(out=OA.ap[:, :, s_], in0=fo[:, :, 1], scalar=c.nlam.ap[:, 0:1], in1=fo[:, :, 0],
                                                          op0=ALU.mult, op1=ALU.add), reads=[f, c.nlam], writes=[OA])
        sq2 = A([128, 4 * NS], F32, "sq2")
        OAf = OA.ap.rearrange("p h s -> p (h s)")
        S.op("dve", lambda e: e.tensor_tensor(out=sq2.ap, in0=OAf, in1=OAf, op=ALU.mult), reads=[OA], writes=[sq2])
        pst = colsum_bc(None, sq2, 4 * NS)
        S.op("dve", lambda e: e.tensor_scalar(out=sq2.ap, in0=pst.ap[:, 0:4 * NS], scalar1=1.0 / 128, scalar2=EPS, op0=ALU.mult, op1=ALU.add), reads=[pst], writes=[sq2])
        _rsq(B, sq2, sq2.ap)
        sza = A([128, 8, NS], F32, "sza")
        evac(sza.ap, PVZ.ap[:, 4:12, :], [PVZ], [sza], func=AF.Silu)
        S.op("dve", lambda e: e.scalar_tensor_tensor(out=OAf, in0=OAf, scalar=fmp.ap[:, 0:1], in1=sq2.ap, op0=ALU.mult, op1=ALU.mult), reads=[OA, fmp, sq2], writes=[OA])
        S.op("dve", lambda e: e.tensor_tensor(out=uT.ap[:, 0:4, :], in0=OA.ap, in1=sza.ap[:, 0:4, :], op=ALU.mult), reads=[OA, sza], writes=[uT])
        cst3 = A([128, 36, NS], F32, "cst3")
        tr_in(cst3, c.state_gdn_conv[ei].rearrange("s i d -> s (i d)"), 36, "gc")
        cbd = A([128, 12, NS], F32, "cbd")
        for j in range(12):
            S.op("dve", lambda e, j=j: e.tensor_scalar(out=cbd.ap[:, j, :], in0=PG.ap[:, j, :], scalar1=c.gcw.ap[:, j, 3:4], scalar2=None, op0=ALU.mult),
                 reads=[PG, c.gcw], writes=[cbd])
            for i in range(3):
                S.op("dve", lambda e, j=j, i=i: e.scalar_tensor_tensor(out=cbd.ap[:, j, :], in0=cst3.ap[:, i * 12 + j, :], scalar=c.gcw.ap[:, j, i:i + 1],
                                                                       in1=cbd.ap[:, j, :], op0=ALU.mult, op1=ALU.add), reads=[cst3, c.gcw, cbd], writes=[cbd])
        for i in range(2):
            S.dma("pool", c.gdn_conv_sample[ei, :, i, :], c.state_gdn_conv[ei, :, i + 1, :])
        tr_out(PG.ap, [PG], 12, c.gdn_conv_sample[ei, :, 2, :], "gcs")
        evac(cbd.ap, cbd.ap, [cbd], [cbd], func=AF.Silu)
        sq3 = A([128, 8 * NS], F32, "sq3")
        cb8 = cbd.ap[:, 0:8, :].rearrange("p b s -> p (b s)")
        S.op("dve", lambda e: e.tensor_tensor(out=sq3.ap, in0=cb8, in1=cb8, op=ALU.mult), reads=[cbd], writes=[sq3])
        pst = colsum_bc(None, sq3, 8 * NS)
        S.op("dve", lambda e: e.tensor_scalar(out=sq3.ap, in0=pst.ap[:, 0:8 * NS], scalar1=1.0, scalar2=EPS, op0=ALU.mult, op1=ALU.add), reads=[pst], writes=[sq3])
        _rsq(B, sq3, sq3.ap)
        S.op("dve", lambda e: e.tensor_tensor(out=cb8, in0=cb8, in1=sq3.ap, op=ALU.mult), reads=[cbd, sq3], writes=[cbd])
        S.op("dve", lambda e: e.tensor_scalar(out=cbd.ap[:, 0:4, :], in0=cbd.ap[:, 0:4, :], scalar1=128 ** -0.5, scalar2=None, op0=ALU.mult), reads=[cbd], writes=[cbd])
        qd, kd, vd = cbd.ap[:, 0:4, :], cbd.ap[:, 4:8, :], cbd.ap[:, 8:12, :]
        KQ = A([128, 4, NS, 2], F32, "KQd")
        S.op("dve", lambda e: e.tensor_copy(out=KQ.ap[:, :, :, 0], in_=kd), reads=[cbd], writes=[KQ])
        S.op("dve", lambda e: e.tensor_copy(out=KQ.ap[:, :, :, 1], in_=qd), reads=[cbd], writes=[KQ])
        Sb = [A([128, 4, 128], F32, f"Sb{i}") for i in range(2)]
        psk = c.ps()
        SKQ = A([128, 4, NS, 2], F32, "SKQ")
        for s_ in range(NS):
            sb_ = Sb[s_ % 2]
            S.dma("sp", sb_.ap, c.state_gdn[ei, s_].rearrange("h k v -> k h v"), writes=[sb_])
            for h in range(4):
                col = (h * NS + s_) * 2
                S.op("pe", lambda e, h=h, col=col: e.matmul(psk.ap[:, col:col + 2], lhsT=sb_.ap[:, h, :], rhs=KQ.ap[:, h, s_, :], start=True, stop=True),
                     reads=[sb_, KQ], writes=[psk])
        evac(SKQ.ap.rearrange("p h s t -> p (h s t)"), psk.ap[:, 0:8 * NS], [psk], [SKQ])
        EG, BET = EGB.ap[:, 0, :, :], EGB.ap[:, 1, :, :]
        vn = A([128, 4, NS], F32, "vnd")
        od = A([128, 4, NS], F32, "odd")
        S.op("dve", lambda e: e.tensor_tensor(out=vn.ap, in0=SKQ.ap[:, :, :, 0], in1=EG, op=ALU.mult), reads=[SKQ, EGB], writes=[vn])
        S.op("dve", lambda e: e.tensor_tensor(out=vn.ap, in0=vd, in1=vn.ap, op=ALU.subtract), reads=[cbd, vn], writes=[vn])
        S.op("dve", lambda e: e.tensor_tensor(out=vn.ap, in0=vn.ap, in1=BET, op=ALU.mult), reads=[vn, EGB], writes=[vn])
        qk = A([128, 4 * NS], F32, "qkd")
        S.op("dve", lambda e: e.tensor_tensor(out=qk.ap.rearrange("p (h s) -> p h s", h=4), in0=qd, in1=kd, op=ALU.mult), reads=[cbd], writes=[qk])
        pst = colsum_bc(None, qk, 4 * NS)
        S.op("dve", lambda e: e.tensor_tensor(out=od.ap, in0=pst.ap[:, 0:4 * NS].rearrange("p (h s) -> p h s", h=4), in1=vn.ap, op=ALU.mult), reads=[pst, vn], writes=[od])
        S.op("dve", lambda e: e.tensor_tensor(out=qk.ap.rearrange("p (h s) -> p h s", h=4), in0=SKQ.ap[:, :, :, 1], in1=EG, op=ALU.mult), reads=[SKQ, EGB], writes=[qk])
        S.op("dve", lambda e: e.tensor_tensor(out=od.ap, in0=od.ap, in1=qk.ap.rearrange("p (h s) -> p h s", h=4), op=ALU.add), reads=[od, qk], writes=[od])
        dg = [A([128, 128], F32, f"dgd{i}") for i in range(2)]
        tmpS = [A([128, 128], F32, f"tmpS{i}") for i in range(2)]
        So = [A([128, 4, 128], F32, f"So{i}") for i in range(2)]
        n_ = 0
        for s_ in range(NS):
            sb_ = Sb[s_ % 2]
            so = So[s_ % 2]
            S.dma("sp", sb_.ap, c.state_gdn[ei, s_].rearrange("h k v -> k h v"), writes=[sb_])
            for h in range(4):
                d_, t_ = dg[n_ % 2], tmpS[n_ % 2]
                n_ += 1
                S.op("dve", lambda e, h=h: e.tensor_scalar(out=d_.ap, in0=c.ident, scalar1=vn.ap[:, h, s_:s_ + 1], scalar2=None, op0=ALU.mult),
                     reads=[vn, c.cstb], writes=[d_])
                pvb = c.ps()
                S.op("pe", lambda e: e.matmul(pvb.ap[:, 0:128], lhsT=c.ones, rhs=d_.ap, start=True, stop=True), reads=[d_, c.cstb], writes=[pvb])
                S.op("dve", lambda e, h=h: e.tensor_scalar(out=t_.ap, in0=sb_.ap[:, h, :], scalar1=EGB.ap[:, 0, h, s_:s_ + 1], scalar2=None, op0=ALU.mult),
                     reads=[sb_, EGB], writes=[t_])
                S.op("dve", lambda e, h=h: e.scalar_tensor_tensor(out=so.ap[:, h, :], in0=pvb.ap[:, 0:128], scalar=cbd.ap[:, 4 + h, s_:s_ + 1], in1=t_.ap,
                                                                  op0=ALU.mult, op1=ALU.add), reads=[pvb, cbd, t_], writes=[so])
            S.dma("pool", c.gdn_sample[ei, s_].rearrange("h k v -> k h v"), so.ap, reads=[so])
        odf = od.ap.rearrange("p h s -> p (h s)")
        S.op("dve", lambda e: e.tensor_tensor(out=qk.ap, in0=odf, in1=odf, op=ALU.mult), reads=[od], writes=[qk])
        pst = colsum_bc(None, qk, 4 * NS)
        S.op("dve", lambda e: e.tensor_scalar(out=qk.ap, in0=pst.ap[:, 0:4 * NS], scalar1=1.0 / 128, scalar2=EPS, op0=ALU.mult, op1=ALU.add), reads=[pst], writes=[qk])
        _rsq(B, qk, qk.ap)
        S.op("dve", lambda e: e.scalar_tensor_tensor(out=odf, in0=odf, scalar=fmp.ap[:, 1:2], in1=qk.ap, op0=ALU.mult, op1=ALU.mult), reads=[od, fmp, qk], writes=[od])
        S.op("dve", lambda e: e.tensor_tensor(out=uT.ap[:, 4:8, :], in0=od.ap, in1=sza.ap[:, 4:8, :], op=ALU.mult), reads=[od, sza], writes=[uT])
        outproj(uT)
        ei += 1
    S.dma("pool", c.y_sample, xs.ap, reads=[xs])


def build(cfg):
    B = Builder(cfg)
    B.declare()
    B.setup()
    if B.do_prompt:
        prompt_alloc(B)
        if B.NE > 0:
            even_alloc(B)
        ei = oi = 0
        for li, ch in enumerate(B.layers):
            if ch == "e":
                prompt_even(B, li, ei)
                ei += 1
            else:
                prompt_odd(B, li, oi)
                oi += 1
    if B.do_decode:
        if not B.do_prompt:
            _prompt_common(B)
            B.scw = B.sb([128, 8, 3], F32, "scw")
            if B.NE > 0:
                even_alloc(B)
        decode_all(B)
    if cfg.get("dbg"):
        B.S.barrier()
        B.S.dma("sp", B.dbg_u, B.u_s)
    B.S.finish()
    return B


def core_inputs(inp, core, cfg, cst, oh, batch=None):
    TP, NS = cfg["TP"], cfg["NS"]
    layers = cfg["layers"]
    NE = max(sum(1 for ch in layers if ch == "e"), 1)
    NO = max(sum(1 for ch in layers if ch == "o"), 1)
    b = core if batch is None else batch
    f = np.ascontiguousarray
    m = {}
    m["x_prompt"] = f(inp["x_prompt"][b].reshape(TP, D))
    m["x_sample"] = f(inp["x_sample"][core * NS:(core + 1) * NS, 0])
    npool = inp["cache_k"].shape[1]
    m["cache_k"] = inp["cache_k"].reshape(-1, npool * 128, 512)[:NE]
    m["cache_v"] = inp["cache_v"].reshape(-1, npool * 128, 512)[:NE]
    m["page_table"] = f(inp["page_table"][core * NS:(core + 1) * NS])
    m["state_gdn"] = f(inp["state_gdn"][:NE, core * NS:(core + 1) * NS])
    m["state_gdn_conv"] = f(inp["state_gdn_conv"][:NE, core * NS:(core + 1) * NS])
    m["state_shortconv"] = f(inp["state_shortconv"][:NO, core * NS:(core + 1) * NS])
    for n in ("norm_w", "rel_table", "w_in_even", "w_out_even", "qn_w", "kn_w", "lam_q1", "lam_k1", "lam_q2", "lam_k2",
              "subln_w", "gdn_conv_w", "gdn_a_log", "gdn_dt_bias", "gdn_norm_w", "w_in_odd", "sc_conv_w", "w_out_odd"):
        m[n] = inp[n]
    m["norm_w"] = inp["norm_w"][:len(layers)]
    for n in ("w_in_even", "w_out_even", "qn_w", "kn_w", "lam_q1", "lam_k1", "lam_q2", "lam_k2", "subln_w", "gdn_conv_w",
              "gdn_a_log", "gdn_dt_bias", "gdn_norm_w"):
        m[n] = inp[n][:NE]
    for n in ("w_in_odd", "sc_conv_w", "w_out_odd"):
        m[n] = inp[n][:NO]
    m["cst"] = cst
    m["onehot"] = oh
    return m


NCORES = 4


def kernel(**inputs):
    inp = {k: np.asarray(v) for k, v in inputs.items()}
    nb, tp = inp["x_prompt"].shape[0], inp["x_prompt"].shape[1]
    nsamp = inp["x_sample"].shape[0]
    assert nb == NCORES and nsamp % NCORES == 0
    NS = nsamp // NCORES
    cfg = dict(TP=tp, NS=NS, NPOOL=inp["cache_k"].shape[1], NPAGES=inp["page_table"].shape[1], layers="eoeo")
    B = build(cfg)
    cst, oh = make_consts()
    in_maps = [core_inputs(inp, c, cfg, cst, oh) for c in range(NCORES)]
    res = run_bass_kernel_spmd(B.nc, in_maps, core_ids=list(range(NCORES)))
    r = res.results
    NE, NO = 2, 2
    st = lambda n, ax: np.stack([np.asarray(r[c][n]) for c in range(NCORES)], axis=ax)
    ct = lambda n, ax: np.concatenate([np.asarray(r[c][n]) for c in range(NCORES)], axis=ax)
    y_prompt = st("y_prompt", 0).reshape(nb, tp, D)
    y_sample = ct("y_sample", 0).reshape(nsamp, 1, D)
    k_prompt = st("k_prompt", 1).reshape(NE, nb, tp, H_A, DA)
    v_prompt = st("v_prompt", 1).reshape(NE, nb, tp, H_A, DA)
    gdn_prompt = st("gdn_prompt", 1).reshape(NE, nb, H_B, 128, 128)
    gdn_conv_prompt = st("gdn_conv_prompt", 1).reshape(NE, nb, 3, QKV_B)
    sc_prompt = st("sc_prompt", 1).reshape(NO, nb, 2, D)
    k_sample = ct("k_sample", 1).reshape(NE, nsamp, 1, H_A, DA)
    v_sample = ct("v_sample", 1).reshape(NE, nsamp, 1, H_A, DA)
    gdn_sample = ct("gdn_sample", 1).reshape(NE, nsamp, H_B, 128, 128)
    gdn_conv_sample = ct("gdn_conv_sample", 1).reshape(NE, nsamp, 3, QKV_B)
    sc_sample = ct("sc_sample", 1).reshape(NO, nsamp, 2, D)
    return tuple(np.ascontiguousarray(a, dtype=np.float32) for a in (
        y_prompt, y_sample, k_prompt, v_prompt, gdn_prompt, gdn_conv_prompt, sc_prompt,
        k_sample, v_sample, gdn_sample, gdn_conv_sample, sc_sample))
```
